# Optimizing a Trainium2 kernel written in Bass

```python
import math
import jax
import jax.numpy as jnp
from jax import lax
import numpy as np

D_MODEL = 1024
BATCH = 32
SEQ = 256
DEPTH = 4
DEC_BATCH = 4
DEC_SEQ = 1024
PAST_LEN = 256

GRID_W = 64
FFN_DIM = 2816
NORM_EPS = 1e-6
ROPE_THETA = 10000.0
Q_BLOCK = 128
N_ADA = 9
NEG_INF = -1e30

MLA_HEADS = 4
MLA_Q_LORA = 256
MLA_KV_LORA = 128
MLA_NOPE = 64
MLA_ROPE = 32
MLA_V = 64
DIFF_HEADS = 4
DIFF_DK = 32
DIFF_DV = 2 * DIFF_DK
NAT_HEADS = 8
NAT_HD = 64
NAT_WIN_ROWS = 8
NAT_WIN_COLS = 16
NAT_QCOLS = 16
NAT_KCOLS = NAT_QCOLS + NAT_WIN_COLS

MLA_SCALE = (MLA_NOPE + MLA_ROPE) ** -0.5
DIFF_SCALE = DIFF_DK ** -0.5
NAT_SCALE = NAT_HD ** -0.5

SPLIT_1 = MLA_Q_LORA
SPLIT_2 = SPLIT_1 + MLA_KV_LORA
SPLIT_3 = SPLIT_2 + MLA_ROPE
SPLIT_4 = SPLIT_3 + DIFF_HEADS * 2 * DIFF_DK
SPLIT_5 = SPLIT_4 + DIFF_HEADS * 2 * DIFF_DK
SPLIT_6 = SPLIT_5 + DIFF_HEADS * DIFF_DV
SPLIT_7 = SPLIT_6 + NAT_HEADS * NAT_HD
SPLIT_8 = SPLIT_7 + NAT_HEADS * NAT_HD
IN_COLS = SPLIT_8 + NAT_HEADS * NAT_HD
IN_SPLITS = (SPLIT_1, SPLIT_2, SPLIT_3, SPLIT_4, SPLIT_5, SPLIT_6, SPLIT_7, SPLIT_8)
MIX_OUT = MLA_HEADS * MLA_V + DIFF_HEADS * DIFF_DV + NAT_HEADS * NAT_HD

kernel_name = 'hybrid_mla_diff_natten_prefix_flow_step'


def rmsnorm(x, g):
    xf = x.astype(jnp.float32)
    y = xf * lax.rsqrt(jnp.mean(xf * xf, axis=-1, keepdims=True) + NORM_EPS)
    return y.astype(x.dtype) * g


def modulate(x, shift, scale):
    return x * (1.0 + scale) + shift


def half_ffn(x, g, shift, scale, gate, w_gate, w_up, w_down):
    h = modulate(rmsnorm(x, g), shift, scale)
    return x + 0.5 * gate * ((jax.nn.silu(h @ w_gate) * (h @ w_up)) @ w_down)


def rope_1d(x, pos):
    half = x.shape[-1] // 2
    freqs = ROPE_THETA ** (-jnp.arange(half, dtype=jnp.float32) / half)
    ang = pos.astype(jnp.float32)[:, None] * freqs[None, :]
    ang = ang.reshape((ang.shape[0],) + (1,) * (x.ndim - 3) + (half,))
    cos = jnp.cos(ang).astype(x.dtype)
    sin = jnp.sin(ang).astype(x.dtype)
    x1, x2 = x[..., :half], x[..., half:]
    return jnp.concatenate([x1 * cos - x2 * sin, x1 * sin + x2 * cos], axis=-1)


def axial_rope(x):
    t = jnp.arange(x.shape[1])
    h = x.shape[-1] // 2
    return jnp.concatenate([rope_1d(x[..., :h], t // GRID_W), rope_1d(x[..., h:], t % GRID_W)], axis=-1)


def rope_pair(x):
    B, S, H, _ = x.shape
    return axial_rope(x.reshape(B, S, H, 2, DIFF_DK)).reshape(B, S, H, 2 * DIFF_DK)


def to_heads(x):
    return jnp.transpose(x, (0, 2, 1, 3))


def from_heads(x):
    B, H, S, d = x.shape
    return jnp.transpose(x, (0, 2, 1, 3)).reshape(B, S, H * d)


def over_query_blocks(fn, q):
    B, H, S, dq = q.shape
    nb = S // Q_BLOCK
    qb = jnp.moveaxis(q.reshape(B, H, nb, Q_BLOCK, dq), 2, 0)
    out = lax.map(fn, qb)
    return jnp.moveaxis(out, 0, 2).reshape(B, H, S, out.shape[-1])


def dense_attention(q, k, v, scale):
    def blk(qb):
        s = jnp.einsum('bhqd,bhkd->bhqk', qb, k).astype(jnp.float32) * scale
        p = jax.nn.softmax(s, axis=-1)
        return jnp.einsum('bhqk,bhkd->bhqd', p.astype(v.dtype), v)
    return over_query_blocks(blk, q)


def diff_attention(q, k, v, lam):
    k1, k2 = k[..., :DIFF_DK], k[..., DIFF_DK:]
    def blk(qb):
        s1 = jnp.einsum('bhqd,bhkd->bhqk', qb[..., :DIFF_DK], k1).astype(jnp.float32) * DIFF_SCALE
        s2 = jnp.einsum('bhqd,bhkd->bhqk', qb[..., DIFF_DK:], k2).astype(jnp.float32) * DIFF_SCALE
        p = jax.nn.softmax(s1, axis=-1) - lam * jax.nn.softmax(s2, axis=-1)
        return jnp.einsum('bhqk,bhkd->bhqd', p.astype(v.dtype), v)
    return over_query_blocks(blk, q)


def mixer_inputs(h, w_in, q_norm, w_uq, kv_norm):
    B, S, _ = h.shape
    c_q, c_kv, k_rope, dq, dk, dv, nq, nk, nv = jnp.split(h @ w_in, IN_SPLITS, axis=-1)
    q_mla = (rmsnorm(c_q, q_norm) @ w_uq).reshape(B, S, MLA_HEADS, MLA_NOPE + MLA_ROPE)
    ckv = rmsnorm(c_kv, kv_norm)
    dq = dq.reshape(B, S, DIFF_HEADS, 2 * DIFF_DK)
    dk = dk.reshape(B, S, DIFF_HEADS, 2 * DIFF_DK)
    dv = dv.reshape(B, S, DIFF_HEADS, DIFF_DV)
    nq = nq.reshape(B, S, NAT_HEADS, NAT_HD)
    nk = nk.reshape(B, S, NAT_HEADS, NAT_HD)
    nv = nv.reshape(B, S, NAT_HEADS, NAT_HD)
    return q_mla, ckv, k_rope, dq, dk, dv, nq, nk, nv


def mla_attend(q_mla, ckv_all, krope_all, w_ukv):
    B, K, _ = ckv_all.shape
    kv = (ckv_all @ w_ukv).reshape(B, K, MLA_HEADS, MLA_NOPE + MLA_V)
    k_nope, v = kv[..., :MLA_NOPE], kv[..., MLA_NOPE:]
    k = jnp.concatenate([k_nope, jnp.broadcast_to(krope_all[:, :, None, :], (B, K, MLA_HEADS, MLA_ROPE))], axis=-1)
    return from_heads(dense_attention(to_heads(q_mla), to_heads(k), to_heads(v), MLA_SCALE))


def diff_attend(dq, k_all, v_all, lq1, lk1, lq2, lk2, subln, lam_init):
    f32 = jnp.float32
    lam = (jnp.exp(jnp.sum(lq1.astype(f32) * lk1.astype(f32)))
           - jnp.exp(jnp.sum(lq2.astype(f32) * lk2.astype(f32))) + lam_init)
    o = diff_attention(to_heads(dq), k_all, v_all, lam)
    return from_heads(rmsnorm(o, subln) * (1.0 - lam_init))


def natten_latent(q, k, v, k_ctx, v_ctx, rpb):
    f32 = jnp.float32
    B, H, S, d = q.shape
    rows = S // GRID_W
    wr = min(NAT_WIN_ROWS, rows)
    n_cb = GRID_W // NAT_QCOLS
    r = jnp.arange(rows)
    row_idx = jnp.clip(r - wr // 2, 0, rows - wr)[:, None] + jnp.arange(wr)[None, :]
    cb = jnp.arange(n_cb)
    col_idx = (jnp.clip(cb * NAT_QCOLS - NAT_WIN_COLS // 2, 0, GRID_W - NAT_KCOLS)[:, None]
               + jnp.arange(NAT_KCOLS)[None, :])
    key_idx = (row_idx[:, None, :, None] * GRID_W + col_idx[None, :, None, :]).reshape(rows, n_cb, wr * NAT_KCOLS)
    k_win = jnp.take(k, key_idx, axis=2)
    v_win = jnp.take(v, key_idx, axis=2)
    q_blk = q.reshape(B, H, rows, n_cb, NAT_QCOLS, d)
    q_col = cb[:, None] * NAT_QCOLS + jnp.arange(NAT_QCOLS)[None, :]
    c_start = jnp.clip(q_col - NAT_WIN_COLS // 2, 0, GRID_W - NAT_WIN_COLS)
    kc = col_idx[:, None, :]
    in_win = (kc >= c_start[..., None]) & (kc < c_start[..., None] + NAT_WIN_COLS)
    dr = row_idx - r[:, None] + (NAT_WIN_ROWS - 1)
    dc = jnp.clip(kc - q_col[..., None], -(NAT_WIN_COLS - 1), NAT_WIN_COLS - 1) + (NAT_WIN_COLS - 1)
    bias = rpb.astype(f32)[:, dr[:, None, None, :, None], dc[None, :, :, None, :]]
    bias = jnp.where(in_win[None, None, :, :, None, :], bias, NEG_INF).reshape(H, rows, n_cb, NAT_QCOLS, wr * NAT_KCOLS)
    s_win = jnp.einsum('bhrcqd,bhrckd->bhrcqk', q_blk, k_win).astype(f32) * NAT_SCALE + bias[None]
    s_ctx = jnp.einsum('bhrcqd,bhld->bhrcql', q_blk, k_ctx).astype(f32) * NAT_SCALE
    p = jax.nn.softmax(jnp.concatenate([s_win, s_ctx], axis=-1), axis=-1).astype(v.dtype)
    n_win = wr * NAT_KCOLS
    o = (jnp.einsum('bhrcqk,bhrckd->bhrcqd', p[..., :n_win], v_win)
         + jnp.einsum('bhrcql,bhld->bhrcqd', p[..., n_win:], v_ctx))
    return o.reshape(B, H, S, d)


def setup_inputs(seed: int = 0) -> dict:
    key = jax.random.key(seed)
    ks = jax.random.split(key, 40)
    counter = iter(range(40))

    def nrm(shape, scale):
        return jax.random.normal(ks[next(counter)], shape, jnp.float32) * scale

    def gain(shape):
        return 1.0 + nrm(shape, 0.05)

    L = DEPTH
    return {
        'x_prompt': nrm((BATCH, SEQ, D_MODEL), 1.0),
        'x_sample': nrm((DEC_BATCH, DEC_SEQ, D_MODEL), 1.0),
        'cache_mla_ckv': nrm((DEC_BATCH, L, PAST_LEN, MLA_KV_LORA), 1.0),
        'cache_mla_krope': nrm((DEC_BATCH, L, PAST_LEN, MLA_ROPE), 1.0),
        'cache_diff_k': nrm((DEC_BATCH, L, DIFF_HEADS, PAST_LEN, 2 * DIFF_DK), 1.0),
        'cache_diff_v': nrm((DEC_BATCH, L, DIFF_HEADS, PAST_LEN, DIFF_DV), 1.0),
        'cache_nat_k': nrm((DEC_BATCH, L, NAT_HEADS, PAST_LEN, NAT_HD), 1.0),
        'cache_nat_v': nrm((DEC_BATCH, L, NAT_HEADS, PAST_LEN, NAT_HD), 1.0),
        'c': nrm((DEC_BATCH, D_MODEL), 1.0),
        'c_ctx': nrm((D_MODEL,), 1.0),
        'w_ada': nrm((L, D_MODEL, N_ADA * D_MODEL), 0.3 * D_MODEL ** -0.5),
        'b_ada': nrm((L, N_ADA * D_MODEL), 0.02),
        'ffn1_norm': gain((L, D_MODEL)),
        'ffn1_w_gate': nrm((L, D_MODEL, FFN_DIM), D_MODEL ** -0.5),
        'ffn1_w_up': nrm((L, D_MODEL, FFN_DIM), D_MODEL ** -0.5),
        'ffn1_w_down': nrm((L, FFN_DIM, D_MODEL), FFN_DIM ** -0.5),
        'mix_norm': gain((L, D_MODEL)),
        'w_in': nrm((L, D_MODEL, IN_COLS), D_MODEL ** -0.5),
        'mla_q_norm': gain((L, MLA_Q_LORA)),
        'mla_w_uq': nrm((L, MLA_Q_LORA, MLA_HEADS * (MLA_NOPE + MLA_ROPE)), MLA_Q_LORA ** -0.5),
        'mla_kv_norm': gain((L, MLA_KV_LORA)),
        'mla_w_ukv': nrm((L, MLA_KV_LORA, MLA_HEADS * (MLA_NOPE + MLA_V)), MLA_KV_LORA ** -0.5),
        'diff_lambda_q1': nrm((L, DIFF_DK), 0.1),
        'diff_lambda_k1': nrm((L, DIFF_DK), 0.1),
        'diff_lambda_q2': nrm((L, DIFF_DK), 0.1),
        'diff_lambda_k2': nrm((L, DIFF_DK), 0.1),
        'diff_subln': gain((L, DIFF_DV)),
        'nat_rpb': nrm((L, NAT_HEADS, 2 * NAT_WIN_ROWS - 1, 2 * NAT_WIN_COLS - 1), 0.1),
        'w_out': nrm((L, MIX_OUT, D_MODEL), MIX_OUT ** -0.5),
        'ffn2_norm': gain((L, D_MODEL)),
        'ffn2_w_gate': nrm((L, D_MODEL, FFN_DIM), D_MODEL ** -0.5),
        'ffn2_w_up': nrm((L, D_MODEL, FFN_DIM), D_MODEL ** -0.5),
        'ffn2_w_down': nrm((L, FFN_DIM, D_MODEL), FFN_DIM ** -0.5),
        'final_norm': gain((D_MODEL,)),
    }


def reference(x_prompt, x_sample, cache_mla_ckv, cache_mla_krope, cache_diff_k, cache_diff_v,
              cache_nat_k, cache_nat_v, c, c_ctx, w_ada, b_ada, ffn1_norm, ffn1_w_gate, ffn1_w_up,
              ffn1_w_down, mix_norm, w_in, mla_q_norm, mla_w_uq, mla_kv_norm, mla_w_ukv,
              diff_lambda_q1, diff_lambda_k1, diff_lambda_q2, diff_lambda_k2, diff_subln, nat_rpb,
              w_out, ffn2_norm, ffn2_w_gate, ffn2_w_up, ffn2_w_down, final_norm):
    x = x_prompt
    st_ckv, st_krope, st_dk, st_dv, st_nk, st_nv = [], [], [], [], [], []
    for l in range(DEPTH):
        lam_init = 0.8 - 0.6 * math.exp(-0.3 * l)
        ada = (jax.nn.silu(c_ctx) @ w_ada[l] + b_ada[l])[None, None, :]
        sh1, sc1, g1, sh2, sc2, g2, sh3, sc3, g3 = jnp.split(ada, N_ADA, axis=-1)
        x = half_ffn(x, ffn1_norm[l], sh1, sc1, g1, ffn1_w_gate[l], ffn1_w_up[l], ffn1_w_down[l])
        h = modulate(rmsnorm(x, mix_norm[l]), sh2, sc2)
        q_mla, ckv, k_rope, dq, dk, dv, nq, nk, nv = mixer_inputs(h, w_in[l], mla_q_norm[l], mla_w_uq[l], mla_kv_norm[l])
        dk_h, dv_h, nk_h, nv_h = to_heads(dk), to_heads(dv), to_heads(nk), to_heads(nv)
        o_mla = mla_attend(q_mla, ckv, k_rope, mla_w_ukv[l])
        o_diff = diff_attend(dq, dk_h, dv_h, diff_lambda_q1[l], diff_lambda_k1[l], diff_lambda_q2[l],
                             diff_lambda_k2[l], diff_subln[l], lam_init)
        o_nat = from_heads(dense_attention(to_heads(nq), nk_h, nv_h, NAT_SCALE))
        x = x + g2 * (jnp.concatenate([o_mla, o_diff, o_nat], axis=-1) @ w_out[l])
        x = half_ffn(x, ffn2_norm[l], sh3, sc3, g3, ffn2_w_gate[l], ffn2_w_up[l], ffn2_w_down[l])
        st_ckv.append(ckv)
        st_krope.append(k_rope)
        st_dk.append(dk_h)
        st_dv.append(dv_h)
        st_nk.append(nk_h)
        st_nv.append(nv_h)
    y_prompt = rmsnorm(x, final_norm)

    x = x_sample
    for l in range(DEPTH):
        lam_init = 0.8 - 0.6 * math.exp(-0.3 * l)
        ada = (jax.nn.silu(c) @ w_ada[l] + b_ada[l])[:, None, :]
        sh1, sc1, g1, sh2, sc2, g2, sh3, sc3, g3 = jnp.split(ada, N_ADA, axis=-1)
        x = half_ffn(x, ffn1_norm[l], sh1, sc1, g1, ffn1_w_gate[l], ffn1_w_up[l], ffn1_w_down[l])
        h = modulate(rmsnorm(x, mix_norm[l]), sh2, sc2)
        q_mla, ckv, k_rope, dq, dk, dv, nq, nk, nv = mixer_inputs(h, w_in[l], mla_q_norm[l], mla_w_uq[l], mla_kv_norm[l])
        q_mla = jnp.concatenate([q_mla[..., :MLA_NOPE], axial_rope(q_mla[..., MLA_NOPE:])], axis=-1)
        ckv_all = jnp.concatenate([cache_mla_ckv[:, l], ckv], axis=1)
        krope_all = jnp.concatenate([cache_mla_krope[:, l], axial_rope(k_rope[:, :, None, :])[:, :, 0, :]], axis=1)
        o_mla = mla_attend(q_mla, ckv_all, krope_all, mla_w_ukv[l])
        dk_all = jnp.concatenate([cache_diff_k[:, l], to_heads(rope_pair(dk))], axis=2)
        dv_all = jnp.concatenate([cache_diff_v[:, l], to_heads(dv)], axis=2)
        o_diff = diff_attend(rope_pair(dq), dk_all, dv_all, diff_lambda_q1[l], diff_lambda_k1[l],
                             diff_lambda_q2[l], diff_lambda_k2[l], diff_subln[l], lam_init)
        o_nat = from_heads(natten_latent(to_heads(nq), to_heads(nk), to_heads(nv),
                                         cache_nat_k[:, l], cache_nat_v[:, l], nat_rpb[l]))
        x = x + g2 * (jnp.concatenate([o_mla, o_diff, o_nat], axis=-1) @ w_out[l])
        x = half_ffn(x, ffn2_norm[l], sh3, sc3, g3, ffn2_w_gate[l], ffn2_w_up[l], ffn2_w_down[l])
    y_sample = rmsnorm(x, final_norm)

    return (y_prompt, y_sample, jnp.stack(st_ckv, axis=1), jnp.stack(st_krope, axis=1),
            jnp.stack(st_dk, axis=1), jnp.stack(st_dv, axis=1), jnp.stack(st_nk, axis=1),
            jnp.stack(st_nv, axis=1))
```

```python
import contextlib
import math
import numpy as np
import concourse.bass as bass
import concourse.mybir as mybir
from concourse.bass_utils import run_bass_kernel_spmd

F32 = mybir.dt.float32
BF16 = mybir.dt.bfloat16
AF = mybir.ActivationFunctionType
ALU = mybir.AluOpType

D = 1024
FF = 2816
L = 4
NCORES = 8
TG = 1024
EPS = 1e-6
MLA_SCALE = 96 ** -0.5
DIFF_SCALE = 32 ** -0.5
NSLOT = 4
SLOT_EL = 4096
NEG = -1e30

O_N1, O_NM, O_N2, O_FN, O_BADA, O_C, O_QN, O_KVN, O_SL, O_LAM = 0, 32, 64, 96, 104, 392, 408, 416, 420, 424
NSV = 424 + 512


class Sched:
    COMPUTE = ("pe", "act", "dve", "pool")

    def __init__(self, nc, stack, ndma_sems=8):
        self.nc = nc
        self.stack = stack
        self.eng = {"pe": nc.tensor, "act": nc.scalar, "dve": nc.vector, "pool": nc.gpsimd, "sp": nc.sync}
        self.nsem = 0
        self.csem = {}
        self.pe_sems = set()
        for e in self.COMPUTE:
            self.csem[e] = [self._newsem(e), 0]
        self.pe_sems.add(id(self.csem["pe"][0]))
        self.dq = {}
        for q, e in (("sp", "sp"), ("pool", "pool")):
            self.dq[q] = {"eng": e, "sems": [[self._newsem("d" + q), 0] for _ in range(ndma_sems)], "i": 0}
        self.waited = {e: {} for e in self.eng}
        self.last_w = {}
        self.readers = {}
        self.ninst = {e: 0 for e in self.eng}
        self.nwait = {e: 0 for e in self.eng}
        self.bar_toks = []

    def _newsem(self, tag):
        self.nsem += 1
        return self.stack.enter_context(self.nc.semaphore("s%s%d" % (tag, self.nsem)))

    def _wait(self, e, tok):
        sem, val = tok
        w = self.waited[e]
        k = id(sem)
        if w.get(k, 0) >= val:
            return
        w[k] = val
        self.eng[e].wait_ge(sem, val)
        self.nwait[e] += 1

    def _deps(self, reads, writes, deps):
        d = {}

        def add(t):
            k = id(t[0])
            if k not in d or d[k][1] < t[1]:
                d[k] = t
        for t in deps:
            add(t)
        for r in reads:
            t = self.last_w.get(r)
            if t is not None:
                add(t)
        for w in writes:
            t = self.last_w.get(w)
            if t is not None:
                add(t)
            for t in self.readers.get(w, {}).values():
                add(t)
        return list(d.values())

    def _record(self, tok, reads, writes):
        for r in reads:
            if not r.startswith("c:"):
                self.readers.setdefault(r, {})[id(tok[0])] = tok
        for w in writes:
            self.last_w[w] = tok
            self.readers[w] = {}

    def op(self, e, fn, reads=(), writes=(), deps=()):
        pr = [r for r in reads if r.startswith("ps")]
        if pr:
            reads = [r for r in reads if not r.startswith("ps")]
            writes = list(writes) + pr
        for t in self._deps(reads, writes, deps):
            if e == "pe" and id(t[0]) in self.pe_sems:
                continue
            self._wait(e, t)
        inst = fn(self.eng[e])
        cs = self.csem[e]
        if cs[1] >= 30000:
            cs[0] = self._newsem(e)
            cs[1] = 0
            if e == "pe":
                self.pe_sems.add(id(cs[0]))
        cs[1] += 1
        inst.then_inc(cs[0], 1)
        tok = (cs[0], cs[1])
        self.ninst[e] += 1
        self._record(tok, reads, writes)
        return tok

    def dma(self, q, out, in_, reads=(), writes=(), deps=(), **kw):
        dq = self.dq[q]
        e = dq["eng"]
        for t in self._deps(reads, writes, deps):
            self._wait(e, t)
        slot = dq["sems"][dq["i"] % len(dq["sems"])]
        dq["i"] += 1
        if slot[1] >= 30000:
            slot[0] = self._newsem("d" + q)
            slot[1] = 0
        if slot[1] > 0:
            self._wait(e, (slot[0], slot[1]))
        inst = self.eng[e].dma_start(out=out, in_=in_, **kw)
        slot[1] += 16
        inst.then_inc(slot[0], 16)
        tok = (slot[0], slot[1])
        self.ninst[e] += 1
        self._record(tok, reads, writes)
        return tok

    def latest(self):
        toks = []
        for e in self.COMPUTE:
            cs = self.csem[e]
            if cs[1] > 0:
                toks.append((cs[0], cs[1]))
        for q in self.dq.values():
            for s in q["sems"]:
                if s[1] > 0:
                    toks.append((s[0], s[1]))
        return toks

    def barrier(self, engines=("pe", "act", "dve", "sp")):
        toks = self.latest()
        self.bar_toks = toks
        for e in engines:
            for t in toks:
                if e == "pe" and id(t[0]) in self.pe_sems:
                    continue
                self._wait(e, t)


def nat_chunks(t):
    def rs(r):
        return min(max(r - 4, 0), 8)
    lo = rs(2 * t)
    hi = rs(2 * t + 1) + 7
    js = list(range(lo // 2, hi // 2 + 1))
    inval = []
    for jj, j in enumerate(js):
        for a in range(2):
            for b in range(2):
                rk, rq = 2 * j + a, 2 * t + b
                if not (rs(rq) <= rk < rs(rq) + 8):
                    inval.append((jj, a, b))
    return js, inval


def build_program(depth=L, stop=None, groups=(0, 1), mixsel="all", stats=None):
    nc = bass.Bass("TRN2", target_bir_lowering=False)

    def din(name, shape):
        return nc.dram_tensor(name, list(shape), F32, kind="ExternalInput").ap()

    def dout(name, shape):
        return nc.dram_tensor(name, list(shape), F32, kind="ExternalOutput").ap()

    xp_d = din("xp", [TG, D]); xs_d = din("xs", [TG, D])
    smallv_d = din("smallv", [128, NSV]); gkvb_d = din("gkvb", [128, L * 128]); cst_d = din("cst", [128, 128 + 2048])
    natb_d = din("natb", [L, 128, 8 * 1024])
    cckv_d = din("c_ckv", [L, 256, 128]); ckr_d = din("c_krope", [L, 256, 32])
    cdk_d = din("c_dk", [L, 4, 256, 64]); cdv_d = din("c_dv", [L, 4, 256, 64])
    cnk_d = din("c_nk", [L, 8, 256, 64]); cnv_d = din("c_nv", [L, 8, 256, 64])
    wada_d = din("w_ada", [L, D, 9 * D])
    wg_d = [din("wg1", [L, D, FF]), din("wg2", [L, D, FF])]
    wu_d = [din("wu1", [L, D, FF]), din("wu2", [L, D, FF])]
    wd_d = [din("wd1", [L, FF, D]), din("wd2", [L, FF, D])]
    win_d = din("w_in", [L, D, 2720]); wuq_d = din("wuq", [L, 256, 384]); wukv_d = din("wukv", [L, 128, 512])
    wout_d = din("w_out", [L, D, D])
    yp_d = dout("y_p", [TG, D]); ys_d = dout("y_s", [TG, D])
    sckv_d = dout("st_ckv", [4, L, 256, 128]); skr_d = dout("st_krope", [4, L, 256, 32])
    sdk_d = dout("st_dk", [4, L, 4, 256, 64]); sdv_d = dout("st_dv", [4, L, 4, 256, 64])
    snk_d = dout("st_nk", [4, L, 8, 256, 64]); snv_d = dout("st_nv", [4, L, 8, 256, 64])

    with contextlib.ExitStack() as st:
        S = Sched(nc, st)

        def sb(name, shape, dt=F32):
            return st.enter_context(nc.sbuf_tensor("sb_" + name, list(shape), dt))

        xT = sb("xT", [128, 8, TG])
        hT = sb("hT", [128, 8, TG], BF16)
        wsl = sb("wsl", [128, NSLOT, SLOT_EL], BF16)
        smallv = sb("smallv", [128, NSV])
        gkvb = sb("gkvb", [128, L * 128])
        cst = sb("cst", [128, 128 + 2048])
        ada = sb("ada", [128, L, 72, 2])
        coef = sb("coef", [128, 9, 8])
        csil = sb("csil", [128, 8, 2], BF16)
        ident_b = sb("ident_b", [128, 128], BF16)
        ones_b = sb("ones_b", [128, 128], BF16)
        bones_b = sb("bones_b", [128, 128], BF16)
        epsc = sb("epsc", [128, 1])
        rstd = sb("rstd", [128, 2, 512])
        tmpa = sb("tmpa", [128, 2, 512])
        tmpb = sb("tmpb", [128, 2, 512])
        sqb = sb("sqb", [128, 3, 512], BF16)
        wuq_s = sb("wuq_s", [128, 2, 384], BF16)
        wuqr_s = sb("wuqr_s", [128, 2, 384], BF16)
        wukv_s = sb("wukv_s", [128, 512], BF16)
        wrot = sb("wrot", [128, 8, 608], BF16)
        lamt = sb("lamt", [128, 12])
        arena = sb("arena", [128, 40960], BF16)
        psum = st.enter_context(nc.psum_tensor("psum", [128, 8, 512], F32))
        ident_f = cst[:, 0:128]
        cosT = cst[:, 128:128 + 1024]
        sinT = cst[:, 1152:1152 + 1024]

        def carve(off, shape, dt=BF16):
            n = 1
            for s_ in shape[1:]:
                n *= s_
            mult = 2 if dt == F32 else 1
            base = arena[:, off:off + n * mult]
            if dt == F32:
                base = base.bitcast(F32)
            names = "abcdefg"[:len(shape) - 1]
            if len(shape) > 2:
                pat = "p (" + " ".join(names) + ") -> p " + " ".join(names)
                kw = {names[i]: shape[i + 1] for i in range(len(shape) - 1)}
                base = base.rearrange(pat, **kw)
            return base[0:shape[0]], off + n * mult

        actT, _ = carve(0, [128, 22, TG])
        o = 0
        Kb, o = carve(o, [128, 4, 1280])
        Vb, o = carve(o, [128, 10, 768])
        Qb, o = carve(o, [128, 4, TG])
        oT, o = carve(o, [128, 8, TG])
        ckvT, o = carve(o, [128, 1280])
        cqnT, o = carve(o, [128, 2, TG])
        Pb, o = carve(o, [128, 3, 1024])
        odT, o = carve(o, [128, 512], F32)
        o_shared = o
        natb_s, o = carve(o, [128, 2, 2048])
        kst, o = carve(o, [128, 2, 1024])
        stg, o2 = carve(o_shared, [128, 2, 1696], F32)
        assert max(o, o2) <= 40960, (o, o2)
        xstage, _ = carve(0, [128, 2, D], F32)

        bank_i = [0]

        presum = {"on": False, "pend": [], "cnt": [0, 0]}

        def nb():
            while True:
                b = bank_i[0] % 8
                bank_i[0] += 1
                if presum["on"] and b in (6, 7):
                    continue
                return b

        def presum_begin():
            presum["on"] = True
            presum["pend"] = []
            presum["cnt"] = [0, 0]

        def presum_add(dc, t):
            k = rr("sq", 3)
            tok = slice(t * 512, (t + 1) * 512)
            S.op("dve", lambda e: e.tensor_tensor(out=sqb[:, k, :], in0=xT[:, dc, tok], in1=xT[:, dc, tok], op=ALU.mult), reads=["x%d" % t], writes=["sq%d" % k])
            first, last = presum["cnt"][t] == 0, presum["cnt"][t] == 7
            presum["cnt"][t] += 1

            def fn():
                S.op("pe", lambda e: mm(e, psum[:, 6 + t, :], ones_b[:], sqb[:, k, :], first, last), reads=["sq%d" % k, "c:ones"], writes=[PSn(6 + t)])
            presum["pend"].append(fn)
            presum_flush(2)

        def presum_flush(keep=0):
            while len(presum["pend"]) > keep:
                presum["pend"].pop(0)()

        def rstd_for_tile(t):
            tok = slice(t * 512, (t + 1) * 512)
            if presum["on"]:
                presum_flush(0)
                assert presum["cnt"] == [8, 8], presum["cnt"]
                r = rr("rstd", 2)
                S.op("act", lambda e: e.activation(out=rstd[:, r, :], in_=psum[:, 6 + t, :], func=AF.Ln, scale=1.0 / D, bias=epsc[:, 0:1]),
                     reads=[PSn(6 + t), "c:eps"], writes=["rstd%d" % r])
                S.op("act", lambda e: e.activation(out=rstd[:, r, :], in_=rstd[:, r, :], func=AF.Exp, scale=-0.5), reads=["rstd%d" % r], writes=["rstd%d" % r])
                if t == 1:
                    presum["on"] = False
                return r
            return sumsq_rstd([(xT[:, c, tok], ["x%d" % t]) for c in range(8)], D, "x")

        def PS(b):
            return psum[:, b, :]

        def PSn(b):
            return "ps%d" % b

        ring = {}

        def rr(name, n):
            ring[name] = (ring.get(name, -1) + 1) % n
            return ring[name]

        def mm(e, out, lhsT, rhs, start, stop):
            return e.matmul(out, lhsT=lhsT, rhs=rhs, start=start, stop=stop)

        jobs = []

        def job(compute, load=None):
            jobs.append((load, compute))

        def run_jobs():
            wj = [i for i, j in enumerate(jobs) if j[0] is not None]
            slot_of = {i: n % NSLOT for n, i in enumerate(wj)}
            loaded = 0
            seen = 0
            for i, (load, compute) in enumerate(jobs):
                if load is not None:
                    seen += 1
                while loaded < len(wj) and loaded < max(seen, 1) + NSLOT - 1:
                    j = wj[loaded]
                    jobs[j][0](slot_of[j])
                    loaded += 1
                compute(slot_of.get(i))

        def wdma(slot, dst, src, part=0):
            S.dma("pool", dst, src, writes=["w%d_%d" % (slot, part)])

        def wres(slot):
            return ["w%d_0" % slot, "w%d_1" % slot]

        def prologue(_):
            S.dma("sp", smallv[:], smallv_d[:, :], writes=["c:smallv"])
            S.dma("sp", gkvb[:], gkvb_d[:, :], writes=["c:gkvb"])
            S.dma("sp", cst[:], cst_d[:, :], writes=["c:cst"])
            S.op("dve", lambda e: e.tensor_copy(out=ident_b[:], in_=ident_f), reads=["c:cst"], writes=["c:identb"])
            S.op("dve", lambda e: e.memset(ones_b[:], 1.0), writes=["c:ones"])
            S.op("dve", lambda e: e.memset(bones_b[:], 0.0), writes=["c:bones"])
            S.op("dve", lambda e: e.memset(bones_b[0:64, 0:64], 1.0), writes=["c:bones"])
            S.op("dve", lambda e: e.memset(bones_b[64:128, 64:128], 1.0), writes=["c:bones"])
            S.op("dve", lambda e: e.memset(epsc[:], EPS), writes=["c:eps"])
            S.op("act", lambda e: e.activation(out=csil[:].rearrange("p a b -> p (a b)"), in_=smallv[:, O_C:O_C + 16], func=AF.Silu),
                 reads=["c:smallv"], writes=["c:csil"])

        job(prologue)

        def ada_layer(l):
            bank = [None]
            for blk in range(18):
                def load(slot, blk=blk):
                    dst = wsl[:, slot, :].rearrange("p (k f) -> p k f", k=8)
                    src = wada_d[l].rearrange("(k p) f -> p k f", p=128)[:, :, blk * 512:(blk + 1) * 512]
                    wdma(slot, dst, src)

                def comp(slot, blk=blk):
                    b = nb()
                    w = wsl[:, slot, :].rearrange("p (k f) -> p k f", k=8)

                    def f(e):
                        r = None
                        for fc in range(4):
                            for k in range(8):
                                r = mm(e, psum[:, b, 2 * fc:2 * fc + 2], w[:, k, fc * 128:(fc + 1) * 128], csil[:, k, :], k == 0, k == 7)
                        return r
                    S.op("pe", f, reads=wres(slot) + ["c:csil"], writes=[PSn(b)])
                    j0 = blk * 4
                    bb = smallv[:, O_BADA + l * 72 + j0:O_BADA + l * 72 + j0 + 4].unsqueeze(2).broadcast_to([128, 4, 2])
                    S.op("dve", lambda e: e.tensor_tensor(out=ada[:, l, j0:j0 + 4, :], in0=psum[:, b, 0:8].rearrange("p (j g) -> p j g", g=2), in1=bb, op=ALU.add),
                         reads=[PSn(b), "c:smallv"], writes=["c:ada"])
                if l == 0:
                    job(comp, load)
                else:
                    ada_pending.append((comp, load))

        ada_pending = []
        for l in range(depth):
            ada_layer(l)

        def make_coef(g, l):
            def f(_):
                a = ada[:, l, :, g]
                for n, (on, i_sc, i_sh, i_g, gmul) in enumerate(((O_N1, 8, 0, 16, 0.5), (O_NM, 32, 24, 40, 1.0), (O_N2, 56, 48, 64, 0.5))):
                    gn = smallv[:, on + l * 8:on + l * 8 + 8]
                    S.op("dve", lambda e: e.scalar_tensor_tensor(out=coef[:, 3 * n, :], in0=a[:, i_sc:i_sc + 8], scalar=1.0, in1=gn, op0=ALU.add, op1=ALU.mult),
                         reads=["c:ada", "c:smallv"], writes=["coef"])
                    S.op("dve", lambda e: e.tensor_copy(out=coef[:, 3 * n + 1, :], in_=a[:, i_sh:i_sh + 8]), reads=["c:ada"], writes=["coef"])
                    S.op("dve", lambda e: e.tensor_scalar(out=coef[:, 3 * n + 2, :], in0=a[:, i_g:i_g + 8], scalar1=gmul, scalar2=None, op0=ALU.mult),
                         reads=["c:ada"], writes=["coef"])
            job(f)

        def sumsq_rstd(srcs, nfeat, tname):
            n = srcs[0][0].shape[-1]
            b = nb()
            for i, (ap, rd) in enumerate(srcs):
                k = rr("sq", 3)
                S.op("dve", lambda e: e.tensor_tensor(out=sqb[:, k, 0:n], in0=ap, in1=ap, op=ALU.mult), reads=rd, writes=["sq%d" % k])
                S.op("pe", lambda e: mm(e, psum[:, b, 0:n], ones_b[:], sqb[:, k, 0:n], i == 0, i == len(srcs) - 1),
                     reads=["sq%d" % k, "c:ones"], writes=[PSn(b)])
            r = rr("rstd", 2)
            S.op("act", lambda e: e.activation(out=rstd[:, r, 0:n], in_=psum[:, b, 0:n], func=AF.Ln, scale=1.0 / nfeat, bias=epsc[:, 0:1]),
                 reads=[PSn(b), "c:eps"], writes=["rstd%d" % r])
            S.op("act", lambda e: e.activation(out=rstd[:, r, 0:n], in_=rstd[:, r, 0:n], func=AF.Exp, scale=-0.5), reads=["rstd%d" % r], writes=["rstd%d" % r])
            return r

        def norm_mod(ia, ib):
            def f(_):
                for t in range(2):
                    tok = slice(t * 512, (t + 1) * 512)
                    r = rstd_for_tile(t)
                    for c in range(8):
                        k = rr("tmpa", 2)
                        S.op("dve", lambda e: e.tensor_tensor(out=tmpa[:, k, :], in0=xT[:, c, tok], in1=rstd[:, r, :], op=ALU.mult),
                             reads=["x%d" % t, "rstd%d" % r], writes=["tmpa%d" % k])
                        S.op("act", lambda e: e.activation(out=hT[:, c, tok], in_=tmpa[:, k, :], func=AF.Identity,
                                                           scale=coef[:, ia, c:c + 1], bias=coef[:, ib, c:c + 1]),
                             reads=["tmpa%d" % k, "coef"], writes=["h%d" % t])
            job(f)

        def barrier_job():
            job(lambda _: S.barrier())

        ada_take = [0]

        def ffn(l, which, ia, ib, ig):
            import os
            ada_take[0] = 9
            parts = os.environ.get("FFN_PARTS", "ngd")
            norm_mod(ia, ib)
            wg, wu, wd = wg_d[which][l], wu_d[which][l], wd_d[which][l]
            for fb in range(11 if "g" in parts else 0):
                def load(slot, fb=fb):
                    dst = wsl[:, slot, :].rearrange("p (k m f) -> p k m f", k=8, m=2)
                    wdma(slot, dst[:, :, 0, :], wg.rearrange("(k p) f -> p k f", p=128)[:, :, fb * 256:(fb + 1) * 256])
                    wdma(slot, dst[:, :, 1, :], wu.rearrange("(k p) f -> p k f", p=128)[:, :, fb * 256:(fb + 1) * 256], part=1)

                def comp(slot, fb=fb):
                    w = wsl[:, slot, :].rearrange("p (k m f) -> p k m f", k=8, m=2)
                    order = [(j, t) for t in range(2) for j in range(2)] if fb == 0 else [(j, t) for j in range(2) for t in range(2)]
                    for (j, t) in order:
                        fc = 2 * fb + j
                        bg, bu = nb(), nb()

                        def f(e):
                            r = None
                            for k in range(8):
                                for m, bk in ((0, bg), (1, bu)):
                                    r = mm(e, PS(bk), w[:, k, m, j * 128:(j + 1) * 128], hT[:, k, t * 512:(t + 1) * 512], k == 0, k == 7)
                            return r
                        S.op("pe", f, reads=wres(slot) + ["h%d" % t], writes=[PSn(bg), PSn(bu)])
                        k = rr("tmpb", 2)
                        S.op("act", lambda e: e.activation(out=tmpb[:, k, :], in_=PS(bg), func=AF.Silu), reads=[PSn(bg)], writes=["tmpb%d" % k])
                        S.op("dve", lambda e: e.tensor_tensor(out=actT[:, fc, t * 512:(t + 1) * 512], in0=PS(bu), in1=tmpb[:, k, :], op=ALU.mult),
                             reads=[PSn(bu), "tmpb%d" % k], writes=["act%d" % t])
                job(comp, load)
                if ada_pending and ada_take[0] > 0:
                    ada_take[0] -= 1
                    c_, l_ = ada_pending.pop(0)
                    job(c_, l_)
            if "d" in parts:
                job(lambda _: presum_begin())
            for dc in range(8 if "d" in parts else 0):
                def load(slot, dc=dc):
                    dst = wsl[:, slot, 0:22 * 128].rearrange("p (k f) -> p k f", k=22)
                    srcw = wd.rearrange("(k p) d -> p k d", p=128)[:, :, dc * 128:(dc + 1) * 128]
                    for i_, (ka, kb) in enumerate(((0, 8), (8, 16), (16, 22))):
                        wdma(slot, dst[:, ka:kb, :], srcw[:, ka:kb, :], part=i_ % 2)

                def comp(slot, dc=dc):
                    w = wsl[:, slot, 0:22 * 128].rearrange("p (k f) -> p k f", k=22)
                    for t in range(2):
                        bk = nb()

                        def f(e):
                            r = None
                            for k in range(22):
                                r = mm(e, PS(bk), w[:, k, :], actT[:, k, t * 512:(t + 1) * 512], k == 0, k == 21)
                            return r
                        S.op("pe", f, reads=wres(slot) + ["act%d" % t], writes=[PSn(bk)])
                        xs_ = xT[:, dc, t * 512:(t + 1) * 512]
                        S.op("dve", lambda e: e.scalar_tensor_tensor(out=xs_, in0=PS(bk), scalar=coef[:, ig, dc:dc + 1], in1=xs_, op0=ALU.mult, op1=ALU.add),
                             reads=[PSn(bk), "coef", "x%d" % t], writes=["x%d" % t])
                        presum_add(dc, t)
                job(comp, load)

        def load_x(g):
            src = xp_d if g == 0 else xs_d

            def f(_):
                presum["on"] = False
                for tt in range(2):
                    for q4 in range(4):
                        ch = tt * 4 + q4
                        k = rr("xst", 2)
                        S.dma("sp", xstage[:, k, :], src[ch * 128:(ch + 1) * 128, :], writes=["xst%d" % k])
                        for c0 in range(0, 8, 4):
                            b = nb()

                            def f2(e):
                                r = None
                                for c in range(c0, c0 + 4):
                                    r = e.transpose(psum[:, b, (c - c0) * 128:(c - c0 + 1) * 128], xstage[:, k, c * 128:(c + 1) * 128], ident_f)
                                return r
                            S.op("pe", f2, reads=["xst%d" % k, "c:cst"], writes=[PSn(b)])
                            S.op("act" if c0 == 0 else "dve",
                                 lambda e: (e.activation(out=xT[:, c0:c0 + 4, ch * 128:(ch + 1) * 128], in_=psum[:, b, :].rearrange("p (c t) -> p c t", c=4), func=AF.Identity)
                                            if c0 == 0 else e.tensor_copy(out=xT[:, c0:c0 + 4, ch * 128:(ch + 1) * 128], in_=psum[:, b, :].rearrange("p (c t) -> p c t", c=4))),
                                 reads=[PSn(b)], writes=["x%d" % tt])
            job(f)

        def store_y(g):
            dst = yp_d if g == 0 else ys_d

            def f(_):
                fn = smallv[:, O_FN:O_FN + 8]
                for t in range(2):
                    tok = slice(t * 512, (t + 1) * 512)
                    r = rstd_for_tile(t)
                    for c in range(8):
                        S.op("dve", lambda e: e.scalar_tensor_tensor(out=xT[:, c, tok], in0=xT[:, c, tok], scalar=fn[:, c:c + 1], in1=rstd[:, r, :], op0=ALU.mult, op1=ALU.mult),
                             reads=["x%d" % t, "rstd%d" % r, "c:smallv"], writes=["x%d" % t])
                    for q4 in range(4):
                        ch = t * 4 + q4
                        k = rr("xst", 2)
                        for c0 in range(0, 8, 4):
                            b = nb()

                            def f2(e):
                                r2 = None
                                for c in range(c0, c0 + 4):
                                    r2 = e.transpose(psum[:, b, (c - c0) * 128:(c - c0 + 1) * 128], xT[:, c, ch * 128:(ch + 1) * 128], ident_f)
                                return r2
                            S.op("pe", f2, reads=["x%d" % t, "c:cst"], writes=[PSn(b)])
                            S.op("act" if c0 == 0 else "dve",
                                 lambda e: (e.activation(out=xstage[:, k, c0 * 128:(c0 + 4) * 128], in_=psum[:, b, :], func=AF.Identity)
                                            if c0 == 0 else e.tensor_copy(out=xstage[:, k, c0 * 128:(c0 + 4) * 128], in_=psum[:, b, :])),
                                 reads=[PSn(b)], writes=["xst%d" % k])
                        S.dma("sp", dst[ch * 128:(ch + 1) * 128, :], xstage[:, k, :], reads=["xst%d" % k])
            job(f)

        pending_subln = []

        def mixer(g, l):
            nkeys = 1024 if g == 0 else 1280
            koff = 0 if g == 0 else 256
            nvch = 8 if g == 0 else 10
            lam_init = 0.8 - 0.6 * math.exp(-0.3 * l)
            norm_mod(3, 4)

            def small(_):
                S.dma("pool", wuq_s[:], wuq_d[l].rearrange("(k p) f -> p k f", p=128), writes=["wuq"])
                wk = wukv_d[l].rearrange("k (h t d) -> k h t d", h=4, t=2)
                S.dma("pool", wukv_s[:, 0:256].rearrange("p (h d) -> p h d", h=4), wk[:, :, 0, :], writes=["wukv"])
                S.dma("pool", wukv_s[:, 256:512].rearrange("p (h d) -> p h d", h=4), wk[:, :, 1, :], writes=["wukv"])
                lv = smallv[:, O_LAM + l * 128:O_LAM + (l + 1) * 128]
                S.op("dve", lambda e: e.tensor_tensor(out=tmpa[:, 0, 0:32], in0=lv[:, 0:32], in1=lv[:, 32:64], op=ALU.mult), reads=["c:smallv"], writes=["tmpa0"])
                S.op("dve", lambda e: e.tensor_tensor(out=tmpa[:, 0, 32:64], in0=lv[:, 64:96], in1=lv[:, 96:128], op=ALU.mult), reads=["c:smallv"], writes=["tmpa0"])
                S.op("dve", lambda e: e.tensor_reduce(out=lamt[:, 0:2], in_=tmpa[:, 0, 0:64].rearrange("p (a b) -> p a b", a=2), axis=mybir.AxisListType.X, op=ALU.add),
                     reads=["tmpa0"], writes=["lamt"])
                S.op("act", lambda e: e.activation(out=lamt[:, 2:4], in_=lamt[:, 0:2], func=AF.Exp), reads=["lamt"], writes=["lamt"])
                S.op("dve", lambda e: e.scalar_tensor_tensor(out=lamt[:, 4:5], in0=lamt[:, 3:4], scalar=-lam_init, in1=lamt[:, 2:3], op0=ALU.add, op1=ALU.subtract),
                     reads=["lamt"], writes=["lamt"])
                S.op("dve", lambda e: e.tensor_scalar(out=lamt[:, 5:6], in0=smallv[:, O_SL + l:O_SL + l + 1], scalar1=1.0 - lam_init, scalar2=None, op0=ALU.mult),
                     reads=["c:smallv"], writes=["lamt"])
                vv = Vb[:, :, :].rearrange("p c (q s) -> p c q s", s=192)
                S.op("dve", lambda e: e.memset(vv[:, :, :, 64:128], 1.0), writes=["V"])
                if g == 1:
                    S.op("dve", lambda e: e.tensor_copy(out=wuqr_s[:], in_=wuq_s[:]), reads=["wuq"], writes=["wuqr"])
                    src = wuq_s[:].rearrange("p k (h c) -> p k h c", h=4)[:, :, :, 64:96].rearrange("p k h (q two e) -> p k h q two e", two=2, e=8)
                    dstv = wuqr_s[:].rearrange("p k (h c) -> p k h c", h=4)[:, :, :, 64:96].rearrange("p k h (q two e) -> p k h q two e", two=2, e=8)
                    for kk in range(2):
                        S.op("dve", lambda e: e.tensor_scalar(out=dstv[:, kk, :, :, 0, :], in0=src[:, kk, :, :, 1, :], scalar1=-1.0, scalar2=None, op0=ALU.mult), reads=["wuq"], writes=["wuqr"])
                        S.op("dve", lambda e: e.tensor_copy(out=dstv[:, kk, :, :, 1, :], in_=src[:, kk, :, :, 0, :]), reads=["wuq"], writes=["wuqr"])
            job(small)

            def wblock(c0, c1):
                def load(slot):
                    n = c1 - c0
                    dst = wsl[:, slot, 0:8 * n].rearrange("p (k f) -> p k f", k=8)
                    wdma(slot, dst, win_d[l].rearrange("(k p) f -> p k f", p=128)[:, :, c0:c1])
                return load

            def wv(slot, n):
                return wsl[:, slot, 0:8 * n].rearrange("p (k f) -> p k f", k=8)

            def proj_fm(w, col0, m, slot_res, evac, extra_reads=()):
                for t in range(2):
                    bk = nb()

                    def f(e):
                        r = None
                        for k in range(8):
                            r = mm(e, psum[0:m, bk, :], w[:, k, col0:col0 + m], hT[:, k, t * 512:(t + 1) * 512], k == 0, k == 7)
                        return r
                    S.op("pe", f, reads=list(slot_res) + ["h%d" % t] + list(extra_reads), writes=[PSn(bk)])
                    evac(t, bk)

            def proj_tm(w, col0, n, slot_res, evac):
                for ch in range(8):
                    b = nb()

                    def f(e):
                        r = None
                        for k in range(8):
                            r = mm(e, psum[:, b, 0:n], hT[:, k, ch * 128:(ch + 1) * 128], w[:, k, col0:col0 + n], k == 0, k == 7)
                        return r
                    S.op("pe", f, reads=list(slot_res) + ["h%d" % (ch // 4)], writes=[PSn(b)])
                    evac(ch, b)

            def rot_cols(dst, src, ncols, reads, writes):
                s5 = src.rearrange("p k (q two e) -> p k q two e", two=2, e=8)
                d5 = dst.rearrange("p k (q two e) -> p k q two e", two=2, e=8)
                S.op("dve", lambda e: e.tensor_scalar(out=d5[:, :, :, 0, :], in0=s5[:, :, :, 1, :], scalar1=-1.0, scalar2=None, op0=ALU.mult), reads=reads, writes=writes)
                S.op("dve", lambda e: e.tensor_copy(out=d5[:, :, :, 1, :], in_=s5[:, :, :, 0, :]), reads=reads, writes=writes)

            def rope_evac(pq, pr, p0, p1, out_ap, tok, reads, writes):
                k = rr("tmpa", 2)
                k2 = rr("tmpb", 2)
                S.op("dve", lambda e: e.tensor_tensor(out=tmpa[p0:p1, k, :], in0=psum[p0:p1, pq, :], in1=cosT[p0:p1, tok], op=ALU.mult),
                     reads=[PSn(pq), "c:cst"], writes=["tmpa%d" % k])
                S.op("dve", lambda e: e.tensor_tensor(out=tmpb[p0:p1, k2, :], in0=psum[p0:p1, pr, :], in1=sinT[p0:p1, tok], op=ALU.mult),
                     reads=[PSn(pr), "c:cst"], writes=["tmpb%d" % k2])
                S.op("dve", lambda e: e.tensor_tensor(out=out_ap, in0=tmpa[p0:p1, k, :], in1=tmpb[p0:p1, k2, :], op=ALU.add),
                     reads=["tmpa%d" % k, "tmpb%d" % k2] + list(reads), writes=writes)

            def stage_out(ch, col0, n, b, dram_fn, pre=None):
                if n == 256:
                    i_ = rr("stg256", 4)
                    k, col0 = i_ % 2, (160, 416)[i_ // 2]
                elif n == 512:
                    i_ = rr("stg512", 4)
                    k, col0 = i_ % 2, (672, 1184)[i_ // 2]
                else:
                    k = rr("stg", 2)
                if pre is None:
                    S.op("act", lambda e: e.activation(out=stg[:, k, col0:col0 + n], in_=psum[:, b, 0:n], func=AF.Identity), reads=[PSn(b)], writes=["stg%d_%d" % (k, col0)])
                else:
                    pre(k)
                s_, i0 = ch // 2, (ch % 2) * 128
                for (dst, c_a, c_b) in dram_fn(s_, i0):
                    srcv = stg[:, k, col0 + c_a:col0 + c_b]
                    if len(dst.shape) == 3:
                        srcv = srcv.rearrange("p (h d) -> p h d", d=64)
                    S.dma("sp", dst, srcv, reads=["stg%d_%d" % (k, col0)])

            def vcopy(ch, b, nh, pair0):
                vch = ch + (0 if g == 0 else 2)
                vv = Vb[:, vch, :].rearrange("p (q s) -> p q s", s=192)
                src = psum[:, b, 0:nh * 64].rearrange("p (q two d) -> p q two d", two=2, d=64)
                S.op("act", lambda e: e.activation(out=vv[:, pair0:pair0 + nh // 2, 0:64], in_=src[:, :, 0, :], func=AF.Identity), reads=[PSn(b)], writes=["V"])
                S.op("dve", lambda e: e.tensor_copy(out=vv[:, pair0:pair0 + nh // 2, 128:192], in_=src[:, :, 1, :]), reads=[PSn(b)], writes=["V"])

            def lhs_v(vch, s):
                base = (s // 2) * 192 + (0 if s % 2 == 0 else 64)
                return Vb[:, vch, base:base + 128]

            def finish_o(s_slot, ob, n, out_ap_fn, dst_writes):
                odd = s_slot % 2
                orow = slice(64, 128) if odd else slice(0, 64)
                drow = slice(0, 64) if odd else slice(64, 128)
                k = rr("tmpa", 2)
                S.op("act", lambda e: e.activation(out=tmpa[orow, k, 0:n], in_=psum[drow, ob, 0:n], func=AF.Ln), reads=[PSn(ob)], writes=["tmpa%d" % k])
                S.op("act", lambda e: e.activation(out=tmpa[orow, k, 0:n], in_=tmpa[orow, k, 0:n], func=AF.Exp, scale=-1.0), reads=["tmpa%d" % k], writes=["tmpa%d" % k])
                return orow, k

            def mla():
                def compA(slot):
                    w = wv(slot, 416)
                    sres = wres(slot)
                    bq = [[None, None], [None, None]]
                    for c in range(2):
                        bks = [nb(), nb()]

                        def f(e):
                            r = None
                            for k in range(8):
                                for t in range(2):
                                    r = mm(e, PS(bks[t]), w[:, k, c * 128:(c + 1) * 128], hT[:, k, t * 512:(t + 1) * 512], k == 0, k == 7)
                            return r
                        S.op("pe", f, reads=sres + ["h0", "h1"], writes=[PSn(b) for b in bks])
                        bq[c] = bks
                    for t in range(2):
                        for c in range(2):
                            S.op("act", lambda e: e.activation(out=tmpb[:, c, :], in_=PS(bq[c][t]), func=AF.Identity), reads=[PSn(bq[c][t])], writes=["tmpb%d" % c])
                        r = sumsq_rstd([(tmpb[:, c, :], ["tmpb%d" % c]) for c in range(2)], 256, "cq")
                        for c in range(2):
                            S.op("dve", lambda e: e.scalar_tensor_tensor(out=cqnT[:, c, t * 512:(t + 1) * 512], in0=tmpb[:, c, :], scalar=smallv[:, O_QN + l * 2 + c:O_QN + l * 2 + c + 1],
                                                                         in1=rstd[:, r, :], op0=ALU.mult, op1=ALU.mult),
                                 reads=["tmpb%d" % c, "rstd%d" % r, "c:smallv"], writes=["cqn"])
                    def ev_ckv(t, b):
                        S.op("act", lambda e: e.activation(out=tmpb[:, 0, :], in_=PS(b), func=AF.Identity), reads=[PSn(b)], writes=["tmpb0"])
                        r = sumsq_rstd([(tmpb[:, 0, :], ["tmpb0"])], 128, "ckv")
                        S.op("dve", lambda e: e.scalar_tensor_tensor(out=ckvT[:, koff + t * 512:koff + (t + 1) * 512], in0=tmpb[:, 0, :], scalar=smallv[:, O_KVN + l:O_KVN + l + 1],
                                                                     in1=rstd[:, r, :], op0=ALU.mult, op1=ALU.mult),
                             reads=["tmpb0", "rstd%d" % r, "c:smallv"], writes=["ckvT"])
                    proj_fm(w, 256, 128, sres, ev_ckv)
                    if g == 0:
                        def ev_kr(t, b):
                            for h in range(4):
                                S.op("act" if h % 2 else "dve",
                                     lambda e: (e.activation(out=Kb[64:96, h, t * 512:(t + 1) * 512], in_=psum[64:96, b, :], func=AF.Identity) if h % 2
                                                else e.tensor_copy(out=Kb[64:96, h, t * 512:(t + 1) * 512], in_=psum[64:96, b, :])),
                                     reads=[PSn(b)], writes=["K"])
                        proj_fm(w, 320, 96, sres, ev_kr)
                    else:
                        rot_cols(wrot[:, :, 0:96][:, :, 64:96], w[:, :, 384:416], 32, sres, ["wrotA"])
                        bks = [nb(), nb()]
                        brs = [nb(), nb()]

                        def f(e):
                            r = None
                            for k in range(8):
                                for t in range(2):
                                    r = mm(e, psum[0:96, bks[t], :], w[:, k, 320:416], hT[:, k, t * 512:(t + 1) * 512], k == 0, k == 7)
                                    r = mm(e, psum[0:96, brs[t], :], wrot[:, k, 0:96], hT[:, k, t * 512:(t + 1) * 512], k == 0, k == 7)
                            return r
                        S.op("pe", f, reads=sres + ["wrotA", "h0", "h1"], writes=[PSn(b) for b in bks + brs])
                        for t in range(2):
                            tok = slice(t * 512, (t + 1) * 512)
                            rope_evac(bks[t], brs[t], 64, 96, Kb[64:96, 0, 256 + t * 512:256 + (t + 1) * 512], tok, [], ["K"])
                            for h in range(1, 4):
                                S.op("act" if h % 2 else "dve",
                                     lambda e: (e.activation(out=Kb[64:96, h, 256 + t * 512:256 + (t + 1) * 512], in_=Kb[64:96, 0, 256 + t * 512:256 + (t + 1) * 512], func=AF.Identity) if h % 2
                                                else e.tensor_copy(out=Kb[64:96, h, 256 + t * 512:256 + (t + 1) * 512], in_=Kb[64:96, 0, 256 + t * 512:256 + (t + 1) * 512])),
                                     reads=["K"], writes=["K"])
                    if g == 0:
                        def ev_tm(ch, b):
                            def pre(k):
                                p = ch % 2
                                cs, cr = 6 + 2 * p, 7 + 2 * p
                                S.op("act", lambda e: e.activation(out=tmpb[:, p, 0:160], in_=psum[:, b, 0:160], func=AF.Identity), reads=[PSn(b)], writes=["tmpb%d" % p])
                                S.op("dve", lambda e: e.memset(lamt[:, cs:cs + 1], 0.0), writes=["lamt%d" % cs])
                                S.op("dve", lambda e: e.scalar_tensor_tensor(out=tmpb[:, p, 256:384], in0=tmpb[:, p, 0:128], scalar=1.0, in1=tmpb[:, p, 0:128], op0=ALU.mult, op1=ALU.mult,
                                                                             accum_out=lamt[:, cs:cs + 1]),
                                     reads=["tmpb%d" % p], writes=["tmpbj%d" % p, "lamt%d" % cs])
                                S.op("act", lambda e: e.activation(out=lamt[:, cr:cr + 1], in_=lamt[:, cs:cs + 1], func=AF.Ln, scale=1.0 / 128, bias=epsc[:, 0:1]), reads=["lamt%d" % cs, "c:eps"], writes=["lamt%d" % cr])
                                S.op("act", lambda e: e.activation(out=lamt[:, cr:cr + 1], in_=lamt[:, cr:cr + 1], func=AF.Exp, scale=-0.5), reads=["lamt%d" % cr], writes=["lamt%d" % cr])
                                S.op("dve", lambda e: e.scalar_tensor_tensor(out=stg[:, k, 0:128], in0=tmpb[:, p, 0:128], scalar=lamt[:, cr:cr + 1], in1=gkvb[:, l * 128:(l + 1) * 128],
                                                                             op0=ALU.mult, op1=ALU.mult),
                                     reads=["tmpb%d" % p, "lamt%d" % cr, "c:gkvb"], writes=["stg%d_0" % k])
                                S.op("dve", lambda e: e.tensor_copy(out=stg[:, k, 128:160], in_=tmpb[:, p, 128:160]), reads=["tmpb%d" % p], writes=["stg%d_0" % k])
                            stage_out(ch, 0, 160, b, lambda s_, i0: [(sckv_d[s_, l, i0:i0 + 128, :], 0, 128), (skr_d[s_, l, i0:i0 + 128, :], 128, 160)], pre=pre)
                        proj_tm(w, 256, 160, sres, ev_tm)
                job(compA, wblock(0, 416))

                def compM(_):
                    if g == 1:
                        S.op("dve", lambda e: e.memset(kst[:, :, 128:192], 0.0), writes=["kst"])
                        S.dma("pool", kst[:, :, 0:128], cckv_d[l].rearrange("(c p) f -> p c f", p=128), writes=["kst"], deps=S.bar_toks)
                        S.dma("pool", kst[:, :, 192:224], ckr_d[l].rearrange("(c p) f -> p c f", p=128), writes=["kst"], deps=S.bar_toks)
                        pst = psum[:, 7, :].bitcast(BF16)
                        for c in range(2):
                            S.op("pe", lambda e: (e.transpose(pst[:, c * 256:c * 256 + 128], kst[:, c, 0:128], ident_b[:]),
                                                  e.transpose(pst[0:96, c * 256 + 128:c * 256 + 256], kst[:, c, 128:224], ident_b[:]))[1],
                                 reads=["kst", "c:identb"], writes=[PSn(7)])
                        bank_i[0] = 0
                        for c in range(2):
                            S.op("dve", lambda e: e.tensor_copy(out=ckvT[:, c * 128:(c + 1) * 128], in_=pst[:, c * 256:c * 256 + 128]), reads=[PSn(7)], writes=["ckvT"])
                            for h in range(4):
                                S.op("act" if h % 2 else "dve",
                                     lambda e: (e.activation(out=Kb[64:96, h, c * 128:(c + 1) * 128], in_=pst[64:96, c * 256 + 128:c * 256 + 256], func=AF.Identity) if h % 2
                                                else e.tensor_copy(out=Kb[64:96, h, c * 128:(c + 1) * 128], in_=pst[64:96, c * 256 + 128:c * 256 + 256])),
                                     reads=[PSn(7)], writes=["K"])
                    for h in range(4):
                        bks = [nb(), nb()]
                        brs = [nb(), nb()] if g == 1 else None

                        def f(e):
                            r = None
                            for k in range(2):
                                for t in range(2):
                                    r = mm(e, psum[0:96, bks[t], :], wuq_s[:, k, h * 96:(h + 1) * 96], cqnT[:, k, t * 512:(t + 1) * 512], k == 0, k == 1)
                                    if g == 1:
                                        r = mm(e, psum[0:96, brs[t], :], wuqr_s[:, k, h * 96:(h + 1) * 96], cqnT[:, k, t * 512:(t + 1) * 512], k == 0, k == 1)
                            return r
                        S.op("pe", f, reads=["wuq", "wuqr", "cqn"], writes=[PSn(b) for b in bks + (brs or [])])
                        for t in range(2):
                            tok = slice(t * 512, (t + 1) * 512)
                            if g == 0:
                                S.op("act", lambda e: e.activation(out=Qb[0:96, h, tok], in_=psum[0:96, bks[t], :], func=AF.Identity), reads=[PSn(bks[t])], writes=["Q"])
                            else:
                                S.op("act", lambda e: e.activation(out=Qb[0:64, h, tok], in_=psum[0:64, bks[t], :], func=AF.Identity), reads=[PSn(bks[t])], writes=["Q"])
                                rope_evac(bks[t], brs[t], 64, 96, Qb[64:96, h, tok], tok, [], ["Q"])
                    nk_t = nkeys // 512 if g == 0 else None
                    kslices = [(i * 512, 512) for i in range(2)] if g == 0 else [(0, 512), (512, 512), (1024, 256)]
                    for h in range(4):
                        for (k0, kn) in kslices:
                            b = nb()
                            S.op("pe", lambda e: mm(e, psum[0:64, b, 0:kn], wukv_s[:, h * 64:(h + 1) * 64], ckvT[:, k0:k0 + kn], True, True),
                                 reads=["wukv", "ckvT"], writes=[PSn(b)])
                            S.op("act" if h % 2 else "dve",
                                 lambda e: (e.activation(out=Kb[0:64, h, k0:k0 + kn], in_=psum[0:64, b, 0:kn], func=AF.Identity) if h % 2
                                            else e.tensor_copy(out=Kb[0:64, h, k0:k0 + kn], in_=psum[0:64, b, 0:kn])),
                                 reads=[PSn(b)], writes=["K"])
                    for vch in range(nvch):
                        b = nb()
                        S.op("pe", lambda e: mm(e, psum[:, b, 0:256], ckvT[:, vch * 128:(vch + 1) * 128], wukv_s[:, 256:512], True, True),
                             reads=["wukv", "ckvT"], writes=[PSn(b)])
                        vv = Vb[:, vch, :].rearrange("p (q s) -> p q s", s=192)
                        src = psum[:, b, 0:256].rearrange("p (q two d) -> p q two d", two=2, d=64)
                        S.op("act", lambda e: e.activation(out=vv[:, 0:2, 0:64], in_=src[:, :, 0, :], func=AF.Identity), reads=[PSn(b)], writes=["V"])
                        S.op("dve", lambda e: e.tensor_copy(out=vv[:, 0:2, 128:192], in_=src[:, :, 1, :]), reads=[PSn(b)], writes=["V"])
                    steps = []
                    for h in range(4):
                        steps += dense_steps(lambda ks, h=h: Kb[0:96, h, ks], lambda qs, h=h: Qb[0:96, h, qs], h, MLA_SCALE, std_fin(h))
                    run_attn(steps)
                job(compM)

            REG = (2, 4, 6)

            def region(rb):
                return psum[:, rb:rb + 2, :].rearrange("p a b -> p (a b)")

            def std_fin(oslot):
                def fin(ob, q0, qn):
                    orow, k = finish_o(oslot, ob, qn, None, None)
                    S.op("dve", lambda e: e.tensor_tensor(out=oT[orow, oslot // 2, q0:q0 + qn], in0=psum[orow, ob, 0:qn], in1=tmpa[orow, k, 0:qn], op=ALU.mult),
                         reads=[PSn(ob), "tmpa%d" % k], writes=["oT"])
                return fin

            def dense_steps(KT, QT, vslot, scale, fin, only_units=None):
                steps = []
                if g == 0:
                    return dense_steps_prompt(KT, QT, vslot, scale, fin, only_units)
                else:
                    units = [(qb * 512, 512, [[(c * 128, c) for c in (2 * i, 2 * i + 1)] for i in range(5)]) for qb in range(2)]
                for ui, (q0, qn, groups_) in enumerate(units):
                    if only_units is not None and ui not in only_units:
                        continue
                    for gi, kcs in enumerate(groups_):
                        first, last = gi == 0, gi == len(groups_) - 1

                        def S_(rb, kcs=kcs, q0=q0, qn=qn):
                            reg = region(rb)

                            def f(e):
                                r = None
                                for jj, (k0, vch) in enumerate(kcs):
                                    r = mm(e, reg[:, jj * qn:(jj + 1) * qn], KT(slice(k0, k0 + 128)), QT(slice(q0, q0 + qn)), True, True)
                                return r
                            S.op("pe", f, reads=["K", "Q"], writes=[PSn(rb), PSn(rb + 1)])

                        def E_(rb, k, kcs=kcs, qn=qn):
                            reg = region(rb)
                            S.op("act", lambda e: e.activation(out=Pb[:, k, 0:len(kcs) * qn], in_=reg[:, 0:len(kcs) * qn], func=AF.Exp, scale=scale),
                                 reads=[PSn(rb), PSn(rb + 1)], writes=["P%d" % k])

                        def PV_(ob, k, kcs=kcs, qn=qn, first=first, last=last):
                            def f2(e):
                                r = None
                                for jj, (k0, vch) in enumerate(kcs):
                                    r = mm(e, psum[:, ob, 0:qn], lhs_v(vch, vslot), Pb[:, k, jj * qn:(jj + 1) * qn], first and jj == 0, last and jj == len(kcs) - 1)
                                return r
                            S.op("pe", f2, reads=["V", "P%d" % k], writes=[PSn(ob)])
                        steps.append({"S": S_, "E": E_, "PV": PV_, "first": first, "last": last, "fin": (lambda ob, q0=q0, qn=qn: fin(ob, q0, qn))})
                return steps

            def dense_steps_prompt(KT, QT, vslot, scale, fin, only_units=None):
                steps = []
                for ui in range(2):
                    if only_units is not None and ui not in only_units:
                        continue
                    q0 = ui * 512

                    def S_(rb, ui=ui):
                        reg = region(rb)

                        def f(e):
                            r = None
                            for sq_ in range(2):
                                s_ = ui * 2 + sq_
                                for c in range(2):
                                    r = mm(e, reg[:, sq_ * 512 + c * 256:sq_ * 512 + (c + 1) * 256], KT(slice(s_ * 256 + c * 128, s_ * 256 + (c + 1) * 128)),
                                           QT(slice(s_ * 256, (s_ + 1) * 256)), True, True)
                            return r
                        S.op("pe", f, reads=["K", "Q"], writes=[PSn(rb), PSn(rb + 1)])

                    def E_(rb, k):
                        reg = region(rb)
                        S.op("act", lambda e: e.activation(out=Pb[:, k, :], in_=reg[:, :], func=AF.Exp, scale=scale), reads=[PSn(rb), PSn(rb + 1)], writes=["P%d" % k])

                    def PV_(ob, k, ui=ui):
                        def f2(e):
                            r = None
                            for sq_ in range(2):
                                s_ = ui * 2 + sq_
                                for c in range(2):
                                    r = mm(e, psum[:, ob, sq_ * 256:(sq_ + 1) * 256], lhs_v(s_ * 2 + c, vslot), Pb[:, k, sq_ * 512 + c * 256:sq_ * 512 + (c + 1) * 256], c == 0, c == 1)
                            return r
                        S.op("pe", f2, reads=["V", "P%d" % k], writes=[PSn(ob)])
                    steps.append({"S": S_, "E": E_, "PV": PV_, "first": True, "last": True, "fin": (lambda ob, q0=q0: fin(ob, q0, 512))})
                return steps

            def run_attn(steps, LA=2):
                n = len(steps)
                reg_of, p_of = {}, {}
                st_ = {"ri": 0, "oi": 0, "ob": None}

                def front(i):
                    rb = REG[st_["ri"] % 3]
                    st_["ri"] += 1
                    k = rr("P", 3)
                    reg_of[i], p_of[i] = rb, k
                    steps[i]["S"](rb)
                    steps[i]["E"](rb, k)
                for i in range(min(LA, n)):
                    front(i)
                for i in range(n):
                    if i + LA < n:
                        front(i + LA)
                    stp = steps[i]
                    if stp["first"]:
                        st_["ob"] = st_["oi"] % 2
                        st_["oi"] += 1
                    stp["PV"](st_["ob"], p_of[i])
                    if stp["last"]:
                        stp["fin"](st_["ob"])

            def diff():
                def compB(slot):
                    w = wv(slot, 512)
                    sres = wres(slot)
                    if g == 1:
                        rot_cols(wrot[:, :, 96:608], w[:, :, 0:512], 512, sres, ["wrotB"])
                    for qk in range(2):
                        for h in range(4):
                            col0 = qk * 256 + h * 64
                            if g == 0:
                                def ev(t, b, qk=qk, h=h):
                                    dst = (Qb if qk == 0 else Kb)[0:64, h, t * 512:(t + 1) * 512]
                                    S.op("act" if h % 2 else "dve",
                                         lambda e: (e.activation(out=dst, in_=psum[0:64, b, :], func=AF.Identity) if h % 2 else e.tensor_copy(out=dst, in_=psum[0:64, b, :])),
                                         reads=[PSn(b)], writes=["Q" if qk == 0 else "K"])
                                proj_fm(w, col0, 64, sres, ev)
                            else:
                                bks = [nb(), nb()]
                                brs = [nb(), nb()]

                                def f(e):
                                    r = None
                                    for k in range(8):
                                        for t in range(2):
                                            r = mm(e, psum[0:64, bks[t], :], w[:, k, col0:col0 + 64], hT[:, k, t * 512:(t + 1) * 512], k == 0, k == 7)
                                            r = mm(e, psum[0:64, brs[t], :], wrot[:, k, 96 + col0:96 + col0 + 64], hT[:, k, t * 512:(t + 1) * 512], k == 0, k == 7)
                                    return r
                                S.op("pe", f, reads=sres + ["wrotB", "h0", "h1"], writes=[PSn(b) for b in bks + brs])
                                for t in range(2):
                                    tok = slice(t * 512, (t + 1) * 512)
                                    dst = Qb[0:64, h, tok] if qk == 0 else Kb[0:64, h, 256 + t * 512:256 + (t + 1) * 512]
                                    rope_evac(bks[t], brs[t], 0, 64, dst, tok, [], ["Q" if qk == 0 else "K"])
                    if g == 0:
                        def ev_tm(ch, b):
                            stage_out(ch, 160, 256, b, lambda s_, i0: [(sdk_d[s_, l, :, i0:i0 + 128, :].rearrange("h t d -> t h d"), 0, 256)])
                        proj_tm(w, 256, 256, sres, ev_tm)
                job(compB, wblock(416, 928))

                def compC(slot):
                    w = wv(slot, 256)
                    sres = wres(slot)

                    def ev(ch, b):
                        vcopy(ch, b, 4, 0)
                        if g == 0:
                            stage_out(ch, 416, 256, b, lambda s_, i0: [(sdv_d[s_, l, :, i0:i0 + 128, :].rearrange("h t d -> t h d"), 0, 256)])
                    proj_tm(w, 0, 256, sres, ev)
                    if g == 1:
                        for c in range(2):
                            S.dma("pool", kst[:, c, 0:256].rearrange("p (h d) -> p h d", h=4), cdk_d[l][:, c * 128:(c + 1) * 128, :].rearrange("h p d -> p h d"), writes=["kst"], deps=S.bar_toks)
                        for c in range(2):
                            vv = Vb[:, c, :].rearrange("p (q s) -> p q s", s=192)
                            srcv = cdv_d[l][:, c * 128:(c + 1) * 128, :].rearrange("(q two) p d -> p q two d", two=2)
                            S.dma("pool", vv[:, 0:2, 0:64], srcv[:, :, 0, :], writes=["V"], deps=S.bar_toks)
                            S.dma("pool", vv[:, 0:2, 128:192], srcv[:, :, 1, :], writes=["V"], deps=S.bar_toks)
                        pst = psum[:, 7, :].bitcast(BF16)
                        for c in range(2):
                            S.op("pe", lambda e: [e.transpose(pst[0:64, (c * 4 + h) * 128:(c * 4 + h + 1) * 128], kst[:, c, h * 64:(h + 1) * 64], ident_b[:]) for h in range(4)][-1],
                                 reads=["kst", "c:identb"], writes=[PSn(7)])
                        bank_i[0] = 0
                        for c in range(2):
                            for h in range(4):
                                S.op("act" if h % 2 else "dve",
                                     lambda e: (e.activation(out=Kb[0:64, h, c * 128:(c + 1) * 128], in_=pst[0:64, (c * 4 + h) * 128:(c * 4 + h + 1) * 128], func=AF.Identity) if h % 2
                                                else e.tensor_copy(out=Kb[0:64, h, c * 128:(c + 1) * 128], in_=pst[0:64, (c * 4 + h) * 128:(c * 4 + h + 1) * 128])),
                                     reads=[PSn(7)], writes=["K"])
                    def subln_all():
                        for pr_ in range(2):
                            for t in range(2):
                                tok = slice(t * 512, (t + 1) * 512)
                                k = rr("sq", 3)
                                b = nb()
                                S.op("dve", lambda e: e.tensor_tensor(out=sqb[:, k, :], in0=oT[:, 2 + pr_, tok], in1=oT[:, 2 + pr_, tok], op=ALU.mult), reads=["oT"], writes=["sq%d" % k])
                                S.op("pe", lambda e: mm(e, psum[:, b, :], bones_b[:], sqb[:, k, :], True, True), reads=["sq%d" % k, "c:bones"], writes=[PSn(b)])
                                r = rr("rstd", 2)
                                S.op("act", lambda e: e.activation(out=rstd[:, r, :], in_=psum[:, b, :], func=AF.Ln, scale=1.0 / 64, bias=epsc[:, 0:1]),
                                     reads=[PSn(b), "c:eps"], writes=["rstd%d" % r])
                                S.op("act", lambda e: e.activation(out=rstd[:, r, :], in_=rstd[:, r, :], func=AF.Exp, scale=-0.5), reads=["rstd%d" % r], writes=["rstd%d" % r])
                                S.op("dve", lambda e: e.scalar_tensor_tensor(out=oT[:, 2 + pr_, tok], in0=oT[:, 2 + pr_, tok], scalar=lamt[:, 5:6], in1=rstd[:, r, :], op0=ALU.mult, op1=ALU.mult),
                                     reads=["rstd%d" % r, "lamt"], writes=["oT"])

                    steps = []
                    for pr_ in range(2):
                        for ui in range(2):
                            for hh in range(2):
                                h = pr_ * 2 + hh
                                orow = slice(64, 128) if hh else slice(0, 64)
                                res = []
                                for half in range(2):
                                    prow = slice(half * 32, half * 32 + 32)

                                    def fin(ob, q0, qn, half=half, h=h, hh=hh, res=res, orow=orow, pr_=pr_):
                                        orow_, k = finish_o(h, ob, qn, None, None)
                                        kk = rr("tmpb", 2)
                                        S.op("dve", lambda e: e.tensor_tensor(out=tmpb[orow_, kk, 0:qn], in0=psum[orow_, ob, 0:qn], in1=tmpa[orow_, k, 0:qn], op=ALU.mult),
                                             reads=[PSn(ob), "tmpa%d" % k], writes=["tmpb%d" % kk])
                                        res.append(kk)
                                        if half == 1:
                                            S.op("dve", lambda e: e.scalar_tensor_tensor(out=oT[orow, 2 + pr_, q0:q0 + qn], in0=tmpb[orow, res[1], 0:qn], scalar=lamt[orow, 4:5], in1=tmpb[orow, res[0], 0:qn],
                                                                                         op0=ALU.mult, op1=ALU.add),
                                                 reads=["tmpb%d" % res[0], "tmpb%d" % res[1], "lamt"], writes=["oT"])
                                    steps += dense_steps(lambda ks, prow=prow, h=h: Kb[prow, h, ks], lambda qs, prow=prow, h=h: Qb[prow, h, qs], h, DIFF_SCALE, fin, only_units=[ui])
                    run_attn(steps)
                    pending_subln.append(subln_all)
                job(compC, wblock(928, 1184))

            def nat():
                def compD(slot):
                    w = wv(slot, 512)
                    for c in range(4):
                        def ev(t, b, c=c):
                            S.op("act", lambda e: e.activation(out=Qb[:, c, t * 512:(t + 1) * 512], in_=PS(b), func=AF.Identity, scale=0.125), reads=[PSn(b)], writes=["Q"])
                        proj_fm(w, c * 128, 128, wres(slot), ev)
                job(compD, wblock(1184, 1696))

                def compE(slot):
                    w = wv(slot, 512)
                    for c in range(4):
                        def ev(t, b, c=c):
                            S.op("dve", lambda e: e.tensor_copy(out=Kb[:, c, koff + t * 512:koff + (t + 1) * 512], in_=PS(b)), reads=[PSn(b)], writes=["K"])
                        proj_fm(w, c * 128, 128, wres(slot), ev)
                    if g == 0:
                        def ev_tm(ch, b):
                            stage_out(ch, 672, 512, b, lambda s_, i0: [(snk_d[s_, l, :, i0:i0 + 128, :].rearrange("h t d -> t h d"), 0, 512)])
                        proj_tm(w, 0, 512, wres(slot), ev_tm)
                job(compE, wblock(1696, 2208))

                def compF(slot):
                    w = wv(slot, 512)

                    def ev(ch, b):
                        vcopy(ch, b, 8, 0)
                        if g == 0:
                            stage_out(ch, 1184, 512, b, lambda s_, i0: [(snv_d[s_, l, :, i0:i0 + 128, :].rearrange("h t d -> t h d"), 0, 512)])
                    proj_tm(w, 0, 512, wres(slot), ev)
                    if g == 0:
                        steps = []
                        for h in range(8):
                            hb = (h % 2) * 64
                            steps += dense_steps(lambda ks, hb=hb, h=h: Kb[hb:hb + 64, h // 2, ks], lambda qs, hb=hb, h=h: Qb[hb:hb + 64, h // 2, qs], h, 1.0, std_fin(8 + h))
                        run_attn(steps)
                        return
                    for c in range(2):
                        S.dma("pool", kst[:, c, 0:512].rearrange("p (h d) -> p h d", h=8), cnk_d[l][:, c * 128:(c + 1) * 128, :].rearrange("h p d -> p h d"), writes=["kst"], deps=S.bar_toks)
                    for c in range(2):
                        vv = Vb[:, c, :].rearrange("p (q s) -> p q s", s=192)
                        srcv = cnv_d[l][:, c * 128:(c + 1) * 128, :].rearrange("(q two) p d -> p q two d", two=2)
                        S.dma("pool", vv[:, 0:4, 0:64], srcv[:, :, 0, :], writes=["V"], deps=S.bar_toks)
                        S.dma("pool", vv[:, 0:4, 128:192], srcv[:, :, 1, :], writes=["V"], deps=S.bar_toks)
                    pst = psum[:, 7, :].bitcast(BF16)
                    for c in range(2):
                        S.op("pe", lambda e: [e.transpose(pst[:, (c * 4 + cc) * 128:(c * 4 + cc + 1) * 128], kst[:, c, cc * 128:(cc + 1) * 128], ident_b[:]) for cc in range(4)][-1],
                             reads=["kst", "c:identb"], writes=[PSn(7)])
                    bank_i[0] = 0
                    for c in range(2):
                        for cc in range(4):
                            S.op("act" if cc % 2 else "dve",
                                 lambda e: (e.activation(out=Kb[:, cc, c * 128:(c + 1) * 128], in_=pst[:, (c * 4 + cc) * 128:(c * 4 + cc + 1) * 128], func=AF.Identity) if cc % 2
                                            else e.tensor_copy(out=Kb[:, cc, c * 128:(c + 1) * 128], in_=pst[:, (c * 4 + cc) * 128:(c * 4 + cc + 1) * 128])),
                                 reads=[PSn(7)], writes=["K"])
                    def natb_load(hp):
                        S.dma("pool", natb_s[:, hp % 2, :], natb_d[l][:, hp * 2048:(hp + 1) * 2048], writes=["natb%d" % (hp % 2)], deps=S.bar_toks)
                    natb_load(0)
                    steps = []
                    for hp in range(4):
                        nbuf = hp % 2
                        for hh in range(2):
                            h = hp * 2 + hh
                            hb = hh * 64
                            c = hp
                            tab = natb_s[:, nbuf, hh * 1024:(hh + 1) * 1024].rearrange("p (j q) -> p j q", q=64)
                            for qb in range(2):
                                q0 = qb * 512
                                pre = (hp + 1) if (hh == 0 and qb == 0 and hp + 1 < 4) else None

                                def S0(rb, hb=hb, c=c, q0=q0, pre=pre):
                                    if pre is not None:
                                        natb_load(pre)
                                    reg = region(rb)

                                    def f(e):
                                        r = None
                                        for kc in range(2):
                                            r = mm(e, reg[:, kc * 512:(kc + 1) * 512], Kb[hb:hb + 64, c, kc * 128:(kc + 1) * 128], Qb[hb:hb + 64, c, q0:q0 + 512], True, True)
                                        return r
                                    S.op("pe", f, reads=["K", "Q"], writes=[PSn(rb), PSn(rb + 1)])

                                def E0(rb, k):
                                    reg = region(rb)
                                    S.op("act", lambda e: e.activation(out=Pb[:, k, :], in_=reg[:, :], func=AF.Exp), reads=[PSn(rb), PSn(rb + 1)], writes=["P%d" % k])

                                def PV0(ob, k, h=h):
                                    def f2(e):
                                        r = None
                                        for kc in range(2):
                                            r = mm(e, PS(ob), lhs_v(kc, h), Pb[:, k, kc * 512:(kc + 1) * 512], kc == 0, False)
                                        return r
                                    S.op("pe", f2, reads=["V", "P%d" % k], writes=[PSn(ob)])
                                steps.append({"S": S0, "E": E0, "PV": PV0, "first": True, "last": False, "fin": None})
                                for tq in range(4):
                                    t = qb * 4 + tq
                                    js, inval = nat_chunks(t)
                                    nj = len(js)

                                    def S1(rb, hb=hb, c=c, t=t, js=js, tab=tab, nbuf=nbuf):
                                        reg = region(rb)

                                        def f(e):
                                            r = None
                                            for jj, j in enumerate(js):
                                                r = mm(e, reg[:, jj * 128:(jj + 1) * 128], Kb[hb:hb + 64, c, 256 + j * 128:256 + (j + 1) * 128], Qb[hb:hb + 64, c, t * 128:(t + 1) * 128], True, False)
                                                jx0 = 8 - (2 * j - 2 * t)
                                                r = mm(e, reg[:, jj * 128:(jj + 1) * 128], ident_b[:], tab[:, jx0:jx0 + 2, :].rearrange("p j q -> p (j q)"), False, True)
                                            return r
                                        S.op("pe", f, reads=["K", "Q", "natb%d" % nbuf, "c:identb"], writes=[PSn(rb), PSn(rb + 1)])

                                    def E1(rb, k, nj=nj, inval=inval):
                                        reg = region(rb)
                                        S.op("act", lambda e: e.activation(out=Pb[:, k, 0:nj * 128], in_=reg[:, 0:nj * 128], func=AF.Exp), reads=[PSn(rb), PSn(rb + 1)], writes=["P%d" % k])
                                        for (jj, a, b_) in inval:
                                            S.op("dve", lambda e: e.memset(Pb[a * 64:(a + 1) * 64, k, jj * 128 + b_ * 64:jj * 128 + b_ * 64 + 64], 0.0), writes=["P%d" % k])

                                    def PV1(ob, k, h=h, js=js, nj=nj, tq=tq):
                                        def f2(e):
                                            r = None
                                            for jj, j in enumerate(js):
                                                r = mm(e, psum[:, ob, tq * 128:(tq + 1) * 128], lhs_v(2 + j, h), Pb[:, k, jj * 128:(jj + 1) * 128], False, jj == nj - 1)
                                            return r
                                        S.op("pe", f2, reads=["V", "P%d" % k], writes=[PSn(ob)])

                                    def fin1(ob, h=h, c=c, q0=q0):
                                        orow, k = finish_o(h, ob, 512, None, None)
                                        S.op("dve", lambda e: e.tensor_tensor(out=oT[orow, 4 + c, q0:q0 + 512], in0=psum[orow, ob, :], in1=tmpa[orow, k, :], op=ALU.mult),
                                             reads=[PSn(ob), "tmpa%d" % k], writes=["oT"])
                                    steps.append({"S": S1, "E": E1, "PV": PV1, "first": False, "last": tq == 3, "fin": fin1})
                    run_attn(steps)
                job(compF, wblock(2208, 2720))

            sel = ("mla", "diff", "nat") if mixsel == "all" else tuple(mixsel.split(","))
            if "mla" in sel:
                mla()
            if "diff" in sel:
                diff()
            if "nat" in sel:
                nat()
            if mixsel != "all":
                def z(_):
                    for c in range(8):
                        typ = "mla" if c < 2 else ("diff" if c < 4 else "nat")
                        if typ not in sel:
                            S.op("dve", lambda e: e.memset(oT[:, c, :], 0.0), writes=["oT"])
                job(z)

            job(lambda _: presum_begin())
            for dq_ in range(2):
                def load(slot, dq_=dq_):
                    dst = wsl[:, slot, :].rearrange("p (k f) -> p k f", k=8)
                    wdma(slot, dst, wout_d[l].rearrange("(k p) f -> p k f", p=128)[:, :, dq_ * 512:(dq_ + 1) * 512])

                def comp(slot, dq_=dq_):
                    w = wsl[:, slot, :].rearrange("p (k f) -> p k f", k=8)
                    while pending_subln:
                        pending_subln.pop(0)()
                    for dd in range(4):
                        dc = dq_ * 4 + dd
                        bks = [nb(), nb()]

                        def f(e):
                            r = None
                            for k in range(8):
                                for t in range(2):
                                    r = mm(e, PS(bks[t]), w[:, k, dd * 128:(dd + 1) * 128], oT[:, k, t * 512:(t + 1) * 512], k == 0, k == 7)
                            return r
                        S.op("pe", f, reads=wres(slot) + ["oT"], writes=[PSn(b) for b in bks])
                        for t in range(2):
                            xs_ = xT[:, dc, t * 512:(t + 1) * 512]
                            S.op("dve", lambda e: e.scalar_tensor_tensor(out=xs_, in0=PS(bks[t]), scalar=coef[:, 5, dc:dc + 1], in1=xs_, op0=ALU.mult, op1=ALU.add),
                                 reads=[PSn(bks[t]), "coef", "x%d" % t], writes=["x%d" % t])
                            presum_add(dc, t)
                job(comp, load)

        for g in groups:
            load_x(g)
            barrier_job()
            done = False
            for l in range(depth):
                make_coef(g, l)
                if stop == (l, 0):
                    break
                ffn(l, 0, 0, 1, 2)
                barrier_job()
                if stop == (l, 1):
                    break
                mixer(g, l)
                barrier_job()
                if stop == (l, 2):
                    break
                ffn(l, 1, 6, 7, 8)
                barrier_job()
                if stop == (l, 3):
                    break
            store_y(g)
            barrier_job()
        run_jobs()
        for q in S.dq.values():
            for s_ in q["sems"]:
                if s_[1] > 0:
                    S._wait("sp", (s_[0], s_[1]))
        if stats is not None:
            stats.update({"ninst": dict(S.ninst), "nwait": dict(S.nwait), "nsem": S.nsem})
    return nc


def _consts():
    ident = np.eye(128, dtype=np.float32)
    t = np.arange(1024)
    freqs = (np.float32(10000.0) ** (-np.arange(8, dtype=np.float32) / np.float32(8))).astype(np.float32)
    cos = np.zeros((128, 1024), np.float32)
    sin = np.zeros((128, 1024), np.float32)
    for p in range(128):
        r = p % 32
        pos = (t // 64) if r < 16 else (t % 64)
        ang = pos.astype(np.float32) * freqs[r % 8]
        cos[p] = np.cos(ang).astype(np.float32)
        sin[p] = np.sin(ang).astype(np.float32)
    return np.concatenate([ident, cos, sin], axis=1)


def _natb(rpb):
    Ln = rpb.shape[0]
    ck = np.arange(64)[:, None]
    cq = np.arange(64)[None, :]
    cs = np.clip(cq - 8, 0, 48)
    inwin = (ck >= cs) & (ck < cs + 16)
    dc = np.clip(ck - cq, -15, 15) + 15
    out = np.zeros((Ln, 128, 8, 16, 64), np.float32)
    for half in range(2):
        for jx in range(16):
            dr = (15 - jx) if half == 0 else (16 - jx)
            if 0 <= dr <= 14:
                val = np.where(inwin[None, None], rpb[:, :, dr][:, :, dc], np.float32(NEG))
                out[:, half * 64:(half + 1) * 64, :, jx, :] = np.transpose(val, (0, 2, 1, 3))
    return out.reshape(Ln, 128, 8 * 16 * 64)


def _smallv(inp, core):
    b = core // 2
    sv = np.zeros((128, NSV), np.float32)

    def fm(v):
        v = np.asarray(v, np.float32)
        return np.moveaxis(v.reshape(v.shape[:-1] + (-1, 128)), -1, 0)
    sv[:, O_N1:O_N1 + 32] = fm(inp["ffn1_norm"]).reshape(128, -1)
    sv[:, O_NM:O_NM + 32] = fm(inp["mix_norm"]).reshape(128, -1)
    sv[:, O_N2:O_N2 + 32] = fm(inp["ffn2_norm"]).reshape(128, -1)
    sv[:, O_FN:O_FN + 8] = fm(inp["final_norm"]).reshape(128, -1)
    sv[:, O_BADA:O_BADA + 288] = fm(inp["b_ada"]).reshape(128, -1)
    cv = np.stack([inp["c_ctx"], inp["c"][b]], axis=0)
    sv[:, O_C:O_C + 16] = np.transpose(fm(cv), (0, 2, 1)).reshape(128, 16)
    sv[:, O_QN:O_QN + 8] = fm(inp["mla_q_norm"]).reshape(128, -1)
    sv[:, O_KVN:O_KVN + 4] = fm(inp["mla_kv_norm"]).reshape(128, -1)
    sv[:, O_SL:O_SL + 4] = np.concatenate([inp["diff_subln"].T, inp["diff_subln"].T], axis=0)
    lam = np.stack([inp["diff_lambda_q1"], inp["diff_lambda_k1"], inp["diff_lambda_q2"], inp["diff_lambda_k2"]], axis=1)
    sv[:, O_LAM:O_LAM + 512] = np.broadcast_to(lam.reshape(1, -1), (128, 512))
    return sv


def make_in_maps(inp, cores=range(NCORES)):
    f = lambda a: np.ascontiguousarray(np.asarray(a, np.float32))
    cst = _consts()
    natb = _natb(np.asarray(inp["nat_rpb"], np.float32))
    gkvb = np.ascontiguousarray(np.broadcast_to(np.asarray(inp["mla_kv_norm"], np.float32).reshape(1, -1), (128, L * 128)))
    shared = {
        "cst": cst, "natb": natb, "gkvb": gkvb,
        "w_ada": f(inp["w_ada"]), "wg1": f(inp["ffn1_w_gate"]), "wu1": f(inp["ffn1_w_up"]), "wd1": f(inp["ffn1_w_down"]),
        "wg2": f(inp["ffn2_w_gate"]), "wu2": f(inp["ffn2_w_up"]), "wd2": f(inp["ffn2_w_down"]),
        "w_in": f(inp["w_in"]), "wuq": f(inp["mla_w_uq"]), "wukv": f(inp["mla_w_ukv"]), "w_out": f(inp["w_out"]),
    }
    maps = []
    for c in cores:
        b = c // 2
        m = dict(shared)
        m["xp"] = f(inp["x_prompt"][4 * c:4 * c + 4]).reshape(TG, D)
        m["xs"] = f(inp["x_sample"][b])
        m["smallv"] = _smallv(inp, c)
        m["c_ckv"] = f(inp["cache_mla_ckv"][b]); m["c_krope"] = f(inp["cache_mla_krope"][b])
        m["c_dk"] = f(inp["cache_diff_k"][b]); m["c_dv"] = f(inp["cache_diff_v"][b])
        m["c_nk"] = f(inp["cache_nat_k"][b]); m["c_nv"] = f(inp["cache_nat_v"][b])
        maps.append(m)
    return maps


def kernel(**inputs):
    nc = build_program()
    in_maps = make_in_maps(inputs)
    res = run_bass_kernel_spmd(nc, in_maps, core_ids=list(range(NCORES)))
    r = res.results
    y_p = np.concatenate([r[c]["y_p"].reshape(4, 256, D) for c in range(NCORES)], axis=0)
    y_s = np.stack([r[2 * b]["y_s"] for b in range(4)], axis=0)
    outs = [y_p, y_s]
    for k in ("st_ckv", "st_krope", "st_dk", "st_dv", "st_nk", "st_nv"):
        outs.append(np.concatenate([r[c][k] for c in range(NCORES)], axis=0))
    return tuple(np.ascontiguousarray(o, dtype=np.float32) for o in outs)
```

```python
import contextlib
import math
import numpy as np
import concourse.bass as bass
import concourse.mybir as mybir
from concourse.bass_utils import run_bass_kernel_spmd

F32 = mybir.dt.float32
BF16 = mybir.dt.bfloat16
AF = mybir.ActivationFunctionType
ALU = mybir.AluOpType

D = 1024
FF = 2816
L = 4
NCORES = 8
TG = 1024
EPS = 1e-6
MLA_SCALE = 96 ** -0.5
DIFF_SCALE = 32 ** -0.5
NSLOT = 4
SLOT_EL = 4096
NEG = -1e30

O_N1, O_NM, O_N2, O_FN, O_BADA, O_C, O_QN, O_KVN, O_SL, O_LAM = 0, 32, 64, 96, 104, 392, 408, 416, 420, 424
NSV = 424 + 512


class Sched:
    COMPUTE = ("pe", "act", "dve", "pool")

    def __init__(self, nc, stack, ndma_sems=8):
        self.nc = nc
        self.stack = stack
        self.eng = {"pe": nc.tensor, "act": nc.scalar, "dve": nc.vector, "pool": nc.gpsimd, "sp": nc.sync}
        self.nsem = 0
        self.csem = {}
        self.pe_sems = set()
        for e in self.COMPUTE:
            self.csem[e] = [self._newsem(e), 0]
        self.pe_sems.add(id(self.csem["pe"][0]))
        self.dq = {}
        for q, e in (("sp", "sp"), ("pool", "pool")):
            self.dq[q] = {"eng": e, "sems": [[self._newsem("d" + q), 0] for _ in range(ndma_sems)], "i": 0}
        self.waited = {e: {} for e in self.eng}
        self.last_w = {}
        self.readers = {}
        self.ninst = {e: 0 for e in self.eng}
        self.nwait = {e: 0 for e in self.eng}
        self.bar_toks = []

    def _newsem(self, tag):
        self.nsem += 1
        return self.stack.enter_context(self.nc.semaphore("s%s%d" % (tag, self.nsem)))

    def _wait(self, e, tok):
        sem, val = tok
        w = self.waited[e]
        k = id(sem)
        if w.get(k, 0) >= val:
            return
        w[k] = val
        self.eng[e].wait_ge(sem, val)
        self.nwait[e] += 1

    def _deps(self, reads, writes, deps):
        d = {}

        def add(t):
            k = id(t[0])
            if k not in d or d[k][1] < t[1]:
                d[k] = t
        for t in deps:
            add(t)
        for r in reads:
            t = self.last_w.get(r)
            if t is not None:
                add(t)
        for w in writes:
            t = self.last_w.get(w)
            if t is not None:
                add(t)
            for t in self.readers.get(w, {}).values():
                add(t)
        return list(d.values())

    def _record(self, tok, reads, writes):
        for r in reads:
            if not r.startswith("c:"):
                self.readers.setdefault(r, {})[id(tok[0])] = tok
        for w in writes:
            self.last_w[w] = tok
            self.readers[w] = {}

    def op(self, e, fn, reads=(), writes=(), deps=()):
        pr = [r for r in reads if r.startswith("ps")]
        if pr:
            reads = [r for r in reads if not r.startswith("ps")]
            writes = list(writes) + pr
        for t in self._deps(reads, writes, deps):
            if e == "pe" and id(t[0]) in self.pe_sems:
                continue
            self._wait(e, t)
        inst = fn(self.eng[e])
        cs = self.csem[e]
        if cs[1] >= 30000:
            cs[0] = self._newsem(e)
            cs[1] = 0
            if e == "pe":
                self.pe_sems.add(id(cs[0]))
        cs[1] += 1
        inst.then_inc(cs[0], 1)
        tok = (cs[0], cs[1])
        self.ninst[e] += 1
        self._record(tok, reads, writes)
        return tok

    def dma(self, q, out, in_, reads=(), writes=(), deps=(), **kw):
        dq = self.dq[q]
        e = dq["eng"]
        for t in self._deps(reads, writes, deps):
            self._wait(e, t)
        slot = dq["sems"][dq["i"] % len(dq["sems"])]
        dq["i"] += 1
        if slot[1] >= 30000:
            slot[0] = self._newsem("d" + q)
            slot[1] = 0
        if slot[1] > 0:
            self._wait(e, (slot[0], slot[1]))
        inst = self.eng[e].dma_start(out=out, in_=in_, **kw)
        slot[1] += 16
        inst.then_inc(slot[0], 16)
        tok = (slot[0], slot[1])
        self.ninst[e] += 1
        self._record(tok, reads, writes)
        return tok

    def latest(self):
        toks = []
        for e in self.COMPUTE:
            cs = self.csem[e]
            if cs[1] > 0:
                toks.append((cs[0], cs[1]))
        for q in self.dq.values():
            for s in q["sems"]:
                if s[1] > 0:
                    toks.append((s[0], s[1]))
        return toks

    def barrier(self, engines=("pe", "act", "dve", "sp")):
        toks = self.latest()
        self.bar_toks = toks
        for e in engines:
            for t in toks:
                if e == "pe" and id(t[0]) in self.pe_sems:
                    continue
                self._wait(e, t)


def nat_chunks(t):
    def rs(r):
        return min(max(r - 4, 0), 8)
    lo = rs(2 * t)
    hi = rs(2 * t + 1) + 7
    js = list(range(lo // 2, hi // 2 + 1))
    inval = []
    for jj, j in enumerate(js):
        for a in range(2):
            for b in range(2):
                rk, rq = 2 * j + a, 2 * t + b
                if not (rs(rq) <= rk < rs(rq) + 8):
                    inval.append((jj, a, b))
    return js, inval


def build_program(depth=L, stop=None, groups=(0, 1), mixsel="all", stats=None):
    nc = bass.Bass("TRN2", target_bir_lowering=False)

    def din(name, shape):
        return nc.dram_tensor(name, list(shape), F32, kind="ExternalInput").ap()

    def dout(name, shape):
        return nc.dram_tensor(name, list(shape), F32, kind="ExternalOutput").ap()

    xp_d = din("xp", [TG, D]); xs_d = din("xs", [TG, D])
    smallv_d = din("smallv", [128, NSV]); gkvb_d = din("gkvb", [128, L * 128]); cst_d = din("cst", [128, 128 + 2048])
    natb_d = din("natb", [L, 128, 8 * 1024])
    cckv_d = din("c_ckv", [L, 256, 128]); ckr_d = din("c_krope", [L, 256, 32])
    cdk_d = din("c_dk", [L, 4, 256, 64]); cdv_d = din("c_dv", [L, 4, 256, 64])
    cnk_d = din("c_nk", [L, 8, 256, 64]); cnv_d = din("c_nv", [L, 8, 256, 64])
    wada_d = din("w_ada", [L, D, 9 * D])
    wg_d = [din("wg1", [L, D, FF]), din("wg2", [L, D, FF])]
    wu_d = [din("wu1", [L, D, FF]), din("wu2", [L, D, FF])]
    wd_d = [din("wd1", [L, FF, D]), din("wd2", [L, FF, D])]
    win_d = din("w_in", [L, D, 2720]); wuq_d = din("wuq", [L, 256, 384]); wukv_d = din("wukv", [L, 128, 512])
    wout_d = din("w_out", [L, D, D])
    yp_d = dout("y_p", [TG, D]); ys_d = dout("y_s", [TG, D])
    sckv_d = dout("st_ckv", [4, L, 256, 128]); skr_d = dout("st_krope", [4, L, 256, 32])
    sdk_d = dout("st_dk", [4, L, 4, 256, 64]); sdv_d = dout("st_dv", [4, L, 4, 256, 64])
    snk_d = dout("st_nk", [4, L, 8, 256, 64]); snv_d = dout("st_nv", [4, L, 8, 256, 64])

    with contextlib.ExitStack() as st:
        S = Sched(nc, st)

        def sb(name, shape, dt=F32):
            return st.enter_context(nc.sbuf_tensor("sb_" + name, list(shape), dt))

        xT = sb("xT", [128, 8, TG])
        hT = sb("hT", [128, 8, TG], BF16)
        wsl = sb("wsl", [128, NSLOT, SLOT_EL], BF16)
        smallv = sb("smallv", [128, NSV])
        gkvb = sb("gkvb", [128, L * 128])
        cst = sb("cst", [128, 128 + 2048])
        ada = sb("ada", [128, L, 72, 2])
        coef = sb("coef", [128, 9, 8])
        csil = sb("csil", [128, 8, 2], BF16)
        ident_b = sb("ident_b", [128, 128], BF16)
        ones_b = sb("ones_b", [128, 128], BF16)
        bones_b = sb("bones_b", [128, 128], BF16)
        epsc = sb("epsc", [128, 1])
        rstd = sb("rstd", [128, 2, 512])
        tmpa = sb("tmpa", [128, 2, 512])
        tmpb = sb("tmpb", [128, 2, 512])
        sqb = sb("sqb", [128, 3, 512], BF16)
        wuq_s = sb("wuq_s", [128, 2, 384], BF16)
        wuqr_s = sb("wuqr_s", [128, 2, 384], BF16)
        wukv_s = sb("wukv_s", [128, 512], BF16)
        wrot = sb("wrot", [128, 8, 608], BF16)
        lamt = sb("lamt", [128, 12])
        arena = sb("arena", [128, 40960], BF16)
        psum = st.enter_context(nc.psum_tensor("psum", [128, 8, 512], F32))
        ident_f = cst[:, 0:128]
        cosT = cst[:, 128:128 + 1024]
        sinT = cst[:, 1152:1152 + 1024]

        def carve(off, shape, dt=BF16):
            n = 1
            for s_ in shape[1:]:
                n *= s_
            mult = 2 if dt == F32 else 1
            base = arena[:, off:off + n * mult]
            if dt == F32:
                base = base.bitcast(F32)
            names = "abcdefg"[:len(shape) - 1]
            if len(shape) > 2:
                pat = "p (" + " ".join(names) + ") -> p " + " ".join(names)
                kw = {names[i]: shape[i + 1] for i in range(len(shape) - 1)}
                base = base.rearrange(pat, **kw)
            return base[0:shape[0]], off + n * mult

        actT, _ = carve(0, [128, 22, TG])
        o = 0
        Kb, o = carve(o, [128, 4, 1280])
        Vb, o = carve(o, [128, 10, 768])
        Qb, o = carve(o, [128, 4, TG])
        oT, o = carve(o, [128, 8, TG])
        ckvT, o = carve(o, [128, 1280])
        cqnT, o = carve(o, [128, 2, TG])
        Pb, o = carve(o, [128, 3, 1024])
        odT, o = carve(o, [128, 512], F32)
        o_shared = o
        natb_s, o = carve(o, [128, 2, 2048])
        kst, o = carve(o, [128, 2, 1024])
        stg, o2 = carve(o_shared, [128, 2, 1696], F32)
        assert max(o, o2) <= 40960, (o, o2)
        xstage, _ = carve(0, [128, 2, D], F32)

        bank_i = [0]

        presum = {"on": False, "pend": [], "cnt": [0, 0]}

        def nb():
            while True:
                b = bank_i[0] % 8
                bank_i[0] += 1
                if presum["on"] and b in (6, 7):
                    continue
                return b

        def presum_begin():
            presum["on"] = True
            presum["pend"] = []
            presum["cnt"] = [0, 0]

        def presum_add(dc, t):
            k = rr("sq", 3)
            tok = slice(t * 512, (t + 1) * 512)
            S.op("dve", lambda e: e.tensor_tensor(out=sqb[:, k, :], in0=xT[:, dc, tok], in1=xT[:, dc, tok], op=ALU.mult), reads=["x%d" % t], writes=["sq%d" % k])
            first, last = presum["cnt"][t] == 0, presum["cnt"][t] == 7
            presum["cnt"][t] += 1

            def fn():
                S.op("pe", lambda e: mm(e, psum[:, 6 + t, :], ones_b[:], sqb[:, k, :], first, last), reads=["sq%d" % k, "c:ones"], writes=[PSn(6 + t)])
            presum["pend"].append(fn)
            presum_flush(2)

        def presum_flush(keep=0):
            while len(presum["pend"]) > keep:
                presum["pend"].pop(0)()

        def rstd_for_tile(t):
            tok = slice(t * 512, (t + 1) * 512)
            if presum["on"]:
                presum_flush(0)
                assert presum["cnt"] == [8, 8], presum["cnt"]
                r = rr("rstd", 2)
                S.op("act", lambda e: e.activation(out=rstd[:, r, :], in_=psum[:, 6 + t, :], func=AF.Ln, scale=1.0 / D, bias=epsc[:, 0:1]),
                     reads=[PSn(6 + t), "c:eps"], writes=["rstd%d" % r])
                S.op("act", lambda e: e.activation(out=rstd[:, r, :], in_=rstd[:, r, :], func=AF.Exp, scale=-0.5), reads=["rstd%d" % r], writes=["rstd%d" % r])
                if t == 1:
                    presum["on"] = False
                return r
            return sumsq_rstd([(xT[:, c, tok], ["x%d" % t]) for c in range(8)], D, "x")

        def PS(b):
            return psum[:, b, :]

        def PSn(b):
            return "ps%d" % b

        ring = {}

        def rr(name, n):
            ring[name] = (ring.get(name, -1) + 1) % n
            return ring[name]

        def mm(e, out, lhsT, rhs, start, stop):
            return e.matmul(out, lhsT=lhsT, rhs=rhs, start=start, stop=stop)

        jobs = []

        def job(compute, load=None):
            jobs.append((load, compute))

        def run_jobs():
            wj = [i for i, j in enumerate(jobs) if j[0] is not None]
            slot_of = {i: n % NSLOT for n, i in enumerate(wj)}
            loaded = 0
            seen = 0
            for i, (load, compute) in enumerate(jobs):
                if load is not None:
                    seen += 1
                while loaded < len(wj) and loaded < max(seen, 1) + NSLOT - 1:
                    j = wj[loaded]
                    jobs[j][0](slot_of[j])
                    loaded += 1
                compute(slot_of.get(i))

        def wdma(slot, dst, src, part=0):
            S.dma("pool", dst, src, writes=["w%d_%d" % (slot, part)])

        def wres(slot):
            return ["w%d_0" % slot, "w%d_1" % slot]

        def prologue(_):
            S.dma("sp", smallv[:], smallv_d[:, :], writes=["c:smallv"])
            S.dma("sp", gkvb[:], gkvb_d[:, :], writes=["c:gkvb"])
            S.dma("sp", cst[:], cst_d[:, :], writes=["c:cst"])
            S.op("dve", lambda e: e.tensor_copy(out=ident_b[:], in_=ident_f), reads=["c:cst"], writes=["c:identb"])
            S.op("dve", lambda e: e.memset(ones_b[:], 1.0), writes=["c:ones"])
            S.op("dve", lambda e: e.memset(bones_b[:], 0.0), writes=["c:bones"])
            S.op("dve", lambda e: e.memset(bones_b[0:64, 0:64], 1.0), writes=["c:bones"])
            S.op("dve", lambda e: e.memset(bones_b[64:128, 64:128], 1.0), writes=["c:bones"])
            S.op("dve", lambda e: e.memset(epsc[:], EPS), writes=["c:eps"])
            S.op("act", lambda e: e.activation(out=csil[:].rearrange("p a b -> p (a b)"), in_=smallv[:, O_C:O_C + 16], func=AF.Silu),
                 reads=["c:smallv"], writes=["c:csil"])

        job(prologue)

        def ada_layer(l):
            bank = [None]
            for blk in range(18):
                def load(slot, blk=blk):
                    dst = wsl[:, slot, :].rearrange("p (k f) -> p k f", k=8)
                    src = wada_d[l].rearrange("(k p) f -> p k f", p=128)[:, :, blk * 512:(blk + 1) * 512]
                    wdma(slot, dst, src)

                def comp(slot, blk=blk):
                    b = nb()
                    w = wsl[:, slot, :].rearrange("p (k f) -> p k f", k=8)

                    def f(e):
                        r = None
                        for fc in range(4):
                            for k in range(8):
                                r = mm(e, psum[:, b, 2 * fc:2 * fc + 2], w[:, k, fc * 128:(fc + 1) * 128], csil[:, k, :], k == 0, k == 7)
                        return r
                    S.op("pe", f, reads=wres(slot) + ["c:csil"], writes=[PSn(b)])
                    j0 = blk * 4
                    bb = smallv[:, O_BADA + l * 72 + j0:O_BADA + l * 72 + j0 + 4].unsqueeze(2).broadcast_to([128, 4, 2])
                    S.op("dve", lambda e: e.tensor_tensor(out=ada[:, l, j0:j0 + 4, :], in0=psum[:, b, 0:8].rearrange("p (j g) -> p j g", g=2), in1=bb, op=ALU.add),
                         reads=[PSn(b), "c:smallv"], writes=["c:ada"])
                if l == 0 and blk < 6:
                    job(comp, load)
                else:
                    ada_pending.append((comp, load))

        ada_pending = []
        for l in range(depth):
            ada_layer(l)

        def make_coef(g, l, n):
            def f(_):
                a = ada[:, l, :, g]
                on, i_sc, i_sh, i_g, gmul = ((O_N1, 8, 0, 16, 0.5), (O_NM, 32, 24, 40, 1.0), (O_N2, 56, 48, 64, 0.5))[n]
                gn = smallv[:, on + l * 8:on + l * 8 + 8]
                S.op("dve", lambda e: e.scalar_tensor_tensor(out=coef[:, 3 * n, :], in0=a[:, i_sc:i_sc + 8], scalar=1.0, in1=gn, op0=ALU.add, op1=ALU.mult),
                     reads=["c:ada", "c:smallv"], writes=["coef"])
                S.op("dve", lambda e: e.tensor_copy(out=coef[:, 3 * n + 1, :], in_=a[:, i_sh:i_sh + 8]), reads=["c:ada"], writes=["coef"])
                S.op("dve", lambda e: e.tensor_scalar(out=coef[:, 3 * n + 2, :], in0=a[:, i_g:i_g + 8], scalar1=gmul, scalar2=None, op0=ALU.mult),
                     reads=["c:ada"], writes=["coef"])
            job(f)

        def sumsq_rstd(srcs, nfeat, tname):
            n = srcs[0][0].shape[-1]
            b = nb()
            for i, (ap, rd) in enumerate(srcs):
                k = rr("sq", 3)
                S.op("dve", lambda e: e.tensor_tensor(out=sqb[:, k, 0:n], in0=ap, in1=ap, op=ALU.mult), reads=rd, writes=["sq%d" % k])
                S.op("pe", lambda e: mm(e, psum[:, b, 0:n], ones_b[:], sqb[:, k, 0:n], i == 0, i == len(srcs) - 1),
                     reads=["sq%d" % k, "c:ones"], writes=[PSn(b)])
            r = rr("rstd", 2)
            S.op("act", lambda e: e.activation(out=rstd[:, r, 0:n], in_=psum[:, b, 0:n], func=AF.Ln, scale=1.0 / nfeat, bias=epsc[:, 0:1]),
                 reads=[PSn(b), "c:eps"], writes=["rstd%d" % r])
            S.op("act", lambda e: e.activation(out=rstd[:, r, 0:n], in_=rstd[:, r, 0:n], func=AF.Exp, scale=-0.5), reads=["rstd%d" % r], writes=["rstd%d" % r])
            return r

        def norm_mod(ia, ib):
            def f(_):
                for t in range(2):
                    tok = slice(t * 512, (t + 1) * 512)
                    r = rstd_for_tile(t)
                    for c in range(8):
                        k = rr("tmpa", 2)
                        S.op("dve", lambda e: e.tensor_tensor(out=tmpa[:, k, :], in0=xT[:, c, tok], in1=rstd[:, r, :], op=ALU.mult),
                             reads=["x%d" % t, "rstd%d" % r], writes=["tmpa%d" % k])
                        S.op("act", lambda e: e.activation(out=hT[:, c, tok], in_=tmpa[:, k, :], func=AF.Identity,
                                                           scale=coef[:, ia, c:c + 1], bias=coef[:, ib, c:c + 1]),
                             reads=["tmpa%d" % k, "coef"], writes=["h%d" % t])
            job(f)

        def barrier_job():
            job(lambda _: S.barrier())

        ada_take = [0]

        def ffn(l, which, ia, ib, ig):
            import os
            ada_take[0] = 12
            parts = os.environ.get("FFN_PARTS", "ngd")
            norm_mod(ia, ib)
            wg, wu, wd = wg_d[which][l], wu_d[which][l], wd_d[which][l]
            for fb in range(11 if "g" in parts else 0):
                def load(slot, fb=fb):
                    dst = wsl[:, slot, :].rearrange("p (k m f) -> p k m f", k=8, m=2)
                    wdma(slot, dst[:, :, 0, :], wg.rearrange("(k p) f -> p k f", p=128)[:, :, fb * 256:(fb + 1) * 256])
                    wdma(slot, dst[:, :, 1, :], wu.rearrange("(k p) f -> p k f", p=128)[:, :, fb * 256:(fb + 1) * 256], part=1)

                def comp(slot, fb=fb):
                    w = wsl[:, slot, :].rearrange("p (k m f) -> p k m f", k=8, m=2)
                    order = [(j, t) for t in range(2) for j in range(2)] if fb == 0 else [(j, t) for j in range(2) for t in range(2)]
                    for (j, t) in order:
                        fc = 2 * fb + j
                        bg, bu = nb(), nb()

                        def f(e):
                            r = None
                            for k in range(8):
                                for m, bk in ((0, bg), (1, bu)):
                                    r = mm(e, PS(bk), w[:, k, m, j * 128:(j + 1) * 128], hT[:, k, t * 512:(t + 1) * 512], k == 0, k == 7)
                            return r
                        S.op("pe", f, reads=wres(slot) + ["h%d" % t], writes=[PSn(bg), PSn(bu)])
                        k = rr("tmpb", 2)
                        S.op("act", lambda e: e.activation(out=tmpb[:, k, :], in_=PS(bg), func=AF.Silu), reads=[PSn(bg)], writes=["tmpb%d" % k])
                        S.op("dve", lambda e: e.tensor_tensor(out=actT[:, fc, t * 512:(t + 1) * 512], in0=PS(bu), in1=tmpb[:, k, :], op=ALU.mult),
                             reads=[PSn(bu), "tmpb%d" % k], writes=["act%d" % t])
                job(comp, load)
                if ada_pending and ada_take[0] > 0:
                    ada_take[0] -= 1
                    c_, l_ = ada_pending.pop(0)
                    job(c_, l_)
            if "d" in parts:
                job(lambda _: presum_begin())
            for dc in range(8 if "d" in parts else 0):
                def load(slot, dc=dc):
                    dst = wsl[:, slot, 0:22 * 128].rearrange("p (k f) -> p k f", k=22)
                    srcw = wd.rearrange("(k p) d -> p k d", p=128)[:, :, dc * 128:(dc + 1) * 128]
                    for i_, (ka, kb) in enumerate(((0, 8), (8, 16), (16, 22))):
                        wdma(slot, dst[:, ka:kb, :], srcw[:, ka:kb, :], part=i_ % 2)

                def comp(slot, dc=dc):
                    w = wsl[:, slot, 0:22 * 128].rearrange("p (k f) -> p k f", k=22)
                    for t in range(2):
                        bk = nb()

                        def f(e):
                            r = None
                            for k in range(22):
                                r = mm(e, PS(bk), w[:, k, :], actT[:, k, t * 512:(t + 1) * 512], k == 0, k == 21)
                            return r
                        S.op("pe", f, reads=wres(slot) + ["act%d" % t], writes=[PSn(bk)])
                        xs_ = xT[:, dc, t * 512:(t + 1) * 512]
                        S.op("dve", lambda e: e.scalar_tensor_tensor(out=xs_, in0=PS(bk), scalar=coef[:, ig, dc:dc + 1], in1=xs_, op0=ALU.mult, op1=ALU.add),
                             reads=[PSn(bk), "coef", "x%d" % t], writes=["x%d" % t])
                        presum_add(dc, t)
                job(comp, load)
                if ada_pending and ada_take[0] > 0:
                    ada_take[0] -= 1
                    c_, l_ = ada_pending.pop(0)
                    job(c_, l_)

        def load_x(g):
            src = xp_d if g == 0 else xs_d

            def f(_):
                presum["on"] = False
                for tt in range(2):
                    for q4 in range(4):
                        ch = tt * 4 + q4
                        k = rr("xst", 2)
                        S.dma("sp", xstage[:, k, :], src[ch * 128:(ch + 1) * 128, :], writes=["xst%d" % k])
                        for c0 in range(0, 8, 4):
                            b = nb()

                            def f2(e):
                                r = None
                                for c in range(c0, c0 + 4):
                                    r = e.transpose(psum[:, b, (c - c0) * 128:(c - c0 + 1) * 128], xstage[:, k, c * 128:(c + 1) * 128], ident_f)
                                return r
                            S.op("pe", f2, reads=["xst%d" % k, "c:cst"], writes=[PSn(b)])
                            S.op("act" if c0 == 0 else "dve",
                                 lambda e: (e.activation(out=xT[:, c0:c0 + 4, ch * 128:(ch + 1) * 128], in_=psum[:, b, :].rearrange("p (c t) -> p c t", c=4), func=AF.Identity)
                                            if c0 == 0 else e.tensor_copy(out=xT[:, c0:c0 + 4, ch * 128:(ch + 1) * 128], in_=psum[:, b, :].rearrange("p (c t) -> p c t", c=4))),
                                 reads=[PSn(b)], writes=["x%d" % tt])
            job(f)

        def store_y(g):
            dst = yp_d if g == 0 else ys_d

            def f(_):
                fn = smallv[:, O_FN:O_FN + 8]
                for t in range(2):
                    tok = slice(t * 512, (t + 1) * 512)
                    r = rstd_for_tile(t)
                    for c in range(8):
                        S.op("dve", lambda e: e.scalar_tensor_tensor(out=xT[:, c, tok], in0=xT[:, c, tok], scalar=fn[:, c:c + 1], in1=rstd[:, r, :], op0=ALU.mult, op1=ALU.mult),
                             reads=["x%d" % t, "rstd%d" % r, "c:smallv"], writes=["x%d" % t])
                    for q4 in range(4):
                        ch = t * 4 + q4
                        k = rr("xst", 2)
                        for c0 in range(0, 8, 4):
                            b = nb()

                            def f2(e):
                                r2 = None
                                for c in range(c0, c0 + 4):
                                    r2 = e.transpose(psum[:, b, (c - c0) * 128:(c - c0 + 1) * 128], xT[:, c, ch * 128:(ch + 1) * 128], ident_f)
                                return r2
                            S.op("pe", f2, reads=["x%d" % t, "c:cst"], writes=[PSn(b)])
                            S.op("act" if c0 == 0 else "dve",
                                 lambda e: (e.activation(out=xstage[:, k, c0 * 128:(c0 + 4) * 128], in_=psum[:, b, :], func=AF.Identity)
                                            if c0 == 0 else e.tensor_copy(out=xstage[:, k, c0 * 128:(c0 + 4) * 128], in_=psum[:, b, :])),
                                 reads=[PSn(b)], writes=["xst%d" % k])
                        S.dma("sp", dst[ch * 128:(ch + 1) * 128, :], xstage[:, k, :], reads=["xst%d" % k])
            job(f)

        pending_subln = []

        def mixer(g, l):
            nkeys = 1024 if g == 0 else 1280
            koff = 0 if g == 0 else 256
            nvch = 8 if g == 0 else 10
            lam_init = 0.8 - 0.6 * math.exp(-0.3 * l)
            norm_mod(3, 4)

            def small(_):
                S.dma("pool", wuq_s[:], wuq_d[l].rearrange("(k p) f -> p k f", p=128), writes=["wuq"])
                wk = wukv_d[l].rearrange("k (h t d) -> k h t d", h=4, t=2)
                S.dma("pool", wukv_s[:, 0:256].rearrange("p (h d) -> p h d", h=4), wk[:, :, 0, :], writes=["wukv"])
                S.dma("pool", wukv_s[:, 256:512].rearrange("p (h d) -> p h d", h=4), wk[:, :, 1, :], writes=["wukv"])
                lv = smallv[:, O_LAM + l * 128:O_LAM + (l + 1) * 128]
                S.op("dve", lambda e: e.tensor_tensor(out=tmpa[:, 0, 0:32], in0=lv[:, 0:32], in1=lv[:, 32:64], op=ALU.mult), reads=["c:smallv"], writes=["tmpa0"])
                S.op("dve", lambda e: e.tensor_tensor(out=tmpa[:, 0, 32:64], in0=lv[:, 64:96], in1=lv[:, 96:128], op=ALU.mult), reads=["c:smallv"], writes=["tmpa0"])
                S.op("dve", lambda e: e.tensor_reduce(out=lamt[:, 0:2], in_=tmpa[:, 0, 0:64].rearrange("p (a b) -> p a b", a=2), axis=mybir.AxisListType.X, op=ALU.add),
                     reads=["tmpa0"], writes=["lamt"])
                S.op("act", lambda e: e.activation(out=lamt[:, 2:4], in_=lamt[:, 0:2], func=AF.Exp), reads=["lamt"], writes=["lamt"])
                S.op("dve", lambda e: e.scalar_tensor_tensor(out=lamt[:, 4:5], in0=lamt[:, 3:4], scalar=-lam_init, in1=lamt[:, 2:3], op0=ALU.add, op1=ALU.subtract),
                     reads=["lamt"], writes=["lamt"])
                S.op("dve", lambda e: e.tensor_scalar(out=lamt[:, 5:6], in0=smallv[:, O_SL + l:O_SL + l + 1], scalar1=1.0 - lam_init, scalar2=None, op0=ALU.mult),
                     reads=["c:smallv"], writes=["lamt"])
                vv = Vb[:, :, :].rearrange("p c (q s) -> p c q s", s=192)
                S.op("dve", lambda e: e.memset(vv[:, :, :, 64:128], 1.0), writes=["V"])
                if g == 1:
                    S.op("dve", lambda e: e.tensor_copy(out=wuqr_s[:], in_=wuq_s[:]), reads=["wuq"], writes=["wuqr"])
                    src = wuq_s[:].rearrange("p k (h c) -> p k h c", h=4)[:, :, :, 64:96].rearrange("p k h (q two e) -> p k h q two e", two=2, e=8)
                    dstv = wuqr_s[:].rearrange("p k (h c) -> p k h c", h=4)[:, :, :, 64:96].rearrange("p k h (q two e) -> p k h q two e", two=2, e=8)
                    for kk in range(2):
                        S.op("dve", lambda e: e.tensor_scalar(out=dstv[:, kk, :, :, 0, :], in0=src[:, kk, :, :, 1, :], scalar1=-1.0, scalar2=None, op0=ALU.mult), reads=["wuq"], writes=["wuqr"])
                        S.op("dve", lambda e: e.tensor_copy(out=dstv[:, kk, :, :, 1, :], in_=src[:, kk, :, :, 0, :]), reads=["wuq"], writes=["wuqr"])
            job(small)

            def wblock(c0, c1):
                def load(slot):
                    n = c1 - c0
                    dst = wsl[:, slot, 0:8 * n].rearrange("p (k f) -> p k f", k=8)
                    wdma(slot, dst, win_d[l].rearrange("(k p) f -> p k f", p=128)[:, :, c0:c1])
                return load

            def wv(slot, n):
                return wsl[:, slot, 0:8 * n].rearrange("p (k f) -> p k f", k=8)

            def proj_fm(w, col0, m, slot_res, evac, extra_reads=()):
                for t in range(2):
                    bk = nb()

                    def f(e):
                        r = None
                        for k in range(8):
                            r = mm(e, psum[0:m, bk, :], w[:, k, col0:col0 + m], hT[:, k, t * 512:(t + 1) * 512], k == 0, k == 7)
                        return r
                    S.op("pe", f, reads=list(slot_res) + ["h%d" % t] + list(extra_reads), writes=[PSn(bk)])
                    evac(t, bk)

            def proj_tm(w, col0, n, slot_res, evac):
                for ch in range(8):
                    b = nb()

                    def f(e):
                        r = None
                        for k in range(8):
                            r = mm(e, psum[:, b, 0:n], hT[:, k, ch * 128:(ch + 1) * 128], w[:, k, col0:col0 + n], k == 0, k == 7)
                        return r
                    S.op("pe", f, reads=list(slot_res) + ["h%d" % (ch // 4)], writes=[PSn(b)])
                    evac(ch, b)

            def rot_cols(dst, src, ncols, reads, writes):
                s5 = src.rearrange("p k (q two e) -> p k q two e", two=2, e=8)
                d5 = dst.rearrange("p k (q two e) -> p k q two e", two=2, e=8)
                S.op("dve", lambda e: e.tensor_scalar(out=d5[:, :, :, 0, :], in0=s5[:, :, :, 1, :], scalar1=-1.0, scalar2=None, op0=ALU.mult), reads=reads, writes=writes)
                S.op("dve", lambda e: e.tensor_copy(out=d5[:, :, :, 1, :], in_=s5[:, :, :, 0, :]), reads=reads, writes=writes)

            def rope_evac(pq, pr, p0, p1, out_ap, tok, reads, writes):
                k = rr("tmpa", 2)
                k2 = rr("tmpb", 2)
                S.op("dve", lambda e: e.tensor_tensor(out=tmpa[p0:p1, k, :], in0=psum[p0:p1, pq, :], in1=cosT[p0:p1, tok], op=ALU.mult),
                     reads=[PSn(pq), "c:cst"], writes=["tmpa%d" % k])
                S.op("dve", lambda e: e.tensor_tensor(out=tmpb[p0:p1, k2, :], in0=psum[p0:p1, pr, :], in1=sinT[p0:p1, tok], op=ALU.mult),
                     reads=[PSn(pr), "c:cst"], writes=["tmpb%d" % k2])
                S.op("dve", lambda e: e.tensor_tensor(out=out_ap, in0=tmpa[p0:p1, k, :], in1=tmpb[p0:p1, k2, :], op=ALU.add),
                     reads=["tmpa%d" % k, "tmpb%d" % k2] + list(reads), writes=writes)

            def stage_out(ch, col0, n, b, dram_fn, pre=None):
                if n == 256:
                    i_ = rr("stg256", 4)
                    k, col0 = i_ % 2, (160, 416)[i_ // 2]
                elif n == 512:
                    i_ = rr("stg512", 4)
                    k, col0 = i_ % 2, (672, 1184)[i_ // 2]
                else:
                    k = rr("stg", 2)
                if pre is None:
                    S.op("act", lambda e: e.activation(out=stg[:, k, col0:col0 + n], in_=psum[:, b, 0:n], func=AF.Identity), reads=[PSn(b)], writes=["stg%d_%d" % (k, col0)])
                else:
                    pre(k)
                s_, i0 = ch // 2, (ch % 2) * 128
                for (dst, c_a, c_b) in dram_fn(s_, i0):
                    srcv = stg[:, k, col0 + c_a:col0 + c_b]
                    if len(dst.shape) == 3:
                        srcv = srcv.rearrange("p (h d) -> p h d", d=64)
                    S.dma("sp", dst, srcv, reads=["stg%d_%d" % (k, col0)])

            def vcopy(ch, b, nh, pair0):
                vch = ch + (0 if g == 0 else 2)
                vv = Vb[:, vch, :].rearrange("p (q s) -> p q s", s=192)
                src = psum[:, b, 0:nh * 64].rearrange("p (q two d) -> p q two d", two=2, d=64)
                S.op("act", lambda e: e.activation(out=vv[:, pair0:pair0 + nh // 2, 0:64], in_=src[:, :, 0, :], func=AF.Identity), reads=[PSn(b)], writes=["V"])
                S.op("dve", lambda e: e.tensor_copy(out=vv[:, pair0:pair0 + nh // 2, 128:192], in_=src[:, :, 1, :]), reads=[PSn(b)], writes=["V"])

            def lhs_v(vch, s):
                base = (s // 2) * 192 + (0 if s % 2 == 0 else 64)
                return Vb[:, vch, base:base + 128]

            def finish_o(s_slot, ob, n, out_ap_fn, dst_writes):
                odd = s_slot % 2
                orow = slice(64, 128) if odd else slice(0, 64)
                drow = slice(0, 64) if odd else slice(64, 128)
                k = rr("tmpa", 2)
                S.op("act", lambda e: e.activation(out=tmpa[orow, k, 0:n], in_=psum[drow, ob, 0:n], func=AF.Ln), reads=[PSn(ob)], writes=["tmpa%d" % k])
                S.op("act", lambda e: e.activation(out=tmpa[orow, k, 0:n], in_=tmpa[orow, k, 0:n], func=AF.Exp, scale=-1.0), reads=["tmpa%d" % k], writes=["tmpa%d" % k])
                return orow, k

            def mla():
                def compA(slot):
                    w = wv(slot, 416)
                    sres = wres(slot)
                    bq = [[None, None], [None, None]]
                    for c in range(2):
                        bks = [nb(), nb()]

                        def f(e):
                            r = None
                            for k in range(8):
                                for t in range(2):
                                    r = mm(e, PS(bks[t]), w[:, k, c * 128:(c + 1) * 128], hT[:, k, t * 512:(t + 1) * 512], k == 0, k == 7)
                            return r
                        S.op("pe", f, reads=sres + ["h0", "h1"], writes=[PSn(b) for b in bks])
                        bq[c] = bks
                    for t in range(2):
                        for c in range(2):
                            S.op("act", lambda e: e.activation(out=tmpb[:, c, :], in_=PS(bq[c][t]), func=AF.Identity), reads=[PSn(bq[c][t])], writes=["tmpb%d" % c])
                        r = sumsq_rstd([(tmpb[:, c, :], ["tmpb%d" % c]) for c in range(2)], 256, "cq")
                        for c in range(2):
                            S.op("dve", lambda e: e.scalar_tensor_tensor(out=cqnT[:, c, t * 512:(t + 1) * 512], in0=tmpb[:, c, :], scalar=smallv[:, O_QN + l * 2 + c:O_QN + l * 2 + c + 1],
                                                                         in1=rstd[:, r, :], op0=ALU.mult, op1=ALU.mult),
                                 reads=["tmpb%d" % c, "rstd%d" % r, "c:smallv"], writes=["cqn"])
                    def ev_ckv(t, b):
                        S.op("act", lambda e: e.activation(out=tmpb[:, 0, :], in_=PS(b), func=AF.Identity), reads=[PSn(b)], writes=["tmpb0"])
                        r = sumsq_rstd([(tmpb[:, 0, :], ["tmpb0"])], 128, "ckv")
                        S.op("dve", lambda e: e.scalar_tensor_tensor(out=ckvT[:, koff + t * 512:koff + (t + 1) * 512], in0=tmpb[:, 0, :], scalar=smallv[:, O_KVN + l:O_KVN + l + 1],
                                                                     in1=rstd[:, r, :], op0=ALU.mult, op1=ALU.mult),
                             reads=["tmpb0", "rstd%d" % r, "c:smallv"], writes=["ckvT"])
                    proj_fm(w, 256, 128, sres, ev_ckv)
                    if g == 0:
                        def ev_kr(t, b):
                            for h in range(4):
                                S.op("act" if h % 2 else "dve",
                                     lambda e: (e.activation(out=Kb[64:96, h, t * 512:(t + 1) * 512], in_=psum[64:96, b, :], func=AF.Identity) if h % 2
                                                else e.tensor_copy(out=Kb[64:96, h, t * 512:(t + 1) * 512], in_=psum[64:96, b, :])),
                                     reads=[PSn(b)], writes=["K"])
                        proj_fm(w, 320, 96, sres, ev_kr)
                    else:
                        rot_cols(wrot[:, :, 0:96][:, :, 64:96], w[:, :, 384:416], 32, sres, ["wrotA"])
                        bks = [nb(), nb()]
                        brs = [nb(), nb()]

                        def f(e):
                            r = None
                            for k in range(8):
                                for t in range(2):
                                    r = mm(e, psum[0:96, bks[t], :], w[:, k, 320:416], hT[:, k, t * 512:(t + 1) * 512], k == 0, k == 7)
                                    r = mm(e, psum[0:96, brs[t], :], wrot[:, k, 0:96], hT[:, k, t * 512:(t + 1) * 512], k == 0, k == 7)
                            return r
                        S.op("pe", f, reads=sres + ["wrotA", "h0", "h1"], writes=[PSn(b) for b in bks + brs])
                        for t in range(2):
                            tok = slice(t * 512, (t + 1) * 512)
                            rope_evac(bks[t], brs[t], 64, 96, Kb[64:96, 0, 256 + t * 512:256 + (t + 1) * 512], tok, [], ["K"])
                            for h in range(1, 4):
                                S.op("act" if h % 2 else "dve",
                                     lambda e: (e.activation(out=Kb[64:96, h, 256 + t * 512:256 + (t + 1) * 512], in_=Kb[64:96, 0, 256 + t * 512:256 + (t + 1) * 512], func=AF.Identity) if h % 2
                                                else e.tensor_copy(out=Kb[64:96, h, 256 + t * 512:256 + (t + 1) * 512], in_=Kb[64:96, 0, 256 + t * 512:256 + (t + 1) * 512])),
                                     reads=["K"], writes=["K"])
                    if g == 0:
                        def ev_tm(ch, b):
                            def pre(k):
                                p = ch % 2
                                cs, cr = 6 + 2 * p, 7 + 2 * p
                                S.op("act", lambda e: e.activation(out=tmpb[:, p, 0:160], in_=psum[:, b, 0:160], func=AF.Identity), reads=[PSn(b)], writes=["tmpb%d" % p])
                                S.op("dve", lambda e: e.memset(lamt[:, cs:cs + 1], 0.0), writes=["lamt%d" % cs])
                                S.op("dve", lambda e: e.scalar_tensor_tensor(out=tmpb[:, p, 256:384], in0=tmpb[:, p, 0:128], scalar=1.0, in1=tmpb[:, p, 0:128], op0=ALU.mult, op1=ALU.mult,
                                                                             accum_out=lamt[:, cs:cs + 1]),
                                     reads=["tmpb%d" % p], writes=["tmpbj%d" % p, "lamt%d" % cs])
                                S.op("act", lambda e: e.activation(out=lamt[:, cr:cr + 1], in_=lamt[:, cs:cs + 1], func=AF.Ln, scale=1.0 / 128, bias=epsc[:, 0:1]), reads=["lamt%d" % cs, "c:eps"], writes=["lamt%d" % cr])
                                S.op("act", lambda e: e.activation(out=lamt[:, cr:cr + 1], in_=lamt[:, cr:cr + 1], func=AF.Exp, scale=-0.5), reads=["lamt%d" % cr], writes=["lamt%d" % cr])
                                S.op("dve", lambda e: e.scalar_tensor_tensor(out=stg[:, k, 0:128], in0=tmpb[:, p, 0:128], scalar=lamt[:, cr:cr + 1], in1=gkvb[:, l * 128:(l + 1) * 128],
                                                                             op0=ALU.mult, op1=ALU.mult),
                                     reads=["tmpb%d" % p, "lamt%d" % cr, "c:gkvb"], writes=["stg%d_0" % k])
                                S.op("dve", lambda e: e.tensor_copy(out=stg[:, k, 128:160], in_=tmpb[:, p, 128:160]), reads=["tmpb%d" % p], writes=["stg%d_0" % k])
                            stage_out(ch, 0, 160, b, lambda s_, i0: [(sckv_d[s_, l, i0:i0 + 128, :], 0, 128), (skr_d[s_, l, i0:i0 + 128, :], 128, 160)], pre=pre)
                        proj_tm(w, 256, 160, sres, ev_tm)
                job(compA, wblock(0, 416))

                def compM(_):
                    if g == 1:
                        S.op("dve", lambda e: e.memset(kst[:, :, 128:192], 0.0), writes=["kst"])
                        S.dma("pool", kst[:, :, 0:128], cckv_d[l].rearrange("(c p) f -> p c f", p=128), writes=["kst"], deps=S.bar_toks)
                        S.dma("pool", kst[:, :, 192:224], ckr_d[l].rearrange("(c p) f -> p c f", p=128), writes=["kst"], deps=S.bar_toks)
                        pst = psum[:, 7, :].bitcast(BF16)
                        for c in range(2):
                            S.op("pe", lambda e: (e.transpose(pst[:, c * 256:c * 256 + 128], kst[:, c, 0:128], ident_b[:]),
                                                  e.transpose(pst[0:96, c * 256 + 128:c * 256 + 256], kst[:, c, 128:224], ident_b[:]))[1],
                                 reads=["kst", "c:identb"], writes=[PSn(7)])
                        bank_i[0] = 0
                        for c in range(2):
                            S.op("dve", lambda e: e.tensor_copy(out=ckvT[:, c * 128:(c + 1) * 128], in_=pst[:, c * 256:c * 256 + 128]), reads=[PSn(7)], writes=["ckvT"])
                            for h in range(4):
                                S.op("act" if h % 2 else "dve",
                                     lambda e: (e.activation(out=Kb[64:96, h, c * 128:(c + 1) * 128], in_=pst[64:96, c * 256 + 128:c * 256 + 256], func=AF.Identity) if h % 2
                                                else e.tensor_copy(out=Kb[64:96, h, c * 128:(c + 1) * 128], in_=pst[64:96, c * 256 + 128:c * 256 + 256])),
                                     reads=[PSn(7)], writes=["K"])
                    for h in range(4):
                        bks = [nb(), nb()]
                        brs = [nb(), nb()] if g == 1 else None

                        def f(e):
                            r = None
                            for k in range(2):
                                for t in range(2):
                                    r = mm(e, psum[0:96, bks[t], :], wuq_s[:, k, h * 96:(h + 1) * 96], cqnT[:, k, t * 512:(t + 1) * 512], k == 0, k == 1)
                                    if g == 1:
                                        r = mm(e, psum[0:96, brs[t], :], wuqr_s[:, k, h * 96:(h + 1) * 96], cqnT[:, k, t * 512:(t + 1) * 512], k == 0, k == 1)
                            return r
                        S.op("pe", f, reads=["wuq", "wuqr", "cqn"], writes=[PSn(b) for b in bks + (brs or [])])
                        for t in range(2):
                            tok = slice(t * 512, (t + 1) * 512)
                            if g == 0:
                                S.op("act", lambda e: e.activation(out=Qb[0:96, h, tok], in_=psum[0:96, bks[t], :], func=AF.Identity), reads=[PSn(bks[t])], writes=["Q"])
                            else:
                                S.op("act", lambda e: e.activation(out=Qb[0:64, h, tok], in_=psum[0:64, bks[t], :], func=AF.Identity), reads=[PSn(bks[t])], writes=["Q"])
                                rope_evac(bks[t], brs[t], 64, 96, Qb[64:96, h, tok], tok, [], ["Q"])
                    nk_t = nkeys // 512 if g == 0 else None
                    kslices = [(i * 512, 512) for i in range(2)] if g == 0 else [(0, 512), (512, 512), (1024, 256)]
                    for h in range(4):
                        for (k0, kn) in kslices:
                            b = nb()
                            S.op("pe", lambda e: mm(e, psum[0:64, b, 0:kn], wukv_s[:, h * 64:(h + 1) * 64], ckvT[:, k0:k0 + kn], True, True),
                                 reads=["wukv", "ckvT"], writes=[PSn(b)])
                            S.op("act" if h % 2 else "dve",
                                 lambda e: (e.activation(out=Kb[0:64, h, k0:k0 + kn], in_=psum[0:64, b, 0:kn], func=AF.Identity) if h % 2
                                            else e.tensor_copy(out=Kb[0:64, h, k0:k0 + kn], in_=psum[0:64, b, 0:kn])),
                                 reads=[PSn(b)], writes=["K"])
                    for vch in range(nvch):
                        b = nb()
                        S.op("pe", lambda e: mm(e, psum[:, b, 0:256], ckvT[:, vch * 128:(vch + 1) * 128], wukv_s[:, 256:512], True, True),
                             reads=["wukv", "ckvT"], writes=[PSn(b)])
                        vv = Vb[:, vch, :].rearrange("p (q s) -> p q s", s=192)
                        src = psum[:, b, 0:256].rearrange("p (q two d) -> p q two d", two=2, d=64)
                        S.op("act", lambda e: e.activation(out=vv[:, 0:2, 0:64], in_=src[:, :, 0, :], func=AF.Identity), reads=[PSn(b)], writes=["V"])
                        S.op("dve", lambda e: e.tensor_copy(out=vv[:, 0:2, 128:192], in_=src[:, :, 1, :]), reads=[PSn(b)], writes=["V"])
                    steps = []
                    for h in range(4):
                        steps += dense_steps(lambda ks, h=h: Kb[0:96, h, ks], lambda qs, h=h: Qb[0:96, h, qs], h, MLA_SCALE, std_fin(h))
                    run_attn(steps)
                job(compM)

            REG = (2, 4, 6)

            def region(rb):
                return psum[:, rb:rb + 2, :].rearrange("p a b -> p (a b)")

            def std_fin(oslot):
                def fin(ob, q0, qn):
                    orow, k = finish_o(oslot, ob, qn, None, None)
                    S.op("dve", lambda e: e.tensor_tensor(out=oT[orow, oslot // 2, q0:q0 + qn], in0=psum[orow, ob, 0:qn], in1=tmpa[orow, k, 0:qn], op=ALU.mult),
                         reads=[PSn(ob), "tmpa%d" % k], writes=["oT"])
                return fin

            def dense_steps(KT, QT, vslot, scale, fin, only_units=None):
                steps = []
                if g == 0:
                    return dense_steps_prompt(KT, QT, vslot, scale, fin, only_units)
                else:
                    units = [(qb * 512, 512, [[(c * 128, c) for c in (2 * i, 2 * i + 1)] for i in range(5)]) for qb in range(2)]
                for ui, (q0, qn, groups_) in enumerate(units):
                    if only_units is not None and ui not in only_units:
                        continue
                    for gi, kcs in enumerate(groups_):
                        first, last = gi == 0, gi == len(groups_) - 1

                        def S_(rb, kcs=kcs, q0=q0, qn=qn):
                            reg = region(rb)

                            def f(e):
                                r = None
                                for jj, (k0, vch) in enumerate(kcs):
                                    r = mm(e, reg[:, jj * qn:(jj + 1) * qn], KT(slice(k0, k0 + 128)), QT(slice(q0, q0 + qn)), True, True)
                                return r
                            S.op("pe", f, reads=["K", "Q"], writes=[PSn(rb), PSn(rb + 1)])

                        def E_(rb, k, kcs=kcs, qn=qn):
                            reg = region(rb)
                            S.op("act", lambda e: e.activation(out=Pb[:, k, 0:len(kcs) * qn], in_=reg[:, 0:len(kcs) * qn], func=AF.Exp, scale=scale),
                                 reads=[PSn(rb), PSn(rb + 1)], writes=["P%d" % k])

                        def PV_(ob, k, kcs=kcs, qn=qn, first=first, last=last):
                            def f2(e):
                                r = None
                                for jj, (k0, vch) in enumerate(kcs):
                                    r = mm(e, psum[:, ob, 0:qn], lhs_v(vch, vslot), Pb[:, k, jj * qn:(jj + 1) * qn], first and jj == 0, last and jj == len(kcs) - 1)
                                return r
                            S.op("pe", f2, reads=["V", "P%d" % k], writes=[PSn(ob)])
                        steps.append({"S": S_, "E": E_, "PV": PV_, "first": first, "last": last, "fin": (lambda ob, q0=q0, qn=qn: fin(ob, q0, qn))})
                return steps

            def dense_steps_prompt(KT, QT, vslot, scale, fin, only_units=None):
                steps = []
                for ui in range(2):
                    if only_units is not None and ui not in only_units:
                        continue
                    q0 = ui * 512

                    def S_(rb, ui=ui):
                        reg = region(rb)

                        def f(e):
                            r = None
                            for sq_ in range(2):
                                s_ = ui * 2 + sq_
                                for c in range(2):
                                    r = mm(e, reg[:, sq_ * 512 + c * 256:sq_ * 512 + (c + 1) * 256], KT(slice(s_ * 256 + c * 128, s_ * 256 + (c + 1) * 128)),
                                           QT(slice(s_ * 256, (s_ + 1) * 256)), True, True)
                            return r
                        S.op("pe", f, reads=["K", "Q"], writes=[PSn(rb), PSn(rb + 1)])

                    def E_(rb, k):
                        reg = region(rb)
                        S.op("act", lambda e: e.activation(out=Pb[:, k, :], in_=reg[:, :], func=AF.Exp, scale=scale), reads=[PSn(rb), PSn(rb + 1)], writes=["P%d" % k])

                    def PV_(ob, k, ui=ui):
                        def f2(e):
                            r = None
                            for sq_ in range(2):
                                s_ = ui * 2 + sq_
                                for c in range(2):
                                    r = mm(e, psum[:, ob, sq_ * 256:(sq_ + 1) * 256], lhs_v(s_ * 2 + c, vslot), Pb[:, k, sq_ * 512 + c * 256:sq_ * 512 + (c + 1) * 256], c == 0, c == 1)
                            return r
                        S.op("pe", f2, reads=["V", "P%d" % k], writes=[PSn(ob)])
                    steps.append({"S": S_, "E": E_, "PV": PV_, "first": True, "last": True, "fin": (lambda ob, q0=q0: fin(ob, q0, 512))})
                return steps

            def run_attn(steps, LA=2):
                n = len(steps)
                reg_of, p_of = {}, {}
                st_ = {"ri": 0, "oi": 0, "ob": None}

                def front(i):
                    rb = REG[st_["ri"] % 3]
                    st_["ri"] += 1
                    k = rr("P", 3)
                    reg_of[i], p_of[i] = rb, k
                    steps[i]["S"](rb)
                    steps[i]["E"](rb, k)
                for i in range(min(LA, n)):
                    front(i)
                for i in range(n):
                    if i + LA < n:
                        front(i + LA)
                    stp = steps[i]
                    if stp["first"]:
                        st_["ob"] = st_["oi"] % 2
                        st_["oi"] += 1
                    stp["PV"](st_["ob"], p_of[i])
                    if stp["last"]:
                        stp["fin"](st_["ob"])

            def diff():
                def compB(slot):
                    w = wv(slot, 512)
                    sres = wres(slot)
                    if g == 1:
                        rot_cols(wrot[:, :, 96:608], w[:, :, 0:512], 512, sres, ["wrotB"])
                    for qk in range(2):
                        for h in range(4):
                            col0 = qk * 256 + h * 64
                            if g == 0:
                                def ev(t, b, qk=qk, h=h):
                                    dst = (Qb if qk == 0 else Kb)[0:64, h, t * 512:(t + 1) * 512]
                                    S.op("act" if h % 2 else "dve",
                                         lambda e: (e.activation(out=dst, in_=psum[0:64, b, :], func=AF.Identity) if h % 2 else e.tensor_copy(out=dst, in_=psum[0:64, b, :])),
                                         reads=[PSn(b)], writes=["Q" if qk == 0 else "K"])
                                proj_fm(w, col0, 64, sres, ev)
                            else:
                                bks = [nb(), nb()]
                                brs = [nb(), nb()]

                                def f(e):
                                    r = None
                                    for k in range(8):
                                        for t in range(2):
                                            r = mm(e, psum[0:64, bks[t], :], w[:, k, col0:col0 + 64], hT[:, k, t * 512:(t + 1) * 512], k == 0, k == 7)
                                            r = mm(e, psum[0:64, brs[t], :], wrot[:, k, 96 + col0:96 + col0 + 64], hT[:, k, t * 512:(t + 1) * 512], k == 0, k == 7)
                                    return r
                                S.op("pe", f, reads=sres + ["wrotB", "h0", "h1"], writes=[PSn(b) for b in bks + brs])
                                for t in range(2):
                                    tok = slice(t * 512, (t + 1) * 512)
                                    dst = Qb[0:64, h, tok] if qk == 0 else Kb[0:64, h, 256 + t * 512:256 + (t + 1) * 512]
                                    rope_evac(bks[t], brs[t], 0, 64, dst, tok, [], ["Q" if qk == 0 else "K"])
                    if g == 0:
                        def ev_tm(ch, b):
                            stage_out(ch, 160, 256, b, lambda s_, i0: [(sdk_d[s_, l, :, i0:i0 + 128, :].rearrange("h t d -> t h d"), 0, 256)])
                        proj_tm(w, 256, 256, sres, ev_tm)
                job(compB, wblock(416, 928))

                def compC(slot):
                    w = wv(slot, 256)
                    sres = wres(slot)

                    def ev(ch, b):
                        vcopy(ch, b, 4, 0)
                        if g == 0:
                            stage_out(ch, 416, 256, b, lambda s_, i0: [(sdv_d[s_, l, :, i0:i0 + 128, :].rearrange("h t d -> t h d"), 0, 256)])
                    proj_tm(w, 0, 256, sres, ev)
                    if g == 1:
                        for c in range(2):
                            S.dma("pool", kst[:, c, 0:256].rearrange("p (h d) -> p h d", h=4), cdk_d[l][:, c * 128:(c + 1) * 128, :].rearrange("h p d -> p h d"), writes=["kst"], deps=S.bar_toks)
                        for c in range(2):
                            vv = Vb[:, c, :].rearrange("p (q s) -> p q s", s=192)
                            srcv = cdv_d[l][:, c * 128:(c + 1) * 128, :].rearrange("(q two) p d -> p q two d", two=2)
                            S.dma("pool", vv[:, 0:2, 0:64], srcv[:, :, 0, :], writes=["V"], deps=S.bar_toks)
                            S.dma("pool", vv[:, 0:2, 128:192], srcv[:, :, 1, :], writes=["V"], deps=S.bar_toks)
                        pst = psum[:, 7, :].bitcast(BF16)
                        for c in range(2):
                            S.op("pe", lambda e: [e.transpose(pst[0:64, (c * 4 + h) * 128:(c * 4 + h + 1) * 128], kst[:, c, h * 64:(h + 1) * 64], ident_b[:]) for h in range(4)][-1],
                                 reads=["kst", "c:identb"], writes=[PSn(7)])
                        bank_i[0] = 0
                        for c in range(2):
                            for h in range(4):
                                S.op("act" if h % 2 else "dve",
                                     lambda e: (e.activation(out=Kb[0:64, h, c * 128:(c + 1) * 128], in_=pst[0:64, (c * 4 + h) * 128:(c * 4 + h + 1) * 128], func=AF.Identity) if h % 2
                                                else e.tensor_copy(out=Kb[0:64, h, c * 128:(c + 1) * 128], in_=pst[0:64, (c * 4 + h) * 128:(c * 4 + h + 1) * 128])),
                                     reads=[PSn(7)], writes=["K"])
                    def subln_all():
                        for pr_ in range(2):
                            for t in range(2):
                                tok = slice(t * 512, (t + 1) * 512)
                                k = rr("sq", 3)
                                b = nb()
                                S.op("dve", lambda e: e.tensor_tensor(out=sqb[:, k, :], in0=oT[:, 2 + pr_, tok], in1=oT[:, 2 + pr_, tok], op=ALU.mult), reads=["oT"], writes=["sq%d" % k])
                                S.op("pe", lambda e: mm(e, psum[:, b, :], bones_b[:], sqb[:, k, :], True, True), reads=["sq%d" % k, "c:bones"], writes=[PSn(b)])
                                r = rr("rstd", 2)
                                S.op("act", lambda e: e.activation(out=rstd[:, r, :], in_=psum[:, b, :], func=AF.Ln, scale=1.0 / 64, bias=epsc[:, 0:1]),
                                     reads=[PSn(b), "c:eps"], writes=["rstd%d" % r])
                                S.op("act", lambda e: e.activation(out=rstd[:, r, :], in_=rstd[:, r, :], func=AF.Exp, scale=-0.5), reads=["rstd%d" % r], writes=["rstd%d" % r])
                                S.op("dve", lambda e: e.scalar_tensor_tensor(out=oT[:, 2 + pr_, tok], in0=oT[:, 2 + pr_, tok], scalar=lamt[:, 5:6], in1=rstd[:, r, :], op0=ALU.mult, op1=ALU.mult),
                                     reads=["rstd%d" % r, "lamt"], writes=["oT"])

                    steps = []
                    for pr_ in range(2):
                        for ui in range(2):
                            for hh in range(2):
                                h = pr_ * 2 + hh
                                orow = slice(64, 128) if hh else slice(0, 64)
                                res = []
                                for half in range(2):
                                    prow = slice(half * 32, half * 32 + 32)

                                    def fin(ob, q0, qn, half=half, h=h, hh=hh, res=res, orow=orow, pr_=pr_):
                                        orow_, k = finish_o(h, ob, qn, None, None)
                                        kk = rr("tmpb", 2)
                                        S.op("dve", lambda e: e.tensor_tensor(out=tmpb[orow_, kk, 0:qn], in0=psum[orow_, ob, 0:qn], in1=tmpa[orow_, k, 0:qn], op=ALU.mult),
                                             reads=[PSn(ob), "tmpa%d" % k], writes=["tmpb%d" % kk])
                                        res.append(kk)
                                        if half == 1:
                                            S.op("dve", lambda e: e.scalar_tensor_tensor(out=oT[orow, 2 + pr_, q0:q0 + qn], in0=tmpb[orow, res[1], 0:qn], scalar=lamt[orow, 4:5], in1=tmpb[orow, res[0], 0:qn],
                                                                                         op0=ALU.mult, op1=ALU.add),
                                                 reads=["tmpb%d" % res[0], "tmpb%d" % res[1], "lamt"], writes=["oT"])
                                    steps += dense_steps(lambda ks, prow=prow, h=h: Kb[prow, h, ks], lambda qs, prow=prow, h=h: Qb[prow, h, qs], h, DIFF_SCALE, fin, only_units=[ui])
                    run_attn(steps)
                    pending_subln.append(subln_all)
                job(compC, wblock(928, 1184))

            def nat():
                def compD(slot):
                    w = wv(slot, 512)
                    for c in range(4):
                        def ev(t, b, c=c):
                            S.op("act", lambda e: e.activation(out=Qb[:, c, t * 512:(t + 1) * 512], in_=PS(b), func=AF.Identity, scale=0.125), reads=[PSn(b)], writes=["Q"])
                        proj_fm(w, c * 128, 128, wres(slot), ev)
                job(compD, wblock(1184, 1696))

                def compE(slot):
                    w = wv(slot, 512)
                    for c in range(4):
                        def ev(t, b, c=c):
                            S.op("dve", lambda e: e.tensor_copy(out=Kb[:, c, koff + t * 512:koff + (t + 1) * 512], in_=PS(b)), reads=[PSn(b)], writes=["K"])
                        proj_fm(w, c * 128, 128, wres(slot), ev)
                    if g == 0:
                        def ev_tm(ch, b):
                            stage_out(ch, 672, 512, b, lambda s_, i0: [(snk_d[s_, l, :, i0:i0 + 128, :].rearrange("h t d -> t h d"), 0, 512)])
                        proj_tm(w, 0, 512, wres(slot), ev_tm)
                job(compE, wblock(1696, 2208))

                def compF(slot):
                    w = wv(slot, 512)

                    def ev(ch, b):
                        vcopy(ch, b, 8, 0)
                        if g == 0:
                            stage_out(ch, 1184, 512, b, lambda s_, i0: [(snv_d[s_, l, :, i0:i0 + 128, :].rearrange("h t d -> t h d"), 0, 512)])
                    proj_tm(w, 0, 512, wres(slot), ev)
                    if g == 0:
                        steps = []
                        for h in range(8):
                            hb = (h % 2) * 64
                            steps += dense_steps(lambda ks, hb=hb, h=h: Kb[hb:hb + 64, h // 2, ks], lambda qs, hb=hb, h=h: Qb[hb:hb + 64, h // 2, qs], h, 1.0, std_fin(8 + h))
                        run_attn(steps)
                        return
                    for c in range(2):
                        S.dma("pool", kst[:, c, 0:512].rearrange("p (h d) -> p h d", h=8), cnk_d[l][:, c * 128:(c + 1) * 128, :].rearrange("h p d -> p h d"), writes=["kst"], deps=S.bar_toks)
                    for c in range(2):
                        vv = Vb[:, c, :].rearrange("p (q s) -> p q s", s=192)
                        srcv = cnv_d[l][:, c * 128:(c + 1) * 128, :].rearrange("(q two) p d -> p q two d", two=2)
                        S.dma("pool", vv[:, 0:4, 0:64], srcv[:, :, 0, :], writes=["V"], deps=S.bar_toks)
                        S.dma("pool", vv[:, 0:4, 128:192], srcv[:, :, 1, :], writes=["V"], deps=S.bar_toks)
                    pst = psum[:, 7, :].bitcast(BF16)
                    for c in range(2):
                        S.op("pe", lambda e: [e.transpose(pst[:, (c * 4 + cc) * 128:(c * 4 + cc + 1) * 128], kst[:, c, cc * 128:(cc + 1) * 128], ident_b[:]) for cc in range(4)][-1],
                             reads=["kst", "c:identb"], writes=[PSn(7)])
                    bank_i[0] = 0
                    for c in range(2):
                        for cc in range(4):
                            S.op("act" if cc % 2 else "dve",
                                 lambda e: (e.activation(out=Kb[:, cc, c * 128:(c + 1) * 128], in_=pst[:, (c * 4 + cc) * 128:(c * 4 + cc + 1) * 128], func=AF.Identity) if cc % 2
                                            else e.tensor_copy(out=Kb[:, cc, c * 128:(c + 1) * 128], in_=pst[:, (c * 4 + cc) * 128:(c * 4 + cc + 1) * 128])),
                                 reads=[PSn(7)], writes=["K"])
                    def natb_load(hp):
                        S.dma("pool", natb_s[:, hp % 2, :], natb_d[l][:, hp * 2048:(hp + 1) * 2048], writes=["natb%d" % (hp % 2)], deps=S.bar_toks)
                    natb_load(0)
                    steps = []
                    for hp in range(4):
                        nbuf = hp % 2
                        for hh in range(2):
                            h = hp * 2 + hh
                            hb = hh * 64
                            c = hp
                            tab = natb_s[:, nbuf, hh * 1024:(hh + 1) * 1024].rearrange("p (j q) -> p j q", q=64)
                            for qb in range(2):
                                q0 = qb * 512
                                pre = (hp + 1) if (hh == 0 and qb == 0 and hp + 1 < 4) else None

                                def S0(rb, hb=hb, c=c, q0=q0, pre=pre):
                                    if pre is not None:
                                        natb_load(pre)
                                    reg = region(rb)

                                    def f(e):
                                        r = None
                                        for kc in range(2):
                                            r = mm(e, reg[:, kc * 512:(kc + 1) * 512], Kb[hb:hb + 64, c, kc * 128:(kc + 1) * 128], Qb[hb:hb + 64, c, q0:q0 + 512], True, True)
                                        return r
                                    S.op("pe", f, reads=["K", "Q"], writes=[PSn(rb), PSn(rb + 1)])

                                def E0(rb, k):
                                    reg = region(rb)
                                    S.op("act", lambda e: e.activation(out=Pb[:, k, :], in_=reg[:, :], func=AF.Exp), reads=[PSn(rb), PSn(rb + 1)], writes=["P%d" % k])

                                def PV0(ob, k, h=h):
                                    def f2(e):
                                        r = None
                                        for kc in range(2):
                                            r = mm(e, PS(ob), lhs_v(kc, h), Pb[:, k, kc * 512:(kc + 1) * 512], kc == 0, False)
                                        return r
                                    S.op("pe", f2, reads=["V", "P%d" % k], writes=[PSn(ob)])
                                steps.append({"S": S0, "E": E0, "PV": PV0, "first": True, "last": False, "fin": None})
                                for tq in range(4):
                                    t = qb * 4 + tq
                                    js, inval = nat_chunks(t)
                                    nj = len(js)

                                    def S1(rb, hb=hb, c=c, t=t, js=js, tab=tab, nbuf=nbuf):
                                        reg = region(rb)

                                        def f(e):
                                            r = None
                                            for jj, j in enumerate(js):
                                                r = mm(e, reg[:, jj * 128:(jj + 1) * 128], Kb[hb:hb + 64, c, 256 + j * 128:256 + (j + 1) * 128], Qb[hb:hb + 64, c, t * 128:(t + 1) * 128], True, False)
                                                jx0 = 8 - (2 * j - 2 * t)
                                                r = mm(e, reg[:, jj * 128:(jj + 1) * 128], ident_b[:], tab[:, jx0:jx0 + 2, :].rearrange("p j q -> p (j q)"), False, True)
                                            return r
                                        S.op("pe", f, reads=["K", "Q", "natb%d" % nbuf, "c:identb"], writes=[PSn(rb), PSn(rb + 1)])

                                    def E1(rb, k, nj=nj, inval=inval):
                                        reg = region(rb)
                                        S.op("act", lambda e: e.activation(out=Pb[:, k, 0:nj * 128], in_=reg[:, 0:nj * 128], func=AF.Exp), reads=[PSn(rb), PSn(rb + 1)], writes=["P%d" % k])
                                        for (jj, a, b_) in inval:
                                            S.op("dve", lambda e: e.memset(Pb[a * 64:(a + 1) * 64, k, jj * 128 + b_ * 64:jj * 128 + b_ * 64 + 64], 0.0), writes=["P%d" % k])

                                    def PV1(ob, k, h=h, js=js, nj=nj, tq=tq):
                                        def f2(e):
                                            r = None
                                            for jj, j in enumerate(js):
                                                r = mm(e, psum[:, ob, tq * 128:(tq + 1) * 128], lhs_v(2 + j, h), Pb[:, k, jj * 128:(jj + 1) * 128], False, jj == nj - 1)
                                            return r
                                        S.op("pe", f2, reads=["V", "P%d" % k], writes=[PSn(ob)])

                                    def fin1(ob, h=h, c=c, q0=q0):
                                        orow, k = finish_o(h, ob, 512, None, None)
                                        S.op("dve", lambda e: e.tensor_tensor(out=oT[orow, 4 + c, q0:q0 + 512], in0=psum[orow, ob, :], in1=tmpa[orow, k, :], op=ALU.mult),
                                             reads=[PSn(ob), "tmpa%d" % k], writes=["oT"])
                                    steps.append({"S": S1, "E": E1, "PV": PV1, "first": False, "last": tq == 3, "fin": fin1})
                    run_attn(steps)
                job(compF, wblock(2208, 2720))

            sel = ("mla", "diff", "nat") if mixsel == "all" else tuple(mixsel.split(","))
            if "mla" in sel:
                mla()
            if "diff" in sel:
                diff()
            if "nat" in sel:
                nat()
            if mixsel != "all":
                def z(_):
                    for c in range(8):
                        typ = "mla" if c < 2 else ("diff" if c < 4 else "nat")
                        if typ not in sel:
                            S.op("dve", lambda e: e.memset(oT[:, c, :], 0.0), writes=["oT"])
                job(z)

            job(lambda _: presum_begin())
            for dq_ in range(2):
                def load(slot, dq_=dq_):
                    dst = wsl[:, slot, :].rearrange("p (k f) -> p k f", k=8)
                    wdma(slot, dst, wout_d[l].rearrange("(k p) f -> p k f", p=128)[:, :, dq_ * 512:(dq_ + 1) * 512])

                def comp(slot, dq_=dq_):
                    w = wsl[:, slot, :].rearrange("p (k f) -> p k f", k=8)
                    while pending_subln:
                        pending_subln.pop(0)()
                    for dd in range(4):
                        dc = dq_ * 4 + dd
                        bks = [nb(), nb()]

                        def f(e):
                            r = None
                            for k in range(8):
                                for t in range(2):
                                    r = mm(e, PS(bks[t]), w[:, k, dd * 128:(dd + 1) * 128], oT[:, k, t * 512:(t + 1) * 512], k == 0, k == 7)
                            return r
                        S.op("pe", f, reads=wres(slot) + ["oT"], writes=[PSn(b) for b in bks])
                        for t in range(2):
                            xs_ = xT[:, dc, t * 512:(t + 1) * 512]
                            S.op("dve", lambda e: e.scalar_tensor_tensor(out=xs_, in0=PS(bks[t]), scalar=coef[:, 5, dc:dc + 1], in1=xs_, op0=ALU.mult, op1=ALU.add),
                                 reads=[PSn(bks[t]), "coef", "x%d" % t], writes=["x%d" % t])
                            presum_add(dc, t)
                job(comp, load)

        for g in groups:
            load_x(g)
            barrier_job()
            done = False
            for l in range(depth):
                make_coef(g, l, 0)
                if stop == (l, 0):
                    break
                ffn(l, 0, 0, 1, 2)
                barrier_job()
                if stop == (l, 1):
                    break
                make_coef(g, l, 1)
                mixer(g, l)
                barrier_job()
                if stop == (l, 2):
                    break
                make_coef(g, l, 2)
                ffn(l, 1, 6, 7, 8)
                barrier_job()
                if stop == (l, 3):
                    break
            store_y(g)
            barrier_job()
        run_jobs()
        for q in S.dq.values():
            for s_ in q["sems"]:
                if s_[1] > 0:
                    S._wait("sp", (s_[0], s_[1]))
        if stats is not None:
            stats.update({"ninst": dict(S.ninst), "nwait": dict(S.nwait), "nsem": S.nsem})
    return nc


def _consts():
    ident = np.eye(128, dtype=np.float32)
    t = np.arange(1024)
    freqs = (np.float32(10000.0) ** (-np.arange(8, dtype=np.float32) / np.float32(8))).astype(np.float32)
    cos = np.zeros((128, 1024), np.float32)
    sin = np.zeros((128, 1024), np.float32)
    for p in range(128):
        r = p % 32
        pos = (t // 64) if r < 16 else (t % 64)
        ang = pos.astype(np.float32) * freqs[r % 8]
        cos[p] = np.cos(ang).astype(np.float32)
        sin[p] = np.sin(ang).astype(np.float32)
    return np.concatenate([ident, cos, sin], axis=1)


def _natb(rpb):
    Ln = rpb.shape[0]
    ck = np.arange(64)[:, None]
    cq = np.arange(64)[None, :]
    cs = np.clip(cq - 8, 0, 48)
    inwin = (ck >= cs) & (ck < cs + 16)
    dc = np.clip(ck - cq, -15, 15) + 15
    out = np.zeros((Ln, 128, 8, 16, 64), np.float32)
    for half in range(2):
        for jx in range(16):
            dr = (15 - jx) if half == 0 else (16 - jx)
            if 0 <= dr <= 14:
                val = np.where(inwin[None, None], rpb[:, :, dr][:, :, dc], np.float32(NEG))
                out[:, half * 64:(half + 1) * 64, :, jx, :] = np.transpose(val, (0, 2, 1, 3))
    return out.reshape(Ln, 128, 8 * 16 * 64)


def _smallv(inp, core):
    b = core // 2
    sv = np.zeros((128, NSV), np.float32)

    def fm(v):
        v = np.asarray(v, np.float32)
        return np.moveaxis(v.reshape(v.shape[:-1] + (-1, 128)), -1, 0)
    sv[:, O_N1:O_N1 + 32] = fm(inp["ffn1_norm"]).reshape(128, -1)
    sv[:, O_NM:O_NM + 32] = fm(inp["mix_norm"]).reshape(128, -1)
    sv[:, O_N2:O_N2 + 32] = fm(inp["ffn2_norm"]).reshape(128, -1)
    sv[:, O_FN:O_FN + 8] = fm(inp["final_norm"]).reshape(128, -1)
    sv[:, O_BADA:O_BADA + 288] = fm(inp["b_ada"]).reshape(128, -1)
    cv = np.stack([inp["c_ctx"], inp["c"][b]], axis=0)
    sv[:, O_C:O_C + 16] = np.transpose(fm(cv), (0, 2, 1)).reshape(128, 16)
    sv[:, O_QN:O_QN + 8] = fm(inp["mla_q_norm"]).reshape(128, -1)
    sv[:, O_KVN:O_KVN + 4] = fm(inp["mla_kv_norm"]).reshape(128, -1)
    sv[:, O_SL:O_SL + 4] = np.concatenate([inp["diff_subln"].T, inp["diff_subln"].T], axis=0)
    lam = np.stack([inp["diff_lambda_q1"], inp["diff_lambda_k1"], inp["diff_lambda_q2"], inp["diff_lambda_k2"]], axis=1)
    sv[:, O_LAM:O_LAM + 512] = np.broadcast_to(lam.reshape(1, -1), (128, 512))
    return sv


def make_in_maps(inp, cores=range(NCORES)):
    f = lambda a: np.ascontiguousarray(np.asarray(a, np.float32))
    cst = _consts()
    natb = _natb(np.asarray(inp["nat_rpb"], np.float32))
    gkvb = np.ascontiguousarray(np.broadcast_to(np.asarray(inp["mla_kv_norm"], np.float32).reshape(1, -1), (128, L * 128)))
    shared = {
        "cst": cst, "natb": natb, "gkvb": gkvb,
        "w_ada": f(inp["w_ada"]), "wg1": f(inp["ffn1_w_gate"]), "wu1": f(inp["ffn1_w_up"]), "wd1": f(inp["ffn1_w_down"]),
        "wg2": f(inp["ffn2_w_gate"]), "wu2": f(inp["ffn2_w_up"]), "wd2": f(inp["ffn2_w_down"]),
        "w_in": f(inp["w_in"]), "wuq": f(inp["mla_w_uq"]), "wukv": f(inp["mla_w_ukv"]), "w_out": f(inp["w_out"]),
    }
    maps = []
    for c in cores:
        b = c // 2
        m = dict(shared)
        m["xp"] = f(inp["x_prompt"][4 * c:4 * c + 4]).reshape(TG, D)
        m["xs"] = f(inp["x_sample"][b])
        m["smallv"] = _smallv(inp, c)
        m["c_ckv"] = f(inp["cache_mla_ckv"][b]); m["c_krope"] = f(inp["cache_mla_krope"][b])
        m["c_dk"] = f(inp["cache_diff_k"][b]); m["c_dv"] = f(inp["cache_diff_v"][b])
        m["c_nk"] = f(inp["cache_nat_k"][b]); m["c_nv"] = f(inp["cache_nat_v"][b])
        maps.append(m)
    return maps


def kernel(**inputs):
    nc = build_program()
    in_maps = make_in_maps(inputs)
    res = run_bass_kernel_spmd(nc, in_maps, core_ids=list(range(NCORES)))
    r = res.results
    y_p = np.concatenate([r[c]["y_p"].reshape(4, 256, D) for c in range(NCORES)], axis=0)
    y_s = np.stack([r[2 * b]["y_s"] for b in range(4)], axis=0)
    outs = [y_p, y_s]
    for k in ("st_ckv", "st_krope", "st_dk", "st_dv", "st_nk", "st_nv"):
        outs.append(np.concatenate([r[c][k] for c in range(NCORES)], axis=0))
    return tuple(np.ascontiguousarray(o, dtype=np.float32) for o in outs)
```

```python
import contextlib
import math
import numpy as np
import concourse.bass as bass
import concourse.mybir as mybir
from concourse.bass_utils import run_bass_kernel_spmd

F32 = mybir.dt.float32
BF16 = mybir.dt.bfloat16
AF = mybir.ActivationFunctionType
ALU = mybir.AluOpType

D = 1024
FF = 2816
L = 4
NCORES = 8
TG = 1024
EPS = 1e-6
MLA_SCALE = 96 ** -0.5
DIFF_SCALE = 32 ** -0.5
NSLOT = 4
SLOT_EL = 4096
NEG = -1e30

O_N1, O_NM, O_N2, O_FN, O_BADA, O_C, O_QN, O_KVN, O_SL, O_LAM = 0, 32, 64, 96, 104, 392, 408, 416, 420, 424
NSV = 424 + 512


class Sched:
    COMPUTE = ("pe", "act", "dve", "pool")

    def __init__(self, nc, stack, ndma_sems=8):
        self.nc = nc
        self.stack = stack
        self.eng = {"pe": nc.tensor, "act": nc.scalar, "dve": nc.vector, "pool": nc.gpsimd, "sp": nc.sync}
        self.nsem = 0
        self.csem = {}
        self.pe_sems = set()
        for e in self.COMPUTE:
            self.csem[e] = [self._newsem(e), 0]
        self.pe_sems.add(id(self.csem["pe"][0]))
        self.dq = {}
        for q, e in (("sp", "sp"), ("pool", "pool")):
            self.dq[q] = {"eng": e, "sems": [[self._newsem("d" + q), 0] for _ in range(ndma_sems)], "i": 0}
        self.waited = {e: {} for e in self.eng}
        self.last_w = {}
        self.readers = {}
        self.ninst = {e: 0 for e in self.eng}
        self.nwait = {e: 0 for e in self.eng}
        self.bar_toks = []

    def _newsem(self, tag):
        self.nsem += 1
        return self.stack.enter_context(self.nc.semaphore("s%s%d" % (tag, self.nsem)))

    def _wait(self, e, tok):
        sem, val = tok
        w = self.waited[e]
        k = id(sem)
        if w.get(k, 0) >= val:
            return
        w[k] = val
        self.eng[e].wait_ge(sem, val)
        self.nwait[e] += 1

    def _deps(self, reads, writes, deps):
        d = {}

        def add(t):
            k = id(t[0])
            if k not in d or d[k][1] < t[1]:
                d[k] = t
        for t in deps:
            add(t)
        for r in reads:
            t = self.last_w.get(r)
            if t is not None:
                add(t)
        for w in writes:
            t = self.last_w.get(w)
            if t is not None:
                add(t)
            for t in self.readers.get(w, {}).values():
                add(t)
        return list(d.values())

    def _record(self, tok, reads, writes):
        for r in reads:
            if not r.startswith("c:"):
                self.readers.setdefault(r, {})[id(tok[0])] = tok
        for w in writes:
            self.last_w[w] = tok
            self.readers[w] = {}

    def op(self, e, fn, reads=(), writes=(), deps=()):
        pr = [r for r in reads if r.startswith("ps")]
        if pr:
            reads = [r for r in reads if not r.startswith("ps")]
            writes = list(writes) + pr
        for t in self._deps(reads, writes, deps):
            if e == "pe" and id(t[0]) in self.pe_sems:
                continue
            self._wait(e, t)
        inst = fn(self.eng[e])
        cs = self.csem[e]
        if cs[1] >= 30000:
            cs[0] = self._newsem(e)
            cs[1] = 0
            if e == "pe":
                self.pe_sems.add(id(cs[0]))
        cs[1] += 1
        inst.then_inc(cs[0], 1)
        tok = (cs[0], cs[1])
        self.ninst[e] += 1
        self._record(tok, reads, writes)
        return tok

    def dma(self, q, out, in_, reads=(), writes=(), deps=(), **kw):
        dq = self.dq[q]
        e = dq["eng"]
        for t in self._deps(reads, writes, deps):
            self._wait(e, t)
        slot = dq["sems"][dq["i"] % len(dq["sems"])]
        dq["i"] += 1
        if slot[1] >= 30000:
            slot[0] = self._newsem("d" + q)
            slot[1] = 0
        if slot[1] > 0:
            self._wait(e, (slot[0], slot[1]))
        inst = self.eng[e].dma_start(out=out, in_=in_, **kw)
        slot[1] += 16
        inst.then_inc(slot[0], 16)
        tok = (slot[0], slot[1])
        self.ninst[e] += 1
        self._record(tok, reads, writes)
        return tok

    def latest(self):
        toks = []
        for e in self.COMPUTE:
            cs = self.csem[e]
            if cs[1] > 0:
                toks.append((cs[0], cs[1]))
        for q in self.dq.values():
            for s in q["sems"]:
                if s[1] > 0:
                    toks.append((s[0], s[1]))
        return toks

    def barrier(self, engines=("pe", "act", "dve", "sp")):
        toks = self.latest()
        self.bar_toks = toks
        for e in engines:
            for t in toks:
                if e == "pe" and id(t[0]) in self.pe_sems:
                    continue
                self._wait(e, t)


def nat_chunks(t):
    def rs(r):
        return min(max(r - 4, 0), 8)
    lo = rs(2 * t)
    hi = rs(2 * t + 1) + 7
    js = list(range(lo // 2, hi // 2 + 1))
    inval = []
    for jj, j in enumerate(js):
        for a in range(2):
            for b in range(2):
                rk, rq = 2 * j + a, 2 * t + b
                if not (rs(rq) <= rk < rs(rq) + 8):
                    inval.append((jj, a, b))
    return js, inval


def build_program(depth=L, stop=None, groups=(0, 1), mixsel="all", stats=None):
    nc = bass.Bass("TRN2", target_bir_lowering=False)

    def din(name, shape):
        return nc.dram_tensor(name, list(shape), F32, kind="ExternalInput").ap()

    def dout(name, shape):
        return nc.dram_tensor(name, list(shape), F32, kind="ExternalOutput").ap()

    xp_d = din("xp", [TG, D]); xs_d = din("xs", [TG, D])
    smallv_d = din("smallv", [128, NSV]); gkvb_d = din("gkvb", [128, L * 128]); cst_d = din("cst", [128, 128 + 2048])
    natb_d = din("natb", [L, 128, 8 * 1024])
    cckv_d = din("c_ckv", [L, 256, 128]); ckr_d = din("c_krope", [L, 256, 32])
    cdk_d = din("c_dk", [L, 4, 256, 64]); cdv_d = din("c_dv", [L, 4, 256, 64])
    cnk_d = din("c_nk", [L, 8, 256, 64]); cnv_d = din("c_nv", [L, 8, 256, 64])
    wada_d = din("w_ada", [L, D, 9 * D])
    wg_d = [din("wg1", [L, D, FF]), din("wg2", [L, D, FF])]
    wu_d = [din("wu1", [L, D, FF]), din("wu2", [L, D, FF])]
    wd_d = [din("wd1", [L, FF, D]), din("wd2", [L, FF, D])]
    win_d = din("w_in", [L, D, 2720]); wuq_d = din("wuq", [L, 256, 384]); wukv_d = din("wukv", [L, 128, 512])
    wout_d = din("w_out", [L, D, D])
    yp_d = dout("y_p", [TG, D]); ys_d = dout("y_s", [TG, D])
    sckv_d = dout("st_ckv", [4, L, 256, 128]); skr_d = dout("st_krope", [4, L, 256, 32])
    sdk_d = dout("st_dk", [4, L, 4, 256, 64]); sdv_d = dout("st_dv", [4, L, 4, 256, 64])
    snk_d = dout("st_nk", [4, L, 8, 256, 64]); snv_d = dout("st_nv", [4, L, 8, 256, 64])

    with contextlib.ExitStack() as st:
        S = Sched(nc, st)

        def sb(name, shape, dt=F32):
            return st.enter_context(nc.sbuf_tensor("sb_" + name, list(shape), dt))

        xT = sb("xT", [128, 8, TG])
        hT = sb("hT", [128, 8, TG], BF16)
        wsl = sb("wsl", [128, NSLOT, SLOT_EL], BF16)
        smallv = sb("smallv", [128, NSV])
        gkvb = sb("gkvb", [128, L * 128])
        cst = sb("cst", [128, 128 + 2048])
        ada = sb("ada", [128, L, 72, 2])
        coef = sb("coef", [128, 9, 8])
        csil = sb("csil", [128, 8, 2], BF16)
        ident_b = sb("ident_b", [128, 128], BF16)
        ones_b = sb("ones_b", [128, 128], BF16)
        bones_b = sb("bones_b", [128, 128], BF16)
        epsc = sb("epsc", [128, 1])
        rstd = sb("rstd", [128, 2, 512])
        tmpa = sb("tmpa", [128, 2, 512])
        tmpb = sb("tmpb", [128, 2, 512])
        sqb = sb("sqb", [128, 3, 512], BF16)
        wuq_s = sb("wuq_s", [128, 2, 384], BF16)
        wuqr_s = sb("wuqr_s", [128, 2, 384], BF16)
        wukv_s = sb("wukv_s", [128, 512], BF16)
        wrot = sb("wrot", [128, 8, 608], BF16)
        lamt = sb("lamt", [128, 12])
        arena = sb("arena", [128, 40960], BF16)
        psum = st.enter_context(nc.psum_tensor("psum", [128, 8, 512], F32))
        ident_f = cst[:, 0:128]
        cosT = cst[:, 128:128 + 1024]
        sinT = cst[:, 1152:1152 + 1024]

        def carve(off, shape, dt=BF16):
            n = 1
            for s_ in shape[1:]:
                n *= s_
            mult = 2 if dt == F32 else 1
            base = arena[:, off:off + n * mult]
            if dt == F32:
                base = base.bitcast(F32)
            names = "abcdefg"[:len(shape) - 1]
            if len(shape) > 2:
                pat = "p (" + " ".join(names) + ") -> p " + " ".join(names)
                kw = {names[i]: shape[i + 1] for i in range(len(shape) - 1)}
                base = base.rearrange(pat, **kw)
            return base[0:shape[0]], off + n * mult

        actT, _ = carve(0, [128, 22, TG])
        o = 0
        Kb, o = carve(o, [128, 4, 1280])
        Vb, o = carve(o, [128, 10, 768])
        Qb, o = carve(o, [128, 4, TG])
        oT, o = carve(o, [128, 8, TG])
        ckvT, o = carve(o, [128, 1280])
        cqnT, o = carve(o, [128, 2, TG])
        Pb, o = carve(o, [128, 3, 1024])
        odT, o = carve(o, [128, 512], F32)
        o_shared = o
        natb_s, o = carve(o, [128, 2, 2048])
        kst, o = carve(o, [128, 2, 1024])
        stg, o2 = carve(o_shared, [128, 2, 1696], F32)
        assert max(o, o2) <= 40960, (o, o2)
        xstage, _ = carve(0, [128, 4, D], F32)

        bank_i = [0]

        presum = {"on": False, "pend": [], "cnt": [0, 0]}

        def nb():
            while True:
                b = bank_i[0] % 8
                bank_i[0] += 1
                if presum["on"] and b in (6, 7):
                    continue
                return b

        def presum_begin():
            presum["on"] = True
            presum["pend"] = []
            presum["cnt"] = [0, 0]

        def presum_add(dc, t):
            k = rr("sq", 3)
            tok = slice(t * 512, (t + 1) * 512)
            S.op("dve", lambda e: e.tensor_tensor(out=sqb[:, k, :], in0=xT[:, dc, tok], in1=xT[:, dc, tok], op=ALU.mult), reads=["x%d" % t], writes=["sq%d" % k])
            first, last = presum["cnt"][t] == 0, presum["cnt"][t] == 7
            presum["cnt"][t] += 1

            def fn():
                S.op("pe", lambda e: mm(e, psum[:, 6 + t, :], ones_b[:], sqb[:, k, :], first, last), reads=["sq%d" % k, "c:ones"], writes=[PSn(6 + t)])
            presum["pend"].append(fn)
            presum_flush(2)

        def presum_flush(keep=0):
            while len(presum["pend"]) > keep:
                presum["pend"].pop(0)()

        def rstd_for_tile(t):
            tok = slice(t * 512, (t + 1) * 512)
            if presum["on"]:
                presum_flush(0)
                assert presum["cnt"] == [8, 8], presum["cnt"]
                r = rr("rstd", 2)
                S.op("act", lambda e: e.activation(out=rstd[:, r, :], in_=psum[:, 6 + t, :], func=AF.Ln, scale=1.0 / D, bias=epsc[:, 0:1]),
                     reads=[PSn(6 + t), "c:eps"], writes=["rstd%d" % r])
                S.op("act", lambda e: e.activation(out=rstd[:, r, :], in_=rstd[:, r, :], func=AF.Exp, scale=-0.5), reads=["rstd%d" % r], writes=["rstd%d" % r])
                if t == 1:
                    presum["on"] = False
                return r
            return sumsq_rstd([(xT[:, c, tok], ["x%d" % t]) for c in range(8)], D, "x")

        def PS(b):
            return psum[:, b, :]

        def PSn(b):
            return "ps%d" % b

        ring = {}

        def rr(name, n):
            ring[name] = (ring.get(name, -1) + 1) % n
            return ring[name]

        def mm(e, out, lhsT, rhs, start, stop):
            return e.matmul(out, lhsT=lhsT, rhs=rhs, start=start, stop=stop)

        jobs = []

        def job(compute, load=None):
            jobs.append((load, compute))

        def run_jobs():
            wj = [i for i, j in enumerate(jobs) if j[0] is not None]
            slot_of = {i: n % NSLOT for n, i in enumerate(wj)}
            loaded = 0
            seen = 0
            for i, (load, compute) in enumerate(jobs):
                if load is not None:
                    seen += 1
                while loaded < len(wj) and loaded < max(seen, 1) + NSLOT - 1:
                    j = wj[loaded]
                    jobs[j][0](slot_of[j])
                    loaded += 1
                compute(slot_of.get(i))

        def wdma(slot, dst, src, part=0):
            S.dma("pool", dst, src, writes=["w%d_%d" % (slot, part)])

        def wres(slot):
            return ["w%d_0" % slot, "w%d_1" % slot]

        def prologue(_):
            S.dma("sp", smallv[:], smallv_d[:, :], writes=["c:smallv"])
            S.dma("sp", gkvb[:], gkvb_d[:, :], writes=["c:gkvb"])
            S.dma("sp", cst[:], cst_d[:, :], writes=["c:cst"])
            S.op("dve", lambda e: e.tensor_copy(out=ident_b[:], in_=ident_f), reads=["c:cst"], writes=["c:identb"])
            S.op("dve", lambda e: e.memset(ones_b[:], 1.0), writes=["c:ones"])
            S.op("dve", lambda e: e.memset(bones_b[:], 0.0), writes=["c:bones"])
            S.op("dve", lambda e: e.memset(bones_b[0:64, 0:64], 1.0), writes=["c:bones"])
            S.op("dve", lambda e: e.memset(bones_b[64:128, 64:128], 1.0), writes=["c:bones"])
            S.op("dve", lambda e: e.memset(epsc[:], EPS), writes=["c:eps"])
            S.op("act", lambda e: e.activation(out=csil[:].rearrange("p a b -> p (a b)"), in_=smallv[:, O_C:O_C + 16], func=AF.Silu),
                 reads=["c:smallv"], writes=["c:csil"])

        job(prologue)

        def ada_layer(l):
            bank = [None]
            for blk in range(18):
                def load(slot, blk=blk):
                    dst = wsl[:, slot, :].rearrange("p (k f) -> p k f", k=8)
                    src = wada_d[l].rearrange("(k p) f -> p k f", p=128)[:, :, blk * 512:(blk + 1) * 512]
                    wdma(slot, dst, src)

                def comp(slot, blk=blk):
                    b = nb()
                    w = wsl[:, slot, :].rearrange("p (k f) -> p k f", k=8)

                    def f(e):
                        r = None
                        for fc in range(4):
                            for k in range(8):
                                r = mm(e, psum[:, b, 2 * fc:2 * fc + 2], w[:, k, fc * 128:(fc + 1) * 128], csil[:, k, :], k == 0, k == 7)
                        return r
                    S.op("pe", f, reads=wres(slot) + ["c:csil"], writes=[PSn(b)])
                    j0 = blk * 4
                    bb = smallv[:, O_BADA + l * 72 + j0:O_BADA + l * 72 + j0 + 4].unsqueeze(2).broadcast_to([128, 4, 2])
                    S.op("dve", lambda e: e.tensor_tensor(out=ada[:, l, j0:j0 + 4, :], in0=psum[:, b, 0:8].rearrange("p (j g) -> p j g", g=2), in1=bb, op=ALU.add),
                         reads=[PSn(b), "c:smallv"], writes=["c:ada"])
                if l == 0 and blk < 6:
                    ada_first.append((comp, load))
                else:
                    ada_pending.append((comp, load))

        ada_pending = []
        ada_first = []
        for l in range(depth):
            ada_layer(l)

        def make_coef(g, l, n):
            def f(_):
                a = ada[:, l, :, g]
                on, i_sc, i_sh, i_g, gmul = ((O_N1, 8, 0, 16, 0.5), (O_NM, 32, 24, 40, 1.0), (O_N2, 56, 48, 64, 0.5))[n]
                gn = smallv[:, on + l * 8:on + l * 8 + 8]
                S.op("dve", lambda e: e.scalar_tensor_tensor(out=coef[:, 3 * n, :], in0=a[:, i_sc:i_sc + 8], scalar=1.0, in1=gn, op0=ALU.add, op1=ALU.mult),
                     reads=["c:ada", "c:smallv"], writes=["coef"])
                S.op("dve", lambda e: e.tensor_copy(out=coef[:, 3 * n + 1, :], in_=a[:, i_sh:i_sh + 8]), reads=["c:ada"], writes=["coef"])
                S.op("dve", lambda e: e.tensor_scalar(out=coef[:, 3 * n + 2, :], in0=a[:, i_g:i_g + 8], scalar1=gmul, scalar2=None, op0=ALU.mult),
                     reads=["c:ada"], writes=["coef"])
            job(f)

        def sumsq_rstd(srcs, nfeat, tname):
            n = srcs[0][0].shape[-1]
            b = nb()
            for i, (ap, rd) in enumerate(srcs):
                k = rr("sq", 3)
                S.op("dve", lambda e: e.tensor_tensor(out=sqb[:, k, 0:n], in0=ap, in1=ap, op=ALU.mult), reads=rd, writes=["sq%d" % k])
                S.op("pe", lambda e: mm(e, psum[:, b, 0:n], ones_b[:], sqb[:, k, 0:n], i == 0, i == len(srcs) - 1),
                     reads=["sq%d" % k, "c:ones"], writes=[PSn(b)])
            r = rr("rstd", 2)
            S.op("act", lambda e: e.activation(out=rstd[:, r, 0:n], in_=psum[:, b, 0:n], func=AF.Ln, scale=1.0 / nfeat, bias=epsc[:, 0:1]),
                 reads=[PSn(b), "c:eps"], writes=["rstd%d" % r])
            S.op("act", lambda e: e.activation(out=rstd[:, r, 0:n], in_=rstd[:, r, 0:n], func=AF.Exp, scale=-0.5), reads=["rstd%d" % r], writes=["rstd%d" % r])
            return r

        def norm_mod(ia, ib):
            def f(_):
                for t in range(2):
                    tok = slice(t * 512, (t + 1) * 512)
                    r = rstd_for_tile(t)
                    for c in range(8):
                        k = rr("tmpa", 2)
                        S.op("dve", lambda e: e.tensor_tensor(out=tmpa[:, k, :], in0=xT[:, c, tok], in1=rstd[:, r, :], op=ALU.mult),
                             reads=["x%d" % t, "rstd%d" % r], writes=["tmpa%d" % k])
                        S.op("act", lambda e: e.activation(out=hT[:, c, tok], in_=tmpa[:, k, :], func=AF.Identity,
                                                           scale=coef[:, ia, c:c + 1], bias=coef[:, ib, c:c + 1]),
                             reads=["tmpa%d" % k, "coef"], writes=["h%d" % t])
            job(f)

        def barrier_job():
            job(lambda _: S.barrier())

        ada_take = [0]

        def ffn(l, which, ia, ib, ig):
            import os
            ada_take[0] = 12
            parts = os.environ.get("FFN_PARTS", "ngd")
            norm_mod(ia, ib)
            wg, wu, wd = wg_d[which][l], wu_d[which][l], wd_d[which][l]
            for fb in range(11 if "g" in parts else 0):
                def load(slot, fb=fb):
                    dst = wsl[:, slot, :].rearrange("p (k m f) -> p k m f", k=8, m=2)
                    wdma(slot, dst[:, :, 0, :], wg.rearrange("(k p) f -> p k f", p=128)[:, :, fb * 256:(fb + 1) * 256])
                    wdma(slot, dst[:, :, 1, :], wu.rearrange("(k p) f -> p k f", p=128)[:, :, fb * 256:(fb + 1) * 256], part=1)

                def comp(slot, fb=fb):
                    w = wsl[:, slot, :].rearrange("p (k m f) -> p k m f", k=8, m=2)
                    order = [(j, t) for t in range(2) for j in range(2)] if fb == 0 else [(j, t) for j in range(2) for t in range(2)]
                    for (j, t) in order:
                        fc = 2 * fb + j
                        bg, bu = nb(), nb()

                        def f(e):
                            r = None
                            for k in range(8):
                                for m, bk in ((0, bg), (1, bu)):
                                    r = mm(e, PS(bk), w[:, k, m, j * 128:(j + 1) * 128], hT[:, k, t * 512:(t + 1) * 512], k == 0, k == 7)
                            return r
                        S.op("pe", f, reads=wres(slot) + ["h%d" % t], writes=[PSn(bg), PSn(bu)])
                        k = rr("tmpb", 2)
                        S.op("act", lambda e: e.activation(out=tmpb[:, k, :], in_=PS(bg), func=AF.Silu), reads=[PSn(bg)], writes=["tmpb%d" % k])
                        S.op("dve", lambda e: e.tensor_tensor(out=actT[:, fc, t * 512:(t + 1) * 512], in0=PS(bu), in1=tmpb[:, k, :], op=ALU.mult),
                             reads=[PSn(bu), "tmpb%d" % k], writes=["act%d" % t])
                job(comp, load)
                if ada_pending and ada_take[0] > 0:
                    ada_take[0] -= 1
                    c_, l_ = ada_pending.pop(0)
                    job(c_, l_)
            if "d" in parts:
                job(lambda _: presum_begin())
            for dc in range(8 if "d" in parts else 0):
                def load(slot, dc=dc):
                    dst = wsl[:, slot, 0:22 * 128].rearrange("p (k f) -> p k f", k=22)
                    srcw = wd.rearrange("(k p) d -> p k d", p=128)[:, :, dc * 128:(dc + 1) * 128]
                    for i_, (ka, kb) in enumerate(((0, 8), (8, 16), (16, 22))):
                        wdma(slot, dst[:, ka:kb, :], srcw[:, ka:kb, :], part=i_ % 2)

                def comp(slot, dc=dc):
                    w = wsl[:, slot, 0:22 * 128].rearrange("p (k f) -> p k f", k=22)
                    for t in range(2):
                        bk = nb()

                        def f(e):
                            r = None
                            for k in range(22):
                                r = mm(e, PS(bk), w[:, k, :], actT[:, k, t * 512:(t + 1) * 512], k == 0, k == 21)
                            return r
                        S.op("pe", f, reads=wres(slot) + ["act%d" % t], writes=[PSn(bk)])
                        xs_ = xT[:, dc, t * 512:(t + 1) * 512]
                        S.op("dve", lambda e: e.scalar_tensor_tensor(out=xs_, in0=PS(bk), scalar=coef[:, ig, dc:dc + 1], in1=xs_, op0=ALU.mult, op1=ALU.add),
                             reads=[PSn(bk), "coef", "x%d" % t], writes=["x%d" % t])
                        presum_add(dc, t)
                job(comp, load)
                if ada_pending and ada_take[0] > 0:
                    ada_take[0] -= 1
                    c_, l_ = ada_pending.pop(0)
                    job(c_, l_)

        def load_x(g):
            src = xp_d if g == 0 else xs_d

            def f(_):
                presum["on"] = False
                for tt in range(2):
                    for q4 in range(4):
                        ch = tt * 4 + q4
                        k = rr("xst", 4)
                        S.dma("sp", xstage[:, k, :], src[ch * 128:(ch + 1) * 128, :], writes=["xst%d" % k])
                        for c0 in range(0, 8, 4):
                            b = nb()

                            def f2(e):
                                r = None
                                for c in range(c0, c0 + 4):
                                    r = e.transpose(psum[:, b, (c - c0) * 128:(c - c0 + 1) * 128], xstage[:, k, c * 128:(c + 1) * 128], ident_f)
                                return r
                            S.op("pe", f2, reads=["xst%d" % k, "c:cst"], writes=[PSn(b)])
                            S.op("act" if c0 == 0 else "dve",
                                 lambda e: (e.activation(out=xT[:, c0:c0 + 4, ch * 128:(ch + 1) * 128], in_=psum[:, b, :].rearrange("p (c t) -> p c t", c=4), func=AF.Identity)
                                            if c0 == 0 else e.tensor_copy(out=xT[:, c0:c0 + 4, ch * 128:(ch + 1) * 128], in_=psum[:, b, :].rearrange("p (c t) -> p c t", c=4))),
                                 reads=[PSn(b)], writes=["x%d" % tt])
            job(f)

        def store_y(g):
            dst = yp_d if g == 0 else ys_d

            def f(_):
                fn = smallv[:, O_FN:O_FN + 8]
                for t in range(2):
                    tok = slice(t * 512, (t + 1) * 512)
                    r = rstd_for_tile(t)
                    for c in range(8):
                        S.op("dve", lambda e: e.scalar_tensor_tensor(out=xT[:, c, tok], in0=xT[:, c, tok], scalar=fn[:, c:c + 1], in1=rstd[:, r, :], op0=ALU.mult, op1=ALU.mult),
                             reads=["x%d" % t, "rstd%d" % r, "c:smallv"], writes=["x%d" % t])
                    for q4 in range(4):
                        ch = t * 4 + q4
                        k = rr("xst", 4)
                        for c0 in range(0, 8, 4):
                            b = nb()

                            def f2(e):
                                r2 = None
                                for c in range(c0, c0 + 4):
                                    r2 = e.transpose(psum[:, b, (c - c0) * 128:(c - c0 + 1) * 128], xT[:, c, ch * 128:(ch + 1) * 128], ident_f)
                                return r2
                            S.op("pe", f2, reads=["x%d" % t, "c:cst"], writes=[PSn(b)])
                            S.op("act" if c0 == 0 else "dve",
                                 lambda e: (e.activation(out=xstage[:, k, c0 * 128:(c0 + 4) * 128], in_=psum[:, b, :], func=AF.Identity)
                                            if c0 == 0 else e.tensor_copy(out=xstage[:, k, c0 * 128:(c0 + 4) * 128], in_=psum[:, b, :])),
                                 reads=[PSn(b)], writes=["xst%d" % k])
                        S.dma("sp", dst[ch * 128:(ch + 1) * 128, :], xstage[:, k, :], reads=["xst%d" % k])
            job(f)

        pending_subln = []

        def mixer(g, l):
            nkeys = 1024 if g == 0 else 1280
            koff = 0 if g == 0 else 256
            nvch = 8 if g == 0 else 10
            lam_init = 0.8 - 0.6 * math.exp(-0.3 * l)
            norm_mod(3, 4)

            def small(_):
                S.dma("pool", wuq_s[:], wuq_d[l].rearrange("(k p) f -> p k f", p=128), writes=["wuq"])
                wk = wukv_d[l].rearrange("k (h t d) -> k h t d", h=4, t=2)
                S.dma("pool", wukv_s[:, 0:256].rearrange("p (h d) -> p h d", h=4), wk[:, :, 0, :], writes=["wukv"])
                S.dma("pool", wukv_s[:, 256:512].rearrange("p (h d) -> p h d", h=4), wk[:, :, 1, :], writes=["wukv"])
                lv = smallv[:, O_LAM + l * 128:O_LAM + (l + 1) * 128]
                S.op("dve", lambda e: e.tensor_tensor(out=tmpa[:, 0, 0:32], in0=lv[:, 0:32], in1=lv[:, 32:64], op=ALU.mult), reads=["c:smallv"], writes=["tmpa0"])
                S.op("dve", lambda e: e.tensor_tensor(out=tmpa[:, 0, 32:64], in0=lv[:, 64:96], in1=lv[:, 96:128], op=ALU.mult), reads=["c:smallv"], writes=["tmpa0"])
                S.op("dve", lambda e: e.tensor_reduce(out=lamt[:, 0:2], in_=tmpa[:, 0, 0:64].rearrange("p (a b) -> p a b", a=2), axis=mybir.AxisListType.X, op=ALU.add),
                     reads=["tmpa0"], writes=["lamt"])
                S.op("act", lambda e: e.activation(out=lamt[:, 2:4], in_=lamt[:, 0:2], func=AF.Exp), reads=["lamt"], writes=["lamt"])
                S.op("dve", lambda e: e.scalar_tensor_tensor(out=lamt[:, 4:5], in0=lamt[:, 3:4], scalar=-lam_init, in1=lamt[:, 2:3], op0=ALU.add, op1=ALU.subtract),
                     reads=["lamt"], writes=["lamt"])
                S.op("dve", lambda e: e.tensor_scalar(out=lamt[:, 5:6], in0=smallv[:, O_SL + l:O_SL + l + 1], scalar1=1.0 - lam_init, scalar2=None, op0=ALU.mult),
                     reads=["c:smallv"], writes=["lamt"])
                vv = Vb[:, :, :].rearrange("p c (q s) -> p c q s", s=192)
                S.op("dve", lambda e: e.memset(vv[:, :, :, 64:128], 1.0), writes=["V"])
                if g == 1:
                    S.op("dve", lambda e: e.tensor_copy(out=wuqr_s[:], in_=wuq_s[:]), reads=["wuq"], writes=["wuqr"])
                    src = wuq_s[:].rearrange("p k (h c) -> p k h c", h=4)[:, :, :, 64:96].rearrange("p k h (q two e) -> p k h q two e", two=2, e=8)
                    dstv = wuqr_s[:].rearrange("p k (h c) -> p k h c", h=4)[:, :, :, 64:96].rearrange("p k h (q two e) -> p k h q two e", two=2, e=8)
                    for kk in range(2):
                        S.op("dve", lambda e: e.tensor_scalar(out=dstv[:, kk, :, :, 0, :], in0=src[:, kk, :, :, 1, :], scalar1=-1.0, scalar2=None, op0=ALU.mult), reads=["wuq"], writes=["wuqr"])
                        S.op("dve", lambda e: e.tensor_copy(out=dstv[:, kk, :, :, 1, :], in_=src[:, kk, :, :, 0, :]), reads=["wuq"], writes=["wuqr"])
            job(small)

            def wblock(c0, c1):
                def load(slot):
                    n = c1 - c0
                    dst = wsl[:, slot, 0:8 * n].rearrange("p (k f) -> p k f", k=8)
                    wdma(slot, dst, win_d[l].rearrange("(k p) f -> p k f", p=128)[:, :, c0:c1])
                return load

            def wv(slot, n):
                return wsl[:, slot, 0:8 * n].rearrange("p (k f) -> p k f", k=8)

            def proj_fm(w, col0, m, slot_res, evac, extra_reads=()):
                for t in range(2):
                    bk = nb()

                    def f(e):
                        r = None
                        for k in range(8):
                            r = mm(e, psum[0:m, bk, :], w[:, k, col0:col0 + m], hT[:, k, t * 512:(t + 1) * 512], k == 0, k == 7)
                        return r
                    S.op("pe", f, reads=list(slot_res) + ["h%d" % t] + list(extra_reads), writes=[PSn(bk)])
                    evac(t, bk)

            def proj_tm(w, col0, n, slot_res, evac):
                for ch in range(8):
                    b = nb()

                    def f(e):
                        r = None
                        for k in range(8):
                            r = mm(e, psum[:, b, 0:n], hT[:, k, ch * 128:(ch + 1) * 128], w[:, k, col0:col0 + n], k == 0, k == 7)
                        return r
                    S.op("pe", f, reads=list(slot_res) + ["h%d" % (ch // 4)], writes=[PSn(b)])
                    evac(ch, b)

            def rot_cols(dst, src, ncols, reads, writes):
                s5 = src.rearrange("p k (q two e) -> p k q two e", two=2, e=8)
                d5 = dst.rearrange("p k (q two e) -> p k q two e", two=2, e=8)
                S.op("dve", lambda e: e.tensor_scalar(out=d5[:, :, :, 0, :], in0=s5[:, :, :, 1, :], scalar1=-1.0, scalar2=None, op0=ALU.mult), reads=reads, writes=writes)
                S.op("dve", lambda e: e.tensor_copy(out=d5[:, :, :, 1, :], in_=s5[:, :, :, 0, :]), reads=reads, writes=writes)

            def rope_evac(pq, pr, p0, p1, out_ap, tok, reads, writes):
                k = rr("tmpa", 2)
                k2 = rr("tmpb", 2)
                S.op("dve", lambda e: e.tensor_tensor(out=tmpa[p0:p1, k, :], in0=psum[p0:p1, pq, :], in1=cosT[p0:p1, tok], op=ALU.mult),
                     reads=[PSn(pq), "c:cst"], writes=["tmpa%d" % k])
                S.op("dve", lambda e: e.tensor_tensor(out=tmpb[p0:p1, k2, :], in0=psum[p0:p1, pr, :], in1=sinT[p0:p1, tok], op=ALU.mult),
                     reads=[PSn(pr), "c:cst"], writes=["tmpb%d" % k2])
                S.op("dve", lambda e: e.tensor_tensor(out=out_ap, in0=tmpa[p0:p1, k, :], in1=tmpb[p0:p1, k2, :], op=ALU.add),
                     reads=["tmpa%d" % k, "tmpb%d" % k2] + list(reads), writes=writes)

            def stage_out(ch, col0, n, b, dram_fn, pre=None):
                if n == 256:
                    i_ = rr("stg256", 4)
                    k, col0 = i_ % 2, (160, 416)[i_ // 2]
                elif n == 512:
                    i_ = rr("stg512", 4)
                    k, col0 = i_ % 2, (672, 1184)[i_ // 2]
                else:
                    k = rr("stg", 2)
                if pre is None:
                    S.op("act", lambda e: e.activation(out=stg[:, k, col0:col0 + n], in_=psum[:, b, 0:n], func=AF.Identity), reads=[PSn(b)], writes=["stg%d_%d" % (k, col0)])
                else:
                    pre(k)
                s_, i0 = ch // 2, (ch % 2) * 128
                for (dst, c_a, c_b) in dram_fn(s_, i0):
                    srcv = stg[:, k, col0 + c_a:col0 + c_b]
                    if len(dst.shape) == 3:
                        srcv = srcv.rearrange("p (h d) -> p h d", d=64)
                    S.dma("sp", dst, srcv, reads=["stg%d_%d" % (k, col0)])

            def vcopy(ch, b, nh, pair0):
                vch = ch + (0 if g == 0 else 2)
                vv = Vb[:, vch, :].rearrange("p (q s) -> p q s", s=192)
                src = psum[:, b, 0:nh * 64].rearrange("p (q two d) -> p q two d", two=2, d=64)
                S.op("act", lambda e: e.activation(out=vv[:, pair0:pair0 + nh // 2, 0:64], in_=src[:, :, 0, :], func=AF.Identity), reads=[PSn(b)], writes=["V"])
                S.op("dve", lambda e: e.tensor_copy(out=vv[:, pair0:pair0 + nh // 2, 128:192], in_=src[:, :, 1, :]), reads=[PSn(b)], writes=["V"])

            def lhs_v(vch, s):
                base = (s // 2) * 192 + (0 if s % 2 == 0 else 64)
                return Vb[:, vch, base:base + 128]

            def finish_o(s_slot, ob, n, out_ap_fn, dst_writes):
                odd = s_slot % 2
                orow = slice(64, 128) if odd else slice(0, 64)
                drow = slice(0, 64) if odd else slice(64, 128)
                k = rr("tmpa", 2)
                S.op("act", lambda e: e.activation(out=tmpa[orow, k, 0:n], in_=psum[drow, ob, 0:n], func=AF.Ln), reads=[PSn(ob)], writes=["tmpa%d" % k])
                S.op("act", lambda e: e.activation(out=tmpa[orow, k, 0:n], in_=tmpa[orow, k, 0:n], func=AF.Exp, scale=-1.0), reads=["tmpa%d" % k], writes=["tmpa%d" % k])
                return orow, k

            def mla():
                def compA(slot):
                    w = wv(slot, 416)
                    sres = wres(slot)
                    bq = [[None, None], [None, None]]
                    for c in range(2):
                        bks = [nb(), nb()]

                        def f(e):
                            r = None
                            for k in range(8):
                                for t in range(2):
                                    r = mm(e, PS(bks[t]), w[:, k, c * 128:(c + 1) * 128], hT[:, k, t * 512:(t + 1) * 512], k == 0, k == 7)
                            return r
                        S.op("pe", f, reads=sres + ["h0", "h1"], writes=[PSn(b) for b in bks])
                        bq[c] = bks
                    for t in range(2):
                        for c in range(2):
                            S.op("act", lambda e: e.activation(out=tmpb[:, c, :], in_=PS(bq[c][t]), func=AF.Identity), reads=[PSn(bq[c][t])], writes=["tmpb%d" % c])
                        r = sumsq_rstd([(tmpb[:, c, :], ["tmpb%d" % c]) for c in range(2)], 256, "cq")
                        for c in range(2):
                            S.op("dve", lambda e: e.scalar_tensor_tensor(out=cqnT[:, c, t * 512:(t + 1) * 512], in0=tmpb[:, c, :], scalar=smallv[:, O_QN + l * 2 + c:O_QN + l * 2 + c + 1],
                                                                         in1=rstd[:, r, :], op0=ALU.mult, op1=ALU.mult),
                                 reads=["tmpb%d" % c, "rstd%d" % r, "c:smallv"], writes=["cqn"])
                    def ev_ckv(t, b):
                        S.op("act", lambda e: e.activation(out=tmpb[:, 0, :], in_=PS(b), func=AF.Identity), reads=[PSn(b)], writes=["tmpb0"])
                        r = sumsq_rstd([(tmpb[:, 0, :], ["tmpb0"])], 128, "ckv")
                        S.op("dve", lambda e: e.scalar_tensor_tensor(out=ckvT[:, koff + t * 512:koff + (t + 1) * 512], in0=tmpb[:, 0, :], scalar=smallv[:, O_KVN + l:O_KVN + l + 1],
                                                                     in1=rstd[:, r, :], op0=ALU.mult, op1=ALU.mult),
                             reads=["tmpb0", "rstd%d" % r, "c:smallv"], writes=["ckvT"])
                    proj_fm(w, 256, 128, sres, ev_ckv)
                    if g == 0:
                        def ev_kr(t, b):
                            for h in range(4):
                                S.op("act" if h % 2 else "dve",
                                     lambda e: (e.activation(out=Kb[64:96, h, t * 512:(t + 1) * 512], in_=psum[64:96, b, :], func=AF.Identity) if h % 2
                                                else e.tensor_copy(out=Kb[64:96, h, t * 512:(t + 1) * 512], in_=psum[64:96, b, :])),
                                     reads=[PSn(b)], writes=["K"])
                        proj_fm(w, 320, 96, sres, ev_kr)
                    else:
                        rot_cols(wrot[:, :, 0:96][:, :, 64:96], w[:, :, 384:416], 32, sres, ["wrotA"])
                        bks = [nb(), nb()]
                        brs = [nb(), nb()]

                        def f(e):
                            r = None
                            for k in range(8):
                                for t in range(2):
                                    r = mm(e, psum[0:96, bks[t], :], w[:, k, 320:416], hT[:, k, t * 512:(t + 1) * 512], k == 0, k == 7)
                                    r = mm(e, psum[0:96, brs[t], :], wrot[:, k, 0:96], hT[:, k, t * 512:(t + 1) * 512], k == 0, k == 7)
                            return r
                        S.op("pe", f, reads=sres + ["wrotA", "h0", "h1"], writes=[PSn(b) for b in bks + brs])
                        for t in range(2):
                            tok = slice(t * 512, (t + 1) * 512)
                            rope_evac(bks[t], brs[t], 64, 96, Kb[64:96, 0, 256 + t * 512:256 + (t + 1) * 512], tok, [], ["K"])
                            for h in range(1, 4):
                                S.op("act" if h % 2 else "dve",
                                     lambda e: (e.activation(out=Kb[64:96, h, 256 + t * 512:256 + (t + 1) * 512], in_=Kb[64:96, 0, 256 + t * 512:256 + (t + 1) * 512], func=AF.Identity) if h % 2
                                                else e.tensor_copy(out=Kb[64:96, h, 256 + t * 512:256 + (t + 1) * 512], in_=Kb[64:96, 0, 256 + t * 512:256 + (t + 1) * 512])),
                                     reads=["K"], writes=["K"])
                    if g == 0:
                        def ev_tm(ch, b):
                            def pre(k):
                                p = ch % 2
                                cs, cr = 6 + 2 * p, 7 + 2 * p
                                S.op("act", lambda e: e.activation(out=tmpb[:, p, 0:160], in_=psum[:, b, 0:160], func=AF.Identity), reads=[PSn(b)], writes=["tmpb%d" % p])
                                S.op("dve", lambda e: e.memset(lamt[:, cs:cs + 1], 0.0), writes=["lamt%d" % cs])
                                S.op("dve", lambda e: e.scalar_tensor_tensor(out=tmpb[:, p, 256:384], in0=tmpb[:, p, 0:128], scalar=1.0, in1=tmpb[:, p, 0:128], op0=ALU.mult, op1=ALU.mult,
                                                                             accum_out=lamt[:, cs:cs + 1]),
                                     reads=["tmpb%d" % p], writes=["tmpbj%d" % p, "lamt%d" % cs])
                                S.op("act", lambda e: e.activation(out=lamt[:, cr:cr + 1], in_=lamt[:, cs:cs + 1], func=AF.Ln, scale=1.0 / 128, bias=epsc[:, 0:1]), reads=["lamt%d" % cs, "c:eps"], writes=["lamt%d" % cr])
                                S.op("act", lambda e: e.activation(out=lamt[:, cr:cr + 1], in_=lamt[:, cr:cr + 1], func=AF.Exp, scale=-0.5), reads=["lamt%d" % cr], writes=["lamt%d" % cr])
                                S.op("dve", lambda e: e.scalar_tensor_tensor(out=stg[:, k, 0:128], in0=tmpb[:, p, 0:128], scalar=lamt[:, cr:cr + 1], in1=gkvb[:, l * 128:(l + 1) * 128],
                                                                             op0=ALU.mult, op1=ALU.mult),
                                     reads=["tmpb%d" % p, "lamt%d" % cr, "c:gkvb"], writes=["stg%d_0" % k])
                                S.op("dve", lambda e: e.tensor_copy(out=stg[:, k, 128:160], in_=tmpb[:, p, 128:160]), reads=["tmpb%d" % p], writes=["stg%d_0" % k])
                            stage_out(ch, 0, 160, b, lambda s_, i0: [(sckv_d[s_, l, i0:i0 + 128, :], 0, 128), (skr_d[s_, l, i0:i0 + 128, :], 128, 160)], pre=pre)
                        proj_tm(w, 256, 160, sres, ev_tm)
                job(compA, wblock(0, 416))

                def compM(_):
                    if g == 1:
                        S.op("dve", lambda e: e.memset(kst[:, :, 128:192], 0.0), writes=["kst"])
                        S.dma("pool", kst[:, :, 0:128], cckv_d[l].rearrange("(c p) f -> p c f", p=128), writes=["kst"], deps=S.bar_toks)
                        S.dma("pool", kst[:, :, 192:224], ckr_d[l].rearrange("(c p) f -> p c f", p=128), writes=["kst"], deps=S.bar_toks)
                        pst = psum[:, 7, :].bitcast(BF16)
                        for c in range(2):
                            S.op("pe", lambda e: (e.transpose(pst[:, c * 256:c * 256 + 128], kst[:, c, 0:128], ident_b[:]),
                                                  e.transpose(pst[0:96, c * 256 + 128:c * 256 + 256], kst[:, c, 128:224], ident_b[:]))[1],
                                 reads=["kst", "c:identb"], writes=[PSn(7)])
                        bank_i[0] = 0
                        for c in range(2):
                            S.op("dve", lambda e: e.tensor_copy(out=ckvT[:, c * 128:(c + 1) * 128], in_=pst[:, c * 256:c * 256 + 128]), reads=[PSn(7)], writes=["ckvT"])
                            for h in range(4):
                                S.op("act" if h % 2 else "dve",
                                     lambda e: (e.activation(out=Kb[64:96, h, c * 128:(c + 1) * 128], in_=pst[64:96, c * 256 + 128:c * 256 + 256], func=AF.Identity) if h % 2
                                                else e.tensor_copy(out=Kb[64:96, h, c * 128:(c + 1) * 128], in_=pst[64:96, c * 256 + 128:c * 256 + 256])),
                                     reads=[PSn(7)], writes=["K"])
                    for h in range(4):
                        bks = [nb(), nb()]
                        brs = [nb(), nb()] if g == 1 else None

                        def f(e):
                            r = None
                            for k in range(2):
                                for t in range(2):
                                    r = mm(e, psum[0:96, bks[t], :], wuq_s[:, k, h * 96:(h + 1) * 96], cqnT[:, k, t * 512:(t + 1) * 512], k == 0, k == 1)
                                    if g == 1:
                                        r = mm(e, psum[0:96, brs[t], :], wuqr_s[:, k, h * 96:(h + 1) * 96], cqnT[:, k, t * 512:(t + 1) * 512], k == 0, k == 1)
                            return r
                        S.op("pe", f, reads=["wuq", "wuqr", "cqn"], writes=[PSn(b) for b in bks + (brs or [])])
                        for t in range(2):
                            tok = slice(t * 512, (t + 1) * 512)
                            if g == 0:
                                S.op("act", lambda e: e.activation(out=Qb[0:96, h, tok], in_=psum[0:96, bks[t], :], func=AF.Identity), reads=[PSn(bks[t])], writes=["Q"])
                            else:
                                S.op("act", lambda e: e.activation(out=Qb[0:64, h, tok], in_=psum[0:64, bks[t], :], func=AF.Identity), reads=[PSn(bks[t])], writes=["Q"])
                                rope_evac(bks[t], brs[t], 64, 96, Qb[64:96, h, tok], tok, [], ["Q"])
                    nk_t = nkeys // 512 if g == 0 else None
                    kslices = [(i * 512, 512) for i in range(2)] if g == 0 else [(0, 512), (512, 512), (1024, 256)]
                    for h in range(4):
                        for (k0, kn) in kslices:
                            b = nb()
                            S.op("pe", lambda e: mm(e, psum[0:64, b, 0:kn], wukv_s[:, h * 64:(h + 1) * 64], ckvT[:, k0:k0 + kn], True, True),
                                 reads=["wukv", "ckvT"], writes=[PSn(b)])
                            S.op("act" if h % 2 else "dve",
                                 lambda e: (e.activation(out=Kb[0:64, h, k0:k0 + kn], in_=psum[0:64, b, 0:kn], func=AF.Identity) if h % 2
                                            else e.tensor_copy(out=Kb[0:64, h, k0:k0 + kn], in_=psum[0:64, b, 0:kn])),
                                 reads=[PSn(b)], writes=["K"])
                    for vch in range(nvch):
                        b = nb()
                        S.op("pe", lambda e: mm(e, psum[:, b, 0:256], ckvT[:, vch * 128:(vch + 1) * 128], wukv_s[:, 256:512], True, True),
                             reads=["wukv", "ckvT"], writes=[PSn(b)])
                        vv = Vb[:, vch, :].rearrange("p (q s) -> p q s", s=192)
                        src = psum[:, b, 0:256].rearrange("p (q two d) -> p q two d", two=2, d=64)
                        S.op("act", lambda e: e.activation(out=vv[:, 0:2, 0:64], in_=src[:, :, 0, :], func=AF.Identity), reads=[PSn(b)], writes=["V"])
                        S.op("dve", lambda e: e.tensor_copy(out=vv[:, 0:2, 128:192], in_=src[:, :, 1, :]), reads=[PSn(b)], writes=["V"])
                    steps = []
                    for h in range(4):
                        steps += dense_steps(lambda ks, h=h: Kb[0:96, h, ks], lambda qs, h=h: Qb[0:96, h, qs], h, MLA_SCALE, std_fin(h))
                    run_attn(steps)
                job(compM)

            REG = (2, 4, 6)

            def region(rb):
                return psum[:, rb:rb + 2, :].rearrange("p a b -> p (a b)")

            def std_fin(oslot):
                def fin(ob, q0, qn):
                    orow, k = finish_o(oslot, ob, qn, None, None)
                    S.op("dve", lambda e: e.tensor_tensor(out=oT[orow, oslot // 2, q0:q0 + qn], in0=psum[orow, ob, 0:qn], in1=tmpa[orow, k, 0:qn], op=ALU.mult),
                         reads=[PSn(ob), "tmpa%d" % k], writes=["oT"])
                return fin

            def dense_steps(KT, QT, vslot, scale, fin, only_units=None):
                steps = []
                if g == 0:
                    return dense_steps_prompt(KT, QT, vslot, scale, fin, only_units)
                else:
                    units = [(qb * 512, 512, [[(c * 128, c) for c in (2 * i, 2 * i + 1)] for i in range(5)]) for qb in range(2)]
                for ui, (q0, qn, groups_) in enumerate(units):
                    if only_units is not None and ui not in only_units:
                        continue
                    for gi, kcs in enumerate(groups_):
                        first, last = gi == 0, gi == len(groups_) - 1

                        def S_(rb, kcs=kcs, q0=q0, qn=qn):
                            reg = region(rb)

                            def f(e):
                                r = None
                                for jj, (k0, vch) in enumerate(kcs):
                                    r = mm(e, reg[:, jj * qn:(jj + 1) * qn], KT(slice(k0, k0 + 128)), QT(slice(q0, q0 + qn)), True, True)
                                return r
                            S.op("pe", f, reads=["K", "Q"], writes=[PSn(rb), PSn(rb + 1)])

                        def E_(rb, k, kcs=kcs, qn=qn):
                            reg = region(rb)
                            S.op("act", lambda e: e.activation(out=Pb[:, k, 0:len(kcs) * qn], in_=reg[:, 0:len(kcs) * qn], func=AF.Exp, scale=scale),
                                 reads=[PSn(rb), PSn(rb + 1)], writes=["P%d" % k])

                        def PV_(ob, k, kcs=kcs, qn=qn, first=first, last=last):
                            def f2(e):
                                r = None
                                for jj, (k0, vch) in enumerate(kcs):
                                    r = mm(e, psum[:, ob, 0:qn], lhs_v(vch, vslot), Pb[:, k, jj * qn:(jj + 1) * qn], first and jj == 0, last and jj == len(kcs) - 1)
                                return r
                            S.op("pe", f2, reads=["V", "P%d" % k], writes=[PSn(ob)])
                        steps.append({"S": S_, "E": E_, "PV": PV_, "first": first, "last": last, "fin": (lambda ob, q0=q0, qn=qn: fin(ob, q0, qn))})
                return steps

            def dense_steps_prompt(KT, QT, vslot, scale, fin, only_units=None):
                steps = []
                for ui in range(2):
                    if only_units is not None and ui not in only_units:
                        continue
                    q0 = ui * 512

                    def S_(rb, ui=ui):
                        reg = region(rb)

                        def f(e):
                            r = None
                            for sq_ in range(2):
                                s_ = ui * 2 + sq_
                                for c in range(2):
                                    r = mm(e, reg[:, sq_ * 512 + c * 256:sq_ * 512 + (c + 1) * 256], KT(slice(s_ * 256 + c * 128, s_ * 256 + (c + 1) * 128)),
                                           QT(slice(s_ * 256, (s_ + 1) * 256)), True, True)
                            return r
                        S.op("pe", f, reads=["K", "Q"], writes=[PSn(rb), PSn(rb + 1)])

                    def E_(rb, k):
                        reg = region(rb)
                        S.op("act", lambda e: e.activation(out=Pb[:, k, :], in_=reg[:, :], func=AF.Exp, scale=scale), reads=[PSn(rb), PSn(rb + 1)], writes=["P%d" % k])

                    def PV_(ob, k, ui=ui):
                        def f2(e):
                            r = None
                            for sq_ in range(2):
                                s_ = ui * 2 + sq_
                                for c in range(2):
                                    r = mm(e, psum[:, ob, sq_ * 256:(sq_ + 1) * 256], lhs_v(s_ * 2 + c, vslot), Pb[:, k, sq_ * 512 + c * 256:sq_ * 512 + (c + 1) * 256], c == 0, c == 1)
                            return r
                        S.op("pe", f2, reads=["V", "P%d" % k], writes=[PSn(ob)])
                    steps.append({"S": S_, "E": E_, "PV": PV_, "first": True, "last": True, "fin": (lambda ob, q0=q0: fin(ob, q0, 512))})
                return steps

            def run_attn(steps, LA=2):
                n = len(steps)
                reg_of, p_of = {}, {}
                st_ = {"ri": 0, "oi": 0, "ob": None}

                def front(i):
                    rb = REG[st_["ri"] % 3]
                    st_["ri"] += 1
                    k = rr("P", 3)
                    reg_of[i], p_of[i] = rb, k
                    steps[i]["S"](rb)
                    steps[i]["E"](rb, k)
                for i in range(min(LA, n)):
                    front(i)
                for i in range(n):
                    if i + LA < n:
                        front(i + LA)
                    stp = steps[i]
                    if stp["first"]:
                        st_["ob"] = st_["oi"] % 2
                        st_["oi"] += 1
                    stp["PV"](st_["ob"], p_of[i])
                    if stp["last"]:
                        stp["fin"](st_["ob"])

            def diff():
                def compB(slot):
                    w = wv(slot, 512)
                    sres = wres(slot)
                    if g == 1:
                        rot_cols(wrot[:, :, 96:608], w[:, :, 0:512], 512, sres, ["wrotB"])
                    for qk in range(2):
                        for h in range(4):
                            col0 = qk * 256 + h * 64
                            if g == 0:
                                def ev(t, b, qk=qk, h=h):
                                    dst = (Qb if qk == 0 else Kb)[0:64, h, t * 512:(t + 1) * 512]
                                    S.op("act" if h % 2 else "dve",
                                         lambda e: (e.activation(out=dst, in_=psum[0:64, b, :], func=AF.Identity) if h % 2 else e.tensor_copy(out=dst, in_=psum[0:64, b, :])),
                                         reads=[PSn(b)], writes=["Q" if qk == 0 else "K"])
                                proj_fm(w, col0, 64, sres, ev)
                            else:
                                bks = [nb(), nb()]
                                brs = [nb(), nb()]

                                def f(e):
                                    r = None
                                    for k in range(8):
                                        for t in range(2):
                                            r = mm(e, psum[0:64, bks[t], :], w[:, k, col0:col0 + 64], hT[:, k, t * 512:(t + 1) * 512], k == 0, k == 7)
                                            r = mm(e, psum[0:64, brs[t], :], wrot[:, k, 96 + col0:96 + col0 + 64], hT[:, k, t * 512:(t + 1) * 512], k == 0, k == 7)
                                    return r
                                S.op("pe", f, reads=sres + ["wrotB", "h0", "h1"], writes=[PSn(b) for b in bks + brs])
                                for t in range(2):
                                    tok = slice(t * 512, (t + 1) * 512)
                                    dst = Qb[0:64, h, tok] if qk == 0 else Kb[0:64, h, 256 + t * 512:256 + (t + 1) * 512]
                                    rope_evac(bks[t], brs[t], 0, 64, dst, tok, [], ["Q" if qk == 0 else "K"])
                    if g == 0:
                        def ev_tm(ch, b):
                            stage_out(ch, 160, 256, b, lambda s_, i0: [(sdk_d[s_, l, :, i0:i0 + 128, :].rearrange("h t d -> t h d"), 0, 256)])
                        proj_tm(w, 256, 256, sres, ev_tm)
                job(compB, wblock(416, 928))

                def compC(slot):
                    w = wv(slot, 256)
                    sres = wres(slot)

                    def ev(ch, b):
                        vcopy(ch, b, 4, 0)
                        if g == 0:
                            stage_out(ch, 416, 256, b, lambda s_, i0: [(sdv_d[s_, l, :, i0:i0 + 128, :].rearrange("h t d -> t h d"), 0, 256)])
                    proj_tm(w, 0, 256, sres, ev)
                    if g == 1:
                        for c in range(2):
                            S.dma("pool", kst[:, c, 0:256].rearrange("p (h d) -> p h d", h=4), cdk_d[l][:, c * 128:(c + 1) * 128, :].rearrange("h p d -> p h d"), writes=["kst"], deps=S.bar_toks)
                        for c in range(2):
                            vv = Vb[:, c, :].rearrange("p (q s) -> p q s", s=192)
                            srcv = cdv_d[l][:, c * 128:(c + 1) * 128, :].rearrange("(q two) p d -> p q two d", two=2)
                            S.dma("pool", vv[:, 0:2, 0:64], srcv[:, :, 0, :], writes=["V"], deps=S.bar_toks)
                            S.dma("pool", vv[:, 0:2, 128:192], srcv[:, :, 1, :], writes=["V"], deps=S.bar_toks)
                        pst = psum[:, 7, :].bitcast(BF16)
                        for c in range(2):
                            S.op("pe", lambda e: [e.transpose(pst[0:64, (c * 4 + h) * 128:(c * 4 + h + 1) * 128], kst[:, c, h * 64:(h + 1) * 64], ident_b[:]) for h in range(4)][-1],
                                 reads=["kst", "c:identb"], writes=[PSn(7)])
                        bank_i[0] = 0
                        for c in range(2):
                            for h in range(4):
                                S.op("act" if h % 2 else "dve",
                                     lambda e: (e.activation(out=Kb[0:64, h, c * 128:(c + 1) * 128], in_=pst[0:64, (c * 4 + h) * 128:(c * 4 + h + 1) * 128], func=AF.Identity) if h % 2
                                                else e.tensor_copy(out=Kb[0:64, h, c * 128:(c + 1) * 128], in_=pst[0:64, (c * 4 + h) * 128:(c * 4 + h + 1) * 128])),
                                     reads=[PSn(7)], writes=["K"])
                    def subln_all():
                        for pr_ in range(2):
                            for t in range(2):
                                tok = slice(t * 512, (t + 1) * 512)
                                k = rr("sq", 3)
                                b = nb()
                                S.op("dve", lambda e: e.tensor_tensor(out=sqb[:, k, :], in0=oT[:, 2 + pr_, tok], in1=oT[:, 2 + pr_, tok], op=ALU.mult), reads=["oT"], writes=["sq%d" % k])
                                S.op("pe", lambda e: mm(e, psum[:, b, :], bones_b[:], sqb[:, k, :], True, True), reads=["sq%d" % k, "c:bones"], writes=[PSn(b)])
                                r = rr("rstd", 2)
                                S.op("act", lambda e: e.activation(out=rstd[:, r, :], in_=psum[:, b, :], func=AF.Ln, scale=1.0 / 64, bias=epsc[:, 0:1]),
                                     reads=[PSn(b), "c:eps"], writes=["rstd%d" % r])
                                S.op("act", lambda e: e.activation(out=rstd[:, r, :], in_=rstd[:, r, :], func=AF.Exp, scale=-0.5), reads=["rstd%d" % r], writes=["rstd%d" % r])
                                S.op("dve", lambda e: e.scalar_tensor_tensor(out=oT[:, 2 + pr_, tok], in0=oT[:, 2 + pr_, tok], scalar=lamt[:, 5:6], in1=rstd[:, r, :], op0=ALU.mult, op1=ALU.mult),
                                     reads=["rstd%d" % r, "lamt"], writes=["oT"])

                    steps = []
                    for pr_ in range(2):
                        for ui in range(2):
                            for hh in range(2):
                                h = pr_ * 2 + hh
                                orow = slice(64, 128) if hh else slice(0, 64)
                                res = []
                                for half in range(2):
                                    prow = slice(half * 32, half * 32 + 32)

                                    def fin(ob, q0, qn, half=half, h=h, hh=hh, res=res, orow=orow, pr_=pr_):
                                        orow_, k = finish_o(h, ob, qn, None, None)
                                        kk = rr("tmpb", 2)
                                        S.op("dve", lambda e: e.tensor_tensor(out=tmpb[orow_, kk, 0:qn], in0=psum[orow_, ob, 0:qn], in1=tmpa[orow_, k, 0:qn], op=ALU.mult),
                                             reads=[PSn(ob), "tmpa%d" % k], writes=["tmpb%d" % kk])
                                        res.append(kk)
                                        if half == 1:
                                            S.op("dve", lambda e: e.scalar_tensor_tensor(out=oT[orow, 2 + pr_, q0:q0 + qn], in0=tmpb[orow, res[1], 0:qn], scalar=lamt[orow, 4:5], in1=tmpb[orow, res[0], 0:qn],
                                                                                         op0=ALU.mult, op1=ALU.add),
                                                 reads=["tmpb%d" % res[0], "tmpb%d" % res[1], "lamt"], writes=["oT"])
                                    steps += dense_steps(lambda ks, prow=prow, h=h: Kb[prow, h, ks], lambda qs, prow=prow, h=h: Qb[prow, h, qs], h, DIFF_SCALE, fin, only_units=[ui])
                    run_attn(steps)
                    pending_subln.append(subln_all)
                job(compC, wblock(928, 1184))

            def nat():
                def compD(slot):
                    w = wv(slot, 512)
                    for c in range(4):
                        def ev(t, b, c=c):
                            S.op("act", lambda e: e.activation(out=Qb[:, c, t * 512:(t + 1) * 512], in_=PS(b), func=AF.Identity, scale=0.125), reads=[PSn(b)], writes=["Q"])
                        proj_fm(w, c * 128, 128, wres(slot), ev)
                job(compD, wblock(1184, 1696))

                def compE(slot):
                    w = wv(slot, 512)
                    for c in range(4):
                        def ev(t, b, c=c):
                            S.op("dve", lambda e: e.tensor_copy(out=Kb[:, c, koff + t * 512:koff + (t + 1) * 512], in_=PS(b)), reads=[PSn(b)], writes=["K"])
                        proj_fm(w, c * 128, 128, wres(slot), ev)
                    if g == 0:
                        def ev_tm(ch, b):
                            stage_out(ch, 672, 512, b, lambda s_, i0: [(snk_d[s_, l, :, i0:i0 + 128, :].rearrange("h t d -> t h d"), 0, 512)])
                        proj_tm(w, 0, 512, wres(slot), ev_tm)
                job(compE, wblock(1696, 2208))

                def compF(slot):
                    w = wv(slot, 512)

                    def ev(ch, b):
                        vcopy(ch, b, 8, 0)
                        if g == 0:
                            stage_out(ch, 1184, 512, b, lambda s_, i0: [(snv_d[s_, l, :, i0:i0 + 128, :].rearrange("h t d -> t h d"), 0, 512)])
                    proj_tm(w, 0, 512, wres(slot), ev)
                    if g == 0:
                        steps = []
                        for h in range(8):
                            hb = (h % 2) * 64
                            steps += dense_steps(lambda ks, hb=hb, h=h: Kb[hb:hb + 64, h // 2, ks], lambda qs, hb=hb, h=h: Qb[hb:hb + 64, h // 2, qs], h, 1.0, std_fin(8 + h))
                        run_attn(steps)
                        return
                    for c in range(2):
                        S.dma("pool", kst[:, c, 0:512].rearrange("p (h d) -> p h d", h=8), cnk_d[l][:, c * 128:(c + 1) * 128, :].rearrange("h p d -> p h d"), writes=["kst"], deps=S.bar_toks)
                    for c in range(2):
                        vv = Vb[:, c, :].rearrange("p (q s) -> p q s", s=192)
                        srcv = cnv_d[l][:, c * 128:(c + 1) * 128, :].rearrange("(q two) p d -> p q two d", two=2)
                        S.dma("pool", vv[:, 0:4, 0:64], srcv[:, :, 0, :], writes=["V"], deps=S.bar_toks)
                        S.dma("pool", vv[:, 0:4, 128:192], srcv[:, :, 1, :], writes=["V"], deps=S.bar_toks)
                    pst = psum[:, 7, :].bitcast(BF16)
                    for c in range(2):
                        S.op("pe", lambda e: [e.transpose(pst[:, (c * 4 + cc) * 128:(c * 4 + cc + 1) * 128], kst[:, c, cc * 128:(cc + 1) * 128], ident_b[:]) for cc in range(4)][-1],
                             reads=["kst", "c:identb"], writes=[PSn(7)])
                    bank_i[0] = 0
                    for c in range(2):
                        for cc in range(4):
                            S.op("act" if cc % 2 else "dve",
                                 lambda e: (e.activation(out=Kb[:, cc, c * 128:(c + 1) * 128], in_=pst[:, (c * 4 + cc) * 128:(c * 4 + cc + 1) * 128], func=AF.Identity) if cc % 2
                                            else e.tensor_copy(out=Kb[:, cc, c * 128:(c + 1) * 128], in_=pst[:, (c * 4 + cc) * 128:(c * 4 + cc + 1) * 128])),
                                 reads=[PSn(7)], writes=["K"])
                    def natb_load(hp):
                        S.dma("pool", natb_s[:, hp % 2, :], natb_d[l][:, hp * 2048:(hp + 1) * 2048], writes=["natb%d" % (hp % 2)], deps=S.bar_toks)
                    natb_load(0)
                    steps = []
                    for hp in range(4):
                        nbuf = hp % 2
                        for hh in range(2):
                            h = hp * 2 + hh
                            hb = hh * 64
                            c = hp
                            tab = natb_s[:, nbuf, hh * 1024:(hh + 1) * 1024].rearrange("p (j q) -> p j q", q=64)
                            for qb in range(2):
                                q0 = qb * 512
                                pre = (hp + 1) if (hh == 0 and qb == 0 and hp + 1 < 4) else None

                                def S0(rb, hb=hb, c=c, q0=q0, pre=pre):
                                    if pre is not None:
                                        natb_load(pre)
                                    reg = region(rb)

                                    def f(e):
                                        r = None
                                        for kc in range(2):
                                            r = mm(e, reg[:, kc * 512:(kc + 1) * 512], Kb[hb:hb + 64, c, kc * 128:(kc + 1) * 128], Qb[hb:hb + 64, c, q0:q0 + 512], True, True)
                                        return r
                                    S.op("pe", f, reads=["K", "Q"], writes=[PSn(rb), PSn(rb + 1)])

                                def E0(rb, k):
                                    reg = region(rb)
                                    S.op("act", lambda e: e.activation(out=Pb[:, k, :], in_=reg[:, :], func=AF.Exp), reads=[PSn(rb), PSn(rb + 1)], writes=["P%d" % k])

                                def PV0(ob, k, h=h):
                                    def f2(e):
                                        r = None
                                        for kc in range(2):
                                            r = mm(e, PS(ob), lhs_v(kc, h), Pb[:, k, kc * 512:(kc + 1) * 512], kc == 0, False)
                                        return r
                                    S.op("pe", f2, reads=["V", "P%d" % k], writes=[PSn(ob)])
                                steps.append({"S": S0, "E": E0, "PV": PV0, "first": True, "last": False, "fin": None})
                                for tq in range(4):
                                    t = qb * 4 + tq
                                    js, inval = nat_chunks(t)
                                    nj = len(js)

                                    def S1(rb, hb=hb, c=c, t=t, js=js, tab=tab, nbuf=nbuf):
                                        reg = region(rb)

                                        def f(e):
                                            r = None
                                            for jj, j in enumerate(js):
                                                r = mm(e, reg[:, jj * 128:(jj + 1) * 128], Kb[hb:hb + 64, c, 256 + j * 128:256 + (j + 1) * 128], Qb[hb:hb + 64, c, t * 128:(t + 1) * 128], True, False)
                                                jx0 = 8 - (2 * j - 2 * t)
                                                r = mm(e, reg[:, jj * 128:(jj + 1) * 128], ident_b[:], tab[:, jx0:jx0 + 2, :].rearrange("p j q -> p (j q)"), False, True)
                                            return r
                                        S.op("pe", f, reads=["K", "Q", "natb%d" % nbuf, "c:identb"], writes=[PSn(rb), PSn(rb + 1)])

                                    def E1(rb, k, nj=nj, inval=inval):
                                        reg = region(rb)
                                        S.op("act", lambda e: e.activation(out=Pb[:, k, 0:nj * 128], in_=reg[:, 0:nj * 128], func=AF.Exp), reads=[PSn(rb), PSn(rb + 1)], writes=["P%d" % k])
                                        for (jj, a, b_) in inval:
                                            S.op("dve", lambda e: e.memset(Pb[a * 64:(a + 1) * 64, k, jj * 128 + b_ * 64:jj * 128 + b_ * 64 + 64], 0.0), writes=["P%d" % k])

                                    def PV1(ob, k, h=h, js=js, nj=nj, tq=tq):
                                        def f2(e):
                                            r = None
                                            for jj, j in enumerate(js):
                                                r = mm(e, psum[:, ob, tq * 128:(tq + 1) * 128], lhs_v(2 + j, h), Pb[:, k, jj * 128:(jj + 1) * 128], False, jj == nj - 1)
                                            return r
                                        S.op("pe", f2, reads=["V", "P%d" % k], writes=[PSn(ob)])

                                    def fin1(ob, h=h, c=c, q0=q0):
                                        orow, k = finish_o(h, ob, 512, None, None)
                                        S.op("dve", lambda e: e.tensor_tensor(out=oT[orow, 4 + c, q0:q0 + 512], in0=psum[orow, ob, :], in1=tmpa[orow, k, :], op=ALU.mult),
                                             reads=[PSn(ob), "tmpa%d" % k], writes=["oT"])
                                    steps.append({"S": S1, "E": E1, "PV": PV1, "first": False, "last": tq == 3, "fin": fin1})
                    run_attn(steps)
                job(compF, wblock(2208, 2720))

            sel = ("mla", "diff", "nat") if mixsel == "all" else tuple(mixsel.split(","))
            if "mla" in sel:
                mla()
            if "diff" in sel:
                diff()
            if "nat" in sel:
                nat()
            if mixsel != "all":
                def z(_):
                    for c in range(8):
                        typ = "mla" if c < 2 else ("diff" if c < 4 else "nat")
                        if typ not in sel:
                            S.op("dve", lambda e: e.memset(oT[:, c, :], 0.0), writes=["oT"])
                job(z)

            job(lambda _: presum_begin())
            for dq_ in range(2):
                def load(slot, dq_=dq_):
                    dst = wsl[:, slot, :].rearrange("p (k f) -> p k f", k=8)
                    wdma(slot, dst, wout_d[l].rearrange("(k p) f -> p k f", p=128)[:, :, dq_ * 512:(dq_ + 1) * 512])

                def comp(slot, dq_=dq_):
                    w = wsl[:, slot, :].rearrange("p (k f) -> p k f", k=8)
                    while pending_subln:
                        pending_subln.pop(0)()
                    for dd in range(4):
                        dc = dq_ * 4 + dd
                        bks = [nb(), nb()]

                        def f(e):
                            r = None
                            for k in range(8):
                                for t in range(2):
                                    r = mm(e, PS(bks[t]), w[:, k, dd * 128:(dd + 1) * 128], oT[:, k, t * 512:(t + 1) * 512], k == 0, k == 7)
                            return r
                        S.op("pe", f, reads=wres(slot) + ["oT"], writes=[PSn(b) for b in bks])
                        for t in range(2):
                            xs_ = xT[:, dc, t * 512:(t + 1) * 512]
                            S.op("dve", lambda e: e.scalar_tensor_tensor(out=xs_, in0=PS(bks[t]), scalar=coef[:, 5, dc:dc + 1], in1=xs_, op0=ALU.mult, op1=ALU.add),
                                 reads=[PSn(bks[t]), "coef", "x%d" % t], writes=["x%d" % t])
                            presum_add(dc, t)
                job(comp, load)

        for g in groups:
            load_x(g)
            barrier_job()
            while ada_first:
                c_, l_ = ada_first.pop(0)
                job(c_, l_)
            done = False
            for l in range(depth):
                make_coef(g, l, 0)
                if stop == (l, 0):
                    break
                ffn(l, 0, 0, 1, 2)
                barrier_job()
                if stop == (l, 1):
                    break
                make_coef(g, l, 1)
                mixer(g, l)
                barrier_job()
                if stop == (l, 2):
                    break
                make_coef(g, l, 2)
                ffn(l, 1, 6, 7, 8)
                barrier_job()
                if stop == (l, 3):
                    break
            store_y(g)
            barrier_job()
        run_jobs()
        for q in S.dq.values():
            for s_ in q["sems"]:
                if s_[1] > 0:
                    S._wait("sp", (s_[0], s_[1]))
        if stats is not None:
            stats.update({"ninst": dict(S.ninst), "nwait": dict(S.nwait), "nsem": S.nsem})
    return nc


def _consts():
    ident = np.eye(128, dtype=np.float32)
    t = np.arange(1024)
    freqs = (np.float32(10000.0) ** (-np.arange(8, dtype=np.float32) / np.float32(8))).astype(np.float32)
    cos = np.zeros((128, 1024), np.float32)
    sin = np.zeros((128, 1024), np.float32)
    for p in range(128):
        r = p % 32
        pos = (t // 64) if r < 16 else (t % 64)
        ang = pos.astype(np.float32) * freqs[r % 8]
        cos[p] = np.cos(ang).astype(np.float32)
        sin[p] = np.sin(ang).astype(np.float32)
    return np.concatenate([ident, cos, sin], axis=1)


def _natb(rpb):
    Ln = rpb.shape[0]
    ck = np.arange(64)[:, None]
    cq = np.arange(64)[None, :]
    cs = np.clip(cq - 8, 0, 48)
    inwin = (ck >= cs) & (ck < cs + 16)
    dc = np.clip(ck - cq, -15, 15) + 15
    out = np.zeros((Ln, 128, 8, 16, 64), np.float32)
    for half in range(2):
        for jx in range(16):
            dr = (15 - jx) if half == 0 else (16 - jx)
            if 0 <= dr <= 14:
                val = np.where(inwin[None, None], rpb[:, :, dr][:, :, dc], np.float32(NEG))
                out[:, half * 64:(half + 1) * 64, :, jx, :] = np.transpose(val, (0, 2, 1, 3))
    return out.reshape(Ln, 128, 8 * 16 * 64)


def _smallv(inp, core):
    b = core // 2
    sv = np.zeros((128, NSV), np.float32)

    def fm(v):
        v = np.asarray(v, np.float32)
        return np.moveaxis(v.reshape(v.shape[:-1] + (-1, 128)), -1, 0)
    sv[:, O_N1:O_N1 + 32] = fm(inp["ffn1_norm"]).reshape(128, -1)
    sv[:, O_NM:O_NM + 32] = fm(inp["mix_norm"]).reshape(128, -1)
    sv[:, O_N2:O_N2 + 32] = fm(inp["ffn2_norm"]).reshape(128, -1)
    sv[:, O_FN:O_FN + 8] = fm(inp["final_norm"]).reshape(128, -1)
    sv[:, O_BADA:O_BADA + 288] = fm(inp["b_ada"]).reshape(128, -1)
    cv = np.stack([inp["c_ctx"], inp["c"][b]], axis=0)
    sv[:, O_C:O_C + 16] = np.transpose(fm(cv), (0, 2, 1)).reshape(128, 16)
    sv[:, O_QN:O_QN + 8] = fm(inp["mla_q_norm"]).reshape(128, -1)
    sv[:, O_KVN:O_KVN + 4] = fm(inp["mla_kv_norm"]).reshape(128, -1)
    sv[:, O_SL:O_SL + 4] = np.concatenate([inp["diff_subln"].T, inp["diff_subln"].T], axis=0)
    lam = np.stack([inp["diff_lambda_q1"], inp["diff_lambda_k1"], inp["diff_lambda_q2"], inp["diff_lambda_k2"]], axis=1)
    sv[:, O_LAM:O_LAM + 512] = np.broadcast_to(lam.reshape(1, -1), (128, 512))
    return sv


def make_in_maps(inp, cores=range(NCORES)):
    f = lambda a: np.ascontiguousarray(np.asarray(a, np.float32))
    cst = _consts()
    natb = _natb(np.asarray(inp["nat_rpb"], np.float32))
    gkvb = np.ascontiguousarray(np.broadcast_to(np.asarray(inp["mla_kv_norm"], np.float32).reshape(1, -1), (128, L * 128)))
    shared = {
        "cst": cst, "natb": natb, "gkvb": gkvb,
        "w_ada": f(inp["w_ada"]), "wg1": f(inp["ffn1_w_gate"]), "wu1": f(inp["ffn1_w_up"]), "wd1": f(inp["ffn1_w_down"]),
        "wg2": f(inp["ffn2_w_gate"]), "wu2": f(inp["ffn2_w_up"]), "wd2": f(inp["ffn2_w_down"]),
        "w_in": f(inp["w_in"]), "wuq": f(inp["mla_w_uq"]), "wukv": f(inp["mla_w_ukv"]), "w_out": f(inp["w_out"]),
    }
    maps = []
    for c in cores:
        b = c // 2
        m = dict(shared)
        m["xp"] = f(inp["x_prompt"][4 * c:4 * c + 4]).reshape(TG, D)
        m["xs"] = f(inp["x_sample"][b])
        m["smallv"] = _smallv(inp, c)
        m["c_ckv"] = f(inp["cache_mla_ckv"][b]); m["c_krope"] = f(inp["cache_mla_krope"][b])
        m["c_dk"] = f(inp["cache_diff_k"][b]); m["c_dv"] = f(inp["cache_diff_v"][b])
        m["c_nk"] = f(inp["cache_nat_k"][b]); m["c_nv"] = f(inp["cache_nat_v"][b])
        maps.append(m)
    return maps


def kernel(**inputs):
    nc = build_program()
    in_maps = make_in_maps(inputs)
    res = run_bass_kernel_spmd(nc, in_maps, core_ids=list(range(NCORES)))
    r = res.results
    y_p = np.concatenate([r[c]["y_p"].reshape(4, 256, D) for c in range(NCORES)], axis=0)
    y_s = np.stack([r[2 * b]["y_s"] for b in range(4)], axis=0)
    outs = [y_p, y_s]
    for k in ("st_ckv", "st_krope", "st_dk", "st_dv", "st_nk", "st_nv"):
        outs.append(np.concatenate([r[c][k] for c in range(NCORES)], axis=0))
    return tuple(np.ascontiguousarray(o, dtype=np.float32) for o in outs)
```
